# Optimizing a Trainium2 kernel written in Bass

```python
import jax, jax.numpy as jnp
from jax import lax
import numpy as np

D_MODEL = 2048
BATCH = 4
SEQ = 2048
DEPTH = 4
DEC_BATCH = 8
DEC_SEQ = 1
PAST_LEN = 16384
PAGE_SIZE = 128

N_MIXERS = 3
N_CONV_LAYERS = (DEPTH + 2) // N_MIXERS
N_HGRN_LAYERS = (DEPTH + 1) // N_MIXERS
N_ATTN_LAYERS = DEPTH // N_MIXERS
CONV_W = 3
CONV_DIM = D_MODEL
HG_DK = 128
HG_HEADS = D_MODEL // HG_DK
HG_DV = D_MODEL // HG_HEADS
GLA_CHUNK = 64
ATT_DH = 128
ATT_HEADS = D_MODEL // ATT_DH
DILATED_GROUPS = ((128, 1), (512, 4), (2048, 16))
N_GROUPS = 3
ROPE_DIM = ATT_DH // 4
ROPE_THETA = 500000.0
MEM_LEN = 256
XA_HEADS = 4
XA_DH = 128
D_FF = 5632
EPS = 1e-6

kernel_name = 'hybrid_conv_hgrn2_dilated_decoder_step'


def rms_norm(x, g):
    xf = x.astype(jnp.float32)
    y = xf * lax.rsqrt(jnp.mean(xf * xf, axis=-1, keepdims=True) + EPS)
    return (y * g.astype(jnp.float32)).astype(x.dtype)


def swiglu(x, w_gu, w_down):
    gate, up = jnp.split(x @ w_gu, 2, axis=-1)
    return (jax.nn.silu(gate) * up) @ w_down


def partial_rotary(x, pos):
    half = ROPE_DIM // 2
    inv_freq = ROPE_THETA ** (-jnp.arange(half, dtype=jnp.float32) * 2.0 / ROPE_DIM)
    ang = pos.astype(jnp.float32)[:, None] * inv_freq[None, :]
    cos = jnp.cos(ang)[None, :, None, :]
    sin = jnp.sin(ang)[None, :, None, :]
    x1 = x[..., :half].astype(jnp.float32)
    x2 = x[..., half:ROPE_DIM].astype(jnp.float32)
    rot = jnp.concatenate([x1 * cos - x2 * sin, x2 * cos + x1 * sin], axis=-1).astype(x.dtype)
    return jnp.concatenate([rot, x[..., ROPE_DIM:]], axis=-1)


def short_conv_mixer(h, w_in, w_conv, w_out, buf):
    t = h.shape[1]
    b_gate, c_gate, z = jnp.split(h @ w_in, 3, axis=-1)
    u = c_gate * z
    ue = jnp.concatenate([buf.astype(u.dtype), u], axis=1)
    conv = w_conv[0] * ue[:, 0:t]
    for j in range(1, CONV_W):
        conv = conv + w_conv[j] * ue[:, j:j + t]
    y = (b_gate * conv) @ w_out
    return y, ue[:, -(CONV_W - 1):]


def gated_linear_recurrence(q, k, v, logf, s0):
    bsz, t, nh, _ = q.shape
    dv = v.shape[-1]
    c = GLA_CHUNK if t % GLA_CHUNK == 0 else t
    n = t // c

    def chunks(a):
        return a.reshape(bsz, n, c, nh, a.shape[-1]).transpose(1, 0, 3, 2, 4).astype(jnp.float32)

    causal = jnp.tril(jnp.ones((c, c), dtype=bool))[:, :, None]

    def step(s, inp):
        qc, kc, vc, gc = inp
        cum = jnp.cumsum(gc, axis=2)
        o = jnp.einsum('bhtd,bhde->bhte', qc * jnp.exp(cum), s)
        rel = jnp.where(causal, cum[:, :, :, None, :] - cum[:, :, None, :, :], -jnp.inf)
        att = jnp.einsum('bhtd,bhtsd,bhsd->bhts', qc, jnp.exp(rel), kc)
        o = o + jnp.einsum('bhts,bhse->bhte', att, vc)
        last = cum[:, :, -1:, :]
        s = jnp.exp(last[:, :, 0, :, None]) * s + jnp.einsum('bhsd,bhse->bhde', kc * jnp.exp(last - cum), vc)
        return s, o

    s, o = lax.scan(step, s0.astype(jnp.float32), (chunks(q), chunks(k), chunks(v), chunks(logf)))
    o = o.transpose(1, 0, 3, 2, 4).reshape(bsz, t, nh, dv)
    return o, s


def hgrn2_mixer(h, w_in, lb, norm_g, w_out, s0):
    bsz, t, _ = h.shape
    q, f, i_in, g = jnp.split(h @ w_in, 4, axis=-1)
    z = f.astype(jnp.float32)
    lbf = lb.astype(jnp.float32)
    logf = jnp.logaddexp(jnp.log(lbf), jnp.log1p(-lbf) + jax.nn.log_sigmoid(z))
    k = (1.0 - lbf) * jax.nn.sigmoid(-z)
    shp = (bsz, t, HG_HEADS, HG_DK)
    o, s = gated_linear_recurrence(jax.nn.silu(q).reshape(shp), k.reshape(shp),
                                   i_in.reshape(bsz, t, HG_HEADS, HG_DV), logf.reshape(shp), s0)
    o = rms_norm(o.astype(h.dtype), norm_g)
    y = (o.reshape(bsz, t, HG_HEADS * HG_DV) * jax.nn.silu(g)) @ w_out
    return y, s.astype(s0.dtype)


def dilated_group_prompt(q, k, v, dil, nk):
    bsz, t, nh, e = q.shape
    lsub = t // dil
    blk = nk
    nb = -(-lsub // blk)
    pad_end = nb * blk - lsub

    def sub(a, front):
        a = a.reshape(bsz, lsub, dil, nh, e).transpose(0, 2, 1, 3, 4)
        return jnp.pad(a, ((0, 0), (0, 0), (front, pad_end), (0, 0), (0, 0)))

    qs = sub(q, 0).reshape(bsz, dil, nb, blk, nh, e)
    ks = sub(k, blk).reshape(bsz, dil, nb + 1, blk, nh, e)
    vs = sub(v, blk).reshape(bsz, dil, nb + 1, blk, nh, e)
    kw = jnp.concatenate([ks[:, :, :-1], ks[:, :, 1:]], axis=3)
    vw = jnp.concatenate([vs[:, :, :-1], vs[:, :, 1:]], axis=3)
    s = jnp.einsum('brnqhe,brnkhe->brnhqk', qs, kw).astype(jnp.float32) * (e ** -0.5)
    qi = jnp.arange(blk)[:, None]
    ki = jnp.arange(2 * blk)[None, :]
    dist = qi + blk - ki
    band = (dist >= 0) & (dist <= nk)
    real = (jnp.arange(nb)[:, None, None] > 0) | (ki >= blk)[None]
    mask = band[None] & real
    s = jnp.where(mask[:, None], s, -jnp.inf)
    lse = jax.nn.logsumexp(s, axis=-1)
    p = jnp.exp(s - lse[..., None])
    o = jnp.einsum('brnhqk,brnkhe->brnqhe', p, vw.astype(jnp.float32))
    o = o.reshape(bsz, dil, nb * blk, nh, e)[:, :, :lsub].transpose(0, 2, 1, 3, 4).reshape(bsz, t, nh, e)
    lse = lse.transpose(0, 1, 2, 4, 3).reshape(bsz, dil, nb * blk, nh)[:, :, :lsub]
    lse = lse.transpose(0, 2, 1, 3).reshape(bsz, t, nh)
    return o, lse


def dilated_group_step(q, k, v, k_buf, v_buf, dil, nk):
    t = q.shape[1]
    e = q.shape[-1]
    rows = k_buf.shape[1]
    k_all = jnp.concatenate([k_buf.astype(k.dtype), k], axis=1)
    v_all = jnp.concatenate([v_buf.astype(v.dtype), v], axis=1)
    idx = rows + jnp.arange(t)[:, None] - dil * jnp.arange(nk + 1)[None, :]
    valid = idx >= 0
    idx = jnp.maximum(idx, 0)
    kg = k_all[:, idx]
    vg = v_all[:, idx]
    s = jnp.einsum('bthe,btkhe->bthk', q, kg).astype(jnp.float32) * (e ** -0.5)
    s = jnp.where(valid[:, None, :], s, -jnp.inf)
    lse = jax.nn.logsumexp(s, axis=-1)
    p = jnp.exp(s - lse[..., None])
    o = jnp.einsum('bthk,btkhe->bthe', p, vg.astype(jnp.float32))
    return o, lse


def dilated_attention(h, w_qkv, w_out, pos0, bufs):
    bsz, t, _ = h.shape
    pos = pos0 + jnp.arange(t)
    qkv = (h @ w_qkv).reshape(bsz, t, N_GROUPS, 3, ATT_HEADS, ATT_DH)
    outs, lses, rows = [], [], []
    for g, (win, dil) in enumerate(DILATED_GROUPS):
        nk = win // dil
        q = partial_rotary(qkv[:, :, g, 0], pos)
        k = partial_rotary(qkv[:, :, g, 1], pos)
        v = qkv[:, :, g, 2]
        if bufs is None:
            o, lse = dilated_group_prompt(q, k, v, dil, nk)
            keep = min(win, t)
            rows += [k[:, t - keep:], v[:, t - keep:]]
        else:
            o, lse = dilated_group_step(q, k, v, bufs[g][0], bufs[g][1], dil, nk)
            rows += [k, v]
        outs.append(o)
        lses.append(lse)
    wgt = jax.nn.softmax(jnp.stack(lses), axis=0)
    o = jnp.einsum('gbth,gbthe->bthe', wgt, jnp.stack(outs))
    y = o.reshape(bsz, t, ATT_HEADS * ATT_DH).astype(h.dtype) @ w_out
    return y, tuple(rows)


def memory_kv(mem, g_mem, w_kv):
    bsz, m, _ = mem.shape
    k, v = jnp.split(rms_norm(mem, g_mem) @ w_kv, 2, axis=-1)
    return k.reshape(bsz, m, XA_HEADS, XA_DH), v.reshape(bsz, m, XA_HEADS, XA_DH)


def cross_attention(h, mem_k, mem_v, w_q, w_o):
    bsz, t, _ = h.shape
    q = (h @ w_q).reshape(bsz, t, XA_HEADS, XA_DH)
    s = jnp.einsum('bthe,bmhe->bhtm', q, mem_k.astype(q.dtype)).astype(jnp.float32) * (XA_DH ** -0.5)
    p = jax.nn.softmax(s, axis=-1)
    o = jnp.einsum('bhtm,bmhe->bthe', p, mem_v.astype(jnp.float32))
    return o.reshape(bsz, t, XA_HEADS * XA_DH).astype(h.dtype) @ w_o


def setup_inputs(seed: int = 0) -> dict:
    keys = iter(jax.random.split(jax.random.key(seed), 64))

    def normal(shape, scale=1.0):
        return jax.random.normal(next(keys), shape, jnp.float32) * scale

    def gain(shape):
        return 1.0 + 0.02 * normal(shape)

    d = D_MODEL
    inp = {}
    inp['x_prompt'] = normal((BATCH, SEQ, d))
    inp['x_sample'] = normal((DEC_BATCH, DEC_SEQ, d))
    inp['state_conv'] = normal((N_CONV_LAYERS, DEC_BATCH, CONV_W - 1, CONV_DIM))
    inp['state_hgrn'] = normal((N_HGRN_LAYERS, DEC_BATCH, HG_HEADS, HG_DK, HG_DV), 0.5)
    for g, (win, _) in enumerate(DILATED_GROUPS):
        rows = min(win, PAST_LEN)
        inp['cache_win_k%d' % g] = normal((N_ATTN_LAYERS, DEC_BATCH, rows, ATT_HEADS, ATT_DH))
        inp['cache_win_v%d' % g] = normal((N_ATTN_LAYERS, DEC_BATCH, rows, ATT_HEADS, ATT_DH))
    inp['cache_mem_k'] = normal((DEPTH, DEC_BATCH, MEM_LEN, XA_HEADS, XA_DH))
    inp['cache_mem_v'] = normal((DEPTH, DEC_BATCH, MEM_LEN, XA_HEADS, XA_DH))
    inp['mem_prompt'] = normal((BATCH, MEM_LEN, d))
    inp['norm_ffn1'] = gain((DEPTH, d))
    inp['ffn1_w_gu'] = normal((DEPTH, d, 2 * D_FF), d ** -0.5)
    inp['ffn1_w_down'] = normal((DEPTH, D_FF, d), D_FF ** -0.5)
    inp['norm_mix'] = gain((DEPTH, d))
    inp['conv_w_in'] = normal((N_CONV_LAYERS, d, 3 * CONV_DIM), d ** -0.5)
    inp['conv_w'] = normal((N_CONV_LAYERS, CONV_W, CONV_DIM), CONV_W ** -0.5)
    inp['conv_w_out'] = normal((N_CONV_LAYERS, CONV_DIM, d), CONV_DIM ** -0.5)
    inp['hgrn_w_in'] = normal((N_HGRN_LAYERS, d, 2 * HG_HEADS * HG_DK + 2 * HG_HEADS * HG_DV), d ** -0.5)
    inp['hgrn_lb_logits'] = normal((DEPTH, HG_HEADS * HG_DK), 0.5)
    inp['hgrn_norm'] = gain((N_HGRN_LAYERS, HG_HEADS, HG_DV))
    inp['hgrn_w_out'] = normal((N_HGRN_LAYERS, HG_HEADS * HG_DV, d), (HG_HEADS * HG_DV) ** -0.5)
    inp['attn_w_qkv'] = normal((N_ATTN_LAYERS, d, N_GROUPS * 3 * ATT_HEADS * ATT_DH), d ** -0.5)
    inp['attn_w_out'] = normal((N_ATTN_LAYERS, ATT_HEADS * ATT_DH, d), (ATT_HEADS * ATT_DH) ** -0.5)
    inp['norm_mem'] = gain((DEPTH, d))
    inp['xattn_w_kv'] = normal((DEPTH, d, 2 * XA_HEADS * XA_DH), d ** -0.5)
    inp['norm_xattn'] = gain((DEPTH, d))
    inp['xattn_w_q'] = normal((DEPTH, d, XA_HEADS * XA_DH), d ** -0.5)
    inp['xattn_w_o'] = normal((DEPTH, XA_HEADS * XA_DH, d), (XA_HEADS * XA_DH) ** -0.5)
    inp['norm_ffn2'] = gain((DEPTH, d))
    inp['ffn2_w_gu'] = normal((DEPTH, d, 2 * D_FF), d ** -0.5)
    inp['ffn2_w_down'] = normal((DEPTH, D_FF, d), D_FF ** -0.5)
    inp['norm_final'] = gain((d,))
    return inp


def reference(x_prompt, x_sample, state_conv, state_hgrn,
              cache_win_k0, cache_win_v0, cache_win_k1, cache_win_v1, cache_win_k2, cache_win_v2,
              cache_mem_k, cache_mem_v, mem_prompt,
              norm_ffn1, ffn1_w_gu, ffn1_w_down, norm_mix,
              conv_w_in, conv_w, conv_w_out,
              hgrn_w_in, hgrn_lb_logits, hgrn_norm, hgrn_w_out,
              attn_w_qkv, attn_w_out,
              norm_mem, xattn_w_kv, norm_xattn, xattn_w_q, xattn_w_o,
              norm_ffn2, ffn2_w_gu, ffn2_w_down, norm_final):
    lb_p = jax.nn.softmax(hgrn_lb_logits.astype(jnp.float32), axis=0)
    lower_bounds = jnp.cumsum(lb_p, axis=0) - lb_p[0]
    win_cache = ((cache_win_k0, cache_win_v0), (cache_win_k1, cache_win_v1), (cache_win_k2, cache_win_v2))

    def run(x, pos0, conv_st, hgrn_st, win_st, mem, mem_k_cache, mem_v_cache):
        nbat = x.shape[0]
        new_conv, new_hgrn, new_win, new_mk, new_mv = [], [], [], [], []
        for i in range(DEPTH):
            j = i // N_MIXERS
            kind = i % N_MIXERS
            x = x + 0.5 * swiglu(rms_norm(x, norm_ffn1[i]), ffn1_w_gu[i], ffn1_w_down[i])
            h = rms_norm(x, norm_mix[i])
            if kind == 0:
                buf = jnp.zeros((nbat, CONV_W - 1, CONV_DIM), x.dtype) if conv_st is None else conv_st[j]
                y, st = short_conv_mixer(h, conv_w_in[j], conv_w[j], conv_w_out[j], buf)
                new_conv.append(st)
            elif kind == 1:
                s0 = jnp.zeros((nbat, HG_HEADS, HG_DK, HG_DV), x.dtype) if hgrn_st is None else hgrn_st[j]
                y, st = hgrn2_mixer(h, hgrn_w_in[j], lower_bounds[i], hgrn_norm[j], hgrn_w_out[j], s0)
                new_hgrn.append(st)
            else:
                bufs = None if win_st is None else tuple((kc[j], vc[j]) for kc, vc in win_st)
                y, rows = dilated_attention(h, attn_w_qkv[j], attn_w_out[j], pos0, bufs)
                new_win.append(rows)
            x = x + y
            if mem is not None:
                mk, mv = memory_kv(mem, norm_mem[i], xattn_w_kv[i])
                new_mk.append(mk)
                new_mv.append(mv)
            else:
                mk, mv = mem_k_cache[i], mem_v_cache[i]
            x = x + cross_attention(rms_norm(x, norm_xattn[i]), mk, mv, xattn_w_q[i], xattn_w_o[i])
            x = x + 0.5 * swiglu(rms_norm(x, norm_ffn2[i]), ffn2_w_gu[i], ffn2_w_down[i])
        return rms_norm(x, norm_final), new_conv, new_hgrn, new_win, new_mk, new_mv

    y_prompt, pc, ph, pw, pmk, pmv = run(x_prompt, 0, None, None, None, mem_prompt, None, None)
    y_sample, sc, sh, sw, _, _ = run(x_sample, PAST_LEN, state_conv, state_hgrn, win_cache,
                                     None, cache_mem_k, cache_mem_v)

    p_state_conv = jnp.stack(pc)
    p_state_hgrn = jnp.stack(ph)
    p_win = [jnp.stack([r[g] for r in pw]) for g in range(2 * N_GROUPS)]
    p_win_k0, p_win_v0, p_win_k1, p_win_v1, p_win_k2, p_win_v2 = p_win
    p_mem_k = jnp.stack(pmk)
    p_mem_v = jnp.stack(pmv)
    s_state_conv = jnp.stack(sc)
    s_state_hgrn = jnp.stack(sh)
    s_win = [jnp.stack([r[g] for r in sw]) for g in range(2 * N_GROUPS)]
    s_win_k0, s_win_v0, s_win_k1, s_win_v1, s_win_k2, s_win_v2 = s_win
    return (y_prompt, y_sample,
            p_state_conv, p_state_hgrn, p_win_k0, p_win_v0, p_win_k1, p_win_v1, p_win_k2, p_win_v2,
            p_mem_k, p_mem_v,
            s_state_conv, s_state_hgrn, s_win_k0, s_win_v0, s_win_k1, s_win_v1, s_win_k2, s_win_v2)
```

```python
import contextlib
import numpy as np
import concourse.bass as bass
import concourse.mybir as mybir
from concourse.bass_utils import run_bass_kernel_spmd

F32 = mybir.dt.float32
BF16 = mybir.dt.bfloat16
AF = mybir.ActivationFunctionType
ALU = mybir.AluOpType

D = 2048
KC = 16
S = 2048
DFF = 5632
FC = 44
DEPTH = 4
EPS = 1e-6
NCORES = 4
NS = 8 // NCORES
TWF = S + NS
MEM = 256
NB_EL = 26048

V_FFN1, V_MIX, V_XA, V_FFN2, V_MEM = 0, 4, 8, 12, 16
V_FINAL = 20
V_CONVW = 21
V_LB = 27
V_HGN = 31
NV = 32

C_IDENT, C_MASK2, C_SCAN = 0, 128, 256
C_PM, C_MASKC, C_MASKP = 1280, 1408, 1536
C_COS = 1664
C_SIN = C_COS + TWF
NCST = C_SIN + TWF


class Op:
    __slots__ = ("eng", "fn", "deps", "chan", "inc", "val", "idx")


class Prog:
    COMPUTE = ("pe", "act", "dve", "pool")

    def __init__(self):
        self.ops = []
        self.lastw = {}
        self.readers = {}
        self.latest = {}
        self.barrier_idx = None

    def op(self, eng, fn, reads=(), writes=(), chan=None):
        o = Op()
        o.eng, o.fn, o.chan, o.inc, o.val = eng, fn, chan, False, 0
        o.idx = len(self.ops)
        deps = set()
        for k in reads:
            w = self.lastw.get(k)
            if w is not None:
                deps.add(w)
        for k in writes:
            w = self.lastw.get(k)
            if w is not None:
                deps.add(w)
            r = self.readers.get(k)
            if r:
                deps.update(r.values())
        if self.barrier_idx is not None:
            deps.add(self.barrier_idx)
        o.deps = deps
        self.ops.append(o)
        lane = ("c", chan) if chan is not None else ("e", eng)
        for k in writes:
            self.lastw[k] = o.idx
            self.readers[k] = {}
        for k in reads:
            rd = self.readers.setdefault(k, {})
            if isinstance(k, tuple) and k[0] == "ps" and lane != ("e", "pe") and any(l != ("e", "pe") and l != lane for l in rd):
                raise RuntimeError("PSUM tile %r read by two engines (%r and %r) - crashes the device" % (k, list(rd), lane))
            rd[lane] = o.idx
        self.latest[lane] = o.idx
        return o

    def barrier(self, eng, fn):
        o = self.op(eng, fn)
        o.deps.update(self.latest.values())
        o.deps.discard(o.idx)
        self.barrier_idx = o.idx
        return o

    def emit(self, nc, final_wait_eng="sp"):
        ops = self.ops
        for o in ops:
            for j in o.deps:
                d = ops[j]
                if d.chan is not None:
                    continue
                if d.eng == o.eng and o.eng == "pe" and o.chan is None:
                    continue
                d.inc = True
        cnt = {}
        for o in ops:
            if o.chan is not None:
                cnt[("c", o.chan)] = cnt.get(("c", o.chan), 0) + 16
                o.val = cnt[("c", o.chan)]
            elif o.inc:
                cnt[("e", o.eng)] = cnt.get(("e", o.eng), 0) + 1
                o.val = cnt[("e", o.eng)]
        lanes = sorted(cnt.keys(), key=str)
        with contextlib.ExitStack() as st:
            sems = {ln: st.enter_context(nc.semaphore("s%d" % i)) for i, ln in enumerate(lanes)}
            block = st.enter_context(nc.Block())
            by_eng = {}
            for o in ops:
                by_eng.setdefault(o.eng, []).append(o)

            def run(engname, e):
                known = {}
                for o in by_eng.get(engname, []):
                    need = {}
                    for j in o.deps:
                        d = ops[j]
                        if d.chan is not None:
                            ln = ("c", d.chan)
                        else:
                            if d.eng == o.eng and o.eng == "pe" and o.chan is None:
                                continue
                            ln = ("e", d.eng)
                        if d.val > need.get(ln, 0):
                            need[ln] = d.val
                    for ln, v in need.items():
                        if known.get(ln, 0) < v:
                            e.wait_ge(sems[ln], v)
                            known[ln] = v
                    ins = o.fn(e)
                    if o.chan is not None:
                        ins.then_inc(sems[("c", o.chan)], 16)
                    elif o.inc:
                        ins.then_inc(sems[("e", o.eng)], 1)
                if engname == final_wait_eng:
                    for ln in lanes:
                        if known.get(ln, 0) < cnt[ln]:
                            e.wait_ge(sems[ln], cnt[ln])

            @block.tensor
            def _(e):
                run("pe", e)

            @block.scalar
            def _(e):
                run("act", e)

            @block.vector
            def _(e):
                run("dve", e)

            @block.gpsimd
            def _(e):
                run("pool", e)

            @block.sync
            def _(e):
                run("sp", e)


class Ring:
    def __init__(self, name, views):
        self.name, self.views, self.i = name, views, 0

    def next(self):
        k = self.i % len(self.views)
        self.i += 1
        return (self.name, k), self.views[k]


class Item:
    def __init__(self):
        self.loads = []
        self.compute = None


class Builder:
    def __init__(self, layers=(0, 1, 2, 3), phases=("ffn1", "mix", "xattn", "ffn2"), do_final=True, wl=None, dbg=()):
        self.dbg = set(dbg)
        self.layers, self.phases, self.do_final = layers, phases, do_final
        self.wl = wl
        self.nc = bass.Bass("TRN2", target_bir_lowering=False)
        self.P = Prog()
        self.items = []
        self.w = {}
        self.outs = {}

    def din(self, name, shape):
        if name not in self.w:
            self.w[name] = self.nc.dram_tensor(name, list(shape), F32, kind="ExternalInput").ap()
        return self.w[name]

    def dout(self, name, shape):
        if name not in self.outs:
            self.outs[name] = self.nc.dram_tensor(name, list(shape), F32, kind="ExternalOutput").ap()
        return self.outs[name]

    def wt(self, nm, shp):
        shp = list(shp)
        if self.wl is not None:
            shp[0] = 1
        return self.din(nm, shp)

    def li(self, idx):
        return 0 if self.wl is not None else idx

    def tiles(self, full=False):
        if full:
            return [(0, TWF)]
        return [(0, 1024), (1024, TWF - 1024)]

    @staticmethod
    def colgroups(c0, w):
        out = []
        o = 0
        while o < w:
            ww = min(512, w - o)
            if o + ww > 2048 - c0 > o:
                ww = 2048 - c0 - o
            out.append((o, ww))
            o += ww
        return out

    def gk(self, name, c, c0, tw):
        return [(name, c, c0 + j0) for (j0, _) in self.colgroups(c0, tw)]

    def add_item(self, loads, compute):
        it = Item()
        it.loads, it.compute = loads, compute
        self.items.append(it)
        return it

    def psum(self):
        k = self.ps_i % 8
        self.ps_i += 1
        return ("ps", k), self.ps[k]

    def dma(self, eng, out, in_, reads, writes, chan):
        self.P.op(eng, lambda e, o=out, i=in_: e.dma_start(out=o, in_=i), reads=reads, writes=writes, chan=chan)

    def barrier_item(self):
        self.add_item([], lambda: self.P.barrier("dve", lambda e: e.memset(self.scr[:, 0:1], 0.0)))

    def wload(self, st, src, n, nk, ncol):
        st["k"], st["v"] = self.wring.next()
        tot = n * nk * ncol
        st["w"] = st["v"][:, 0:tot].rearrange("p (t k n) -> p k t n", t=n, k=nk)
        self.dma("pool", st["v"][:, 0:tot], src, reads=[], writes=[st["k"]], chan=st["k"])

    def mm_acc(self, pv, jw, lhs_list, rhs_list):
        n = len(lhs_list)

        def fn(e):
            for i in range(n):
                ins = e.matmul(pv[:, 0:jw], lhsT=lhs_list[i], rhs=rhs_list[i], start=(i == 0), stop=(i == n - 1))
            return ins
        return fn

    def prologue(self, c0, tw, grow, hview=None, hoff=0):
        P = self.P
        if hview is None:
            hview = self.hT
        for c in range(KC):
            st = {}

            def load(c=c, st=st):
                st["k"], st["v"] = self.xring.next()
                self.dma("sp", st["v"][:, 0:tw], self.xres[:, c, c0:c0 + tw], reads=self.gk("xres", c, c0, tw), writes=[st["k"]], chan=st["k"])

            def comp(c=c, st=st):
                g = self.vec[:, grow * 16 + c: grow * 16 + c + 1]
                P.op("dve", lambda e: e.scalar_tensor_tensor(out=hview[:, c, hoff:hoff + tw], in0=st["v"][:, 0:tw], scalar=g,
                                                             in1=self.rstd[:, c0:c0 + tw], op0=ALU.mult, op1=ALU.mult),
                     reads=[st["k"]] + self.gk("rstd", 0, c0, tw), writes=self.gk("hT", c, c0, tw))
            self.add_item([load], comp)

    def epilogue(self, c0, m, j0, jw, psk, psv, xk, xv, scale, final, hv=None, ho=0):
        P = self.P
        hv = self.hT if hv is None else hv
        sk, sv = self.xnring.next()
        P.op("dve", lambda e: e.scalar_tensor_tensor(out=sv[:, 0:jw], in0=psv[:, 0:jw], scalar=float(scale), in1=xv[:, j0:j0 + jw],
                                                     op0=ALU.mult, op1=ALU.add),
             reads=[psk, xk], writes=[sk])
        self.dma("sp", self.xres[:, m, c0 + j0:c0 + j0 + jw], sv[:, 0:jw], reads=[sk], writes=[("xres", m, c0 + j0)], chan=sk)
        if final:
            P.op("act", lambda e: e.activation(out=hv[:, m, ho + j0:ho + j0 + jw], in_=sv[:, 0:jw], func=AF.Square),
                 reads=[sk], writes=[("hT", m, c0 + j0)])

    def stats(self, c0, tw, hv=None, ho=0):
        P = self.P
        hv = self.hT if hv is None else hv
        for (j0, jw) in self.colgroups(c0, tw):
            psk, psv = self.psum()
            P.op("pe", self.mm_acc(psv, jw, [self.onesD[:, :]] * KC, [hv[:, c, ho + j0:ho + j0 + jw] for c in range(KC)]),
                 reads=[("hT", c, c0 + j0) for c in range(KC)], writes=[psk])
            tk, tv = self.tmpring.next()
            P.op("act", lambda e, psv=psv, tv=tv, jw=jw: e.activation(out=tv[:, 0:jw], in_=psv[:, 0:jw], func=AF.Sqrt, bias=self.epsc[:, 0:1]),
                 reads=[psk], writes=[tk])
            P.op("dve", lambda e, tv=tv, j0=j0, jw=jw: e.reciprocal(out=self.rstd[:, c0 + j0:c0 + j0 + jw], in_=tv[:, 0:jw]),
                 reads=[tk], writes=[("rstd", 0, c0 + j0)])

    def out_proj(self, c0, tw, w_fn, nk, rhs_fn, rhs_keys_fn, scale, final=True, hv=None, ho=0):
        P = self.P
        cgs = self.colgroups(c0, tw)
        for m in range(KC):
            st = {}

            def load(m=m, st=st):
                self.wload(st, w_fn(m), 1, nk, 128)
                st["xk"], st["xv"] = self.xring.next()
                self.dma("sp", st["xv"][:, 0:tw], self.xres[:, m, c0:c0 + tw], reads=self.gk("xres", m, c0, tw), writes=[st["xk"]], chan=st["xk"])

            def comp(m=m, st=st):
                for (j0, jw) in cgs:
                    pk, pv = self.psum()
                    P.op("pe", self.mm_acc(pv, jw, [st["w"][:, k, 0, :] for k in range(nk)], [rhs_fn(k, j0, jw) for k in range(nk)]),
                         reads=[st["k"]] + rhs_keys_fn(c0 + j0), writes=[pk])
                    self.epilogue(c0, m, j0, jw, pk, pv, st["xk"], st["xv"], scale, final, hv, ho)
                if final and m == KC - 1:
                    self.stats(c0, tw, hv, ho)
            self.add_item([load], comp)

    def init_stats(self):
        P = self.P
        for (c0, tw) in self.tiles():
            for c in range(KC):
                st = {}

                def load(c=c, st=st, c0=c0, tw=tw):
                    st["k"], st["v"] = self.xring.next()
                    self.dma("sp", st["v"][:, 0:tw], self.xT[:, c, c0:c0 + tw], reads=[], writes=[st["k"]], chan=st["k"])

                def comp(c=c, st=st, c0=c0, tw=tw):
                    P.op("act", lambda e: e.activation(out=self.hT[:, c, 0:tw], in_=st["v"][:, 0:tw], func=AF.Square),
                         reads=[st["k"]], writes=self.gk("hT", c, c0, tw))
                    self.dma("sp", self.xres[:, c, c0:c0 + tw], st["v"][:, 0:tw], reads=[st["k"]], writes=self.gk("xres", c, c0, tw), chan=("xs", c % 4))
                    if c == KC - 1:
                        self.stats(c0, tw)
                self.add_item([load], comp)

    def final_norm(self):
        P = self.P
        for (c0, tw) in self.tiles():
            for c in range(KC):
                st = {}

                def load(c=c, st=st, c0=c0, tw=tw):
                    st["k"], st["v"] = self.xring.next()
                    self.dma("sp", st["v"][:, 0:tw], self.xres[:, c, c0:c0 + tw], reads=self.gk("xres", c, c0, tw), writes=[st["k"]], chan=st["k"])

                def comp(c=c, st=st, c0=c0, tw=tw):
                    g = self.vec[:, V_FINAL * 16 + c: V_FINAL * 16 + c + 1]
                    ok, ov = self.yring.next()
                    P.op("dve", lambda e: e.scalar_tensor_tensor(out=ov[:, 0:tw], in0=st["v"][:, 0:tw], scalar=g,
                                                                 in1=self.rstd[:, c0:c0 + tw], op0=ALU.mult, op1=ALU.mult),
                         reads=[st["k"]] + self.gk("rstd", 0, c0, tw), writes=[ok])
                    self.dma("sp", self.yT[:, c, c0:c0 + tw], ov[:, 0:tw], reads=[ok], writes=[("yT", c, c0)], chan=ok)
                self.add_item([load], comp)

    def ffn(self, L, which):
        P = self.P
        wgu = self.wt(FFN_W_NAMES[which][0], [DEPTH, FC, 128, 2 * KC * 128])
        wdn = self.wt(FFN_W_NAMES[which][1], [DEPTH, KC, 128, FC * 128])
        Lw = self.li(L)
        grow = (V_FFN1 if which == 1 else V_FFN2) + L
        act = self.B[:, 0:22 * 1026].rearrange("p (c t) -> p c t", c=22)
        self.barrier_item()
        hTf = self.hTf
        tiles = self.tiles()
        for ti, (c0, tw) in enumerate(tiles):
            cgs = self.colgroups(c0, tw)
            if ti == 0:
                self.prologue(c0, tw, grow, hview=hTf, hoff=c0)
            for half in range(2):
                for cc in range(22):
                    fc = half * 22 + cc
                    st = {}

                    def load(fc=fc, st=st):
                        self.wload(st, wgu[Lw, fc], 2, KC, 128)

                    def comp(cc=cc, st=st, cgs=cgs, c0=c0):
                        v = st["w"]
                        for (j0, jw) in cgs:
                            gk, gv = self.psum()
                            uk, uv = self.psum()

                            def mm(e, j0=j0, jw=jw, gv=gv, uv=uv, v=v):
                                for t, pv in ((0, gv), (1, uv)):
                                    for k in range(KC):
                                        ins = e.matmul(pv[:, 0:jw], lhsT=v[:, k, t, :], rhs=hTf[:, k, c0 + j0:c0 + j0 + jw], start=(k == 0), stop=(k == KC - 1))
                                return ins
                            P.op("pe", mm, reads=[st["k"]] + [("hT", c, c0 + j0) for c in range(KC)], writes=[gk, uk])
                            tk, tv = self.tmpring.next()
                            P.op("act", lambda e, gv=gv, tv=tv, jw=jw: e.activation(out=tv[:, 0:jw], in_=gv[:, 0:jw], func=AF.Silu),
                                 reads=[gk], writes=[tk])
                            P.op("dve", lambda e, uv=uv, tv=tv, j0=j0, jw=jw, cc=cc: e.tensor_tensor(out=act[:, cc, j0:j0 + jw], in0=tv[:, 0:jw], in1=uv[:, 0:jw], op=ALU.mult),
                                 reads=[tk, uk], writes=[("act", cc, c0 + j0)])
                    self.add_item([load], comp)
                if half == 1 and ti + 1 < len(tiles):
                    self.prologue(tiles[ti + 1][0], tiles[ti + 1][1], grow, hview=hTf, hoff=tiles[ti + 1][0])
                self.out_proj(c0, tw, lambda m, half=half: wdn[Lw, m][:, half * 2816:(half + 1) * 2816], 22,
                              lambda k, j0, jw: act[:, k, j0:j0 + jw],
                              lambda gc: [("act", c, gc) for c in range(22)], 0.5, final=(half == 1), hv=hTf, ho=c0)

    def conv_mixer(self, L):
        P = self.P
        l = L // 3
        lw = self.li(l)
        w_in = self.wt("conv_w_in", [2, KC, 128, 3 * KC * 128])
        w_out = self.wt("conv_w_out", [2, KC, 128, KC * 128])
        sconvT = self.din("sconvT", [128, 2, KC, NS, 2])
        o_pconv = self.dout("o_pconv", [128, 2, KC, 2])
        o_sconv = self.dout("o_sconv", [128, 2, KC, NS, 2])
        B = self.B
        yT = B[:, 0:16 * 1026].rearrange("p (c t) -> p c t", c=16)
        o = 16 * 1026
        bsb = [B[:, o + i * 1026: o + (i + 1) * 1026] for i in range(2)]
        o += 2 * 1026
        UW = 1026 + 3 * NS + 2
        ue = [B[:, o + i * 2 * UW: o + (i + 1) * 2 * UW].bitcast(F32) for i in range(2)]
        o += 4 * UW
        tsm = [B[:, o + i * 2 * NS: o + (i + 1) * 2 * NS].bitcast(F32) for i in range(2)]
        wp = 1024
        self.barrier_item()
        for ti, (c0, tw) in enumerate(self.tiles()):
            cgs = self.colgroups(c0, tw)
            self.prologue(c0, tw, V_MIX + L)
            for m in range(KC):
                mst = {}
                stA, stB = {}, {}
                wv = [self.vec[:, (V_CONVW + l * 3 + t) * 16 + m:(V_CONVW + l * 3 + t) * 16 + m + 1] for t in range(3)]

                def loadA(m=m, st=stA):
                    self.wload(st, w_in[lw, m][:, 2048:6144], 2, KC, 128)

                def compA(m=m, st=stA, mst=mst, ti=ti, c0=c0, cgs=cgs):
                    r = self.ue_i % 2
                    self.ue_i += 1
                    mst["r"] = r
                    uk = ("ue", r)
                    u = ue[r]
                    us = u[:, 1026:1026 + 3 * NS].rearrange("p (s t) -> p s t", t=3)
                    if ti == 0:
                        P.op("dve", lambda e: e.memset(u[:, 0:2], 0.0), writes=[uk])
                    else:
                        P.op("act", lambda e: e.copy(out=u[:, 0:2], in_=self.carry[:, m, :]), reads=[("carry", m)], writes=[uk])
                        self.dma("sp", us[:, :, 0:2], sconvT[:, l, m, :, :], reads=[], writes=[uk], chan=("us", r))
                    for (j0, jw) in cgs:
                        ck, cv = self.psum()
                        zk, zv = self.psum()

                        def mm(e, j0=j0, jw=jw, cv=cv, zv=zv, v=st["w"]):
                            for t, pv in ((0, cv), (1, zv)):
                                for k in range(KC):
                                    ins = e.matmul(pv[:, 0:jw], lhsT=v[:, k, t, :], rhs=self.hT[:, k, j0:j0 + jw], start=(k == 0), stop=(k == KC - 1))
                            return ins
                        P.op("pe", mm, reads=[st["k"]] + [("hT", c, c0 + j0) for c in range(KC)], writes=[ck, zk])
                        tk, tv = self.tmpring.next()
                        P.op("act", lambda e, cv=cv, tv=tv, jw=jw: e.copy(out=tv[:, 0:jw], in_=cv[:, 0:jw]), reads=[ck], writes=[tk])
                        if j0 < wp:
                            dst = u[:, 2 + j0:2 + j0 + jw]
                        else:
                            dst = us[:, :, 2]
                        P.op("dve", lambda e, zv=zv, tv=tv, jw=jw, dst=dst: e.tensor_tensor(out=dst, in0=tv[:, 0:jw], in1=zv[:, 0:jw], op=ALU.mult),
                             reads=[tk, zk, uk], writes=[uk])
                self.add_item([loadA], compA)

                def loadB(m=m, st=stB):
                    self.wload(st, w_in[lw, m][:, 0:2048], 1, KC, 128)

                def compB(m=m, st=stB, mst=mst, ti=ti, c0=c0, cgs=cgs, wv=wv):
                    r = mst["r"]
                    uk = ("ue", r)
                    u = ue[r]
                    us = u[:, 1026:1026 + 3 * NS].rearrange("p (s t) -> p s t", t=3)
                    bk = ("bsb", r)
                    bs = bsb[r]
                    for (j0, jw) in cgs:
                        pk, pv = self.psum()
                        P.op("pe", self.mm_acc(pv, jw, [st["w"][:, k, 0, :] for k in range(KC)], [self.hT[:, k, j0:j0 + jw] for k in range(KC)]),
                             reads=[st["k"]] + [("hT", c, c0 + j0) for c in range(KC)], writes=[pk])
                        P.op("act", lambda e, pv=pv, j0=j0, jw=jw: e.copy(out=bs[:, j0:j0 + jw], in_=pv[:, 0:jw]), reads=[pk], writes=[bk])
                    yk, t = self.yring.next()
                    P.op("dve", lambda e: e.tensor_scalar(out=t[:, 0:wp], in0=u[:, 0:wp], scalar1=wv[0], scalar2=None, op0=ALU.mult), reads=[uk], writes=[yk])
                    P.op("dve", lambda e: e.scalar_tensor_tensor(out=t[:, 0:wp], in0=u[:, 1:wp + 1], scalar=wv[1], in1=t[:, 0:wp], op0=ALU.mult, op1=ALU.add), reads=[uk, yk], writes=[yk])
                    P.op("dve", lambda e: e.scalar_tensor_tensor(out=t[:, 0:wp], in0=u[:, 2:wp + 2], scalar=wv[2], in1=t[:, 0:wp], op0=ALU.mult, op1=ALU.add), reads=[uk, yk], writes=[yk])
                    P.op("dve", lambda e: e.tensor_tensor(out=yT[:, m, 0:wp], in0=t[:, 0:wp], in1=bs[:, 0:wp], op=ALU.mult), reads=[yk, bk],
                         writes=[("yT", m, c0), ("yT", m, c0 + 512)])
                    if ti == 0:
                        P.op("act", lambda e: e.copy(out=self.carry[:, m, :], in_=u[:, wp:wp + 2]), reads=[uk], writes=[("carry", m)])
                    else:
                        ts = tsm[r]
                        sk = ("tsm", r)
                        P.op("dve", lambda e: e.tensor_scalar(out=ts[:, 0:NS], in0=us[:, :, 0], scalar1=wv[0], scalar2=None, op0=ALU.mult), reads=[uk], writes=[sk])
                        P.op("dve", lambda e: e.scalar_tensor_tensor(out=ts[:, 0:NS], in0=us[:, :, 1], scalar=wv[1], in1=ts[:, 0:NS], op0=ALU.mult, op1=ALU.add), reads=[uk, sk], writes=[sk])
                        P.op("dve", lambda e: e.scalar_tensor_tensor(out=ts[:, 0:NS], in0=us[:, :, 2], scalar=wv[2], in1=ts[:, 0:NS], op0=ALU.mult, op1=ALU.add), reads=[uk, sk], writes=[sk])
                        P.op("dve", lambda e: e.tensor_tensor(out=yT[:, m, wp:wp + NS], in0=ts[:, 0:NS], in1=bs[:, wp:wp + NS], op=ALU.mult), reads=[sk, bk],
                             writes=[("yT", m, c0 + wp)])
                        self.dma("sp", o_pconv[:, l, m, :], u[:, wp:wp + 2], reads=[uk], writes=[("o_pconv", l, m)], chan=("uo", r))
                        self.dma("sp", o_sconv[:, l, m, :, :], us[:, :, 1:3], reads=[uk], writes=[("o_sconv", l, m)], chan=("uo", r))
                self.add_item([loadB], compB)
            self.out_proj(c0, tw, lambda m: w_out[lw, m], KC, lambda k, j0, jw: yT[:, k, j0:j0 + jw],
                          lambda gc: [("yT", c, gc) for c in range(KC)], 1.0)

    def xattn_views(self):
        B = self.B
        o = 0
        v = {}
        v["qT"] = B[:, o:o + 4 * 1026].rearrange("p (h t) -> p h t", h=4); o += 4 * 1026
        v["oT"] = B[:, o:o + 4 * 1026].rearrange("p (h t) -> p h t", h=4); o += 4 * 1026
        v["PT"] = [B[:, o + i * 1024:o + (i + 1) * 1024].rearrange("p (b t) -> p b t", b=2) for i in range(2)]; o += 2048
        v["hmT"] = B[:, o:o + 16 * 256].rearrange("p (c t) -> p c t", c=16); o += 4096
        v["kmT"] = B[:, o:o + 1024].rearrange("p (h t) -> p h t", h=4); o += 1024
        v["vm"] = B[:, o:o + 1024].rearrange("p (b t) -> p b t", b=2); o += 1024
        v["ckv"] = [B[:, o + i * 2048:o + (i + 1) * 2048] for i in range(NS)]; o += 2048 * NS
        return v

    def mem_stats(self):
        P = self.P
        memT = self.din("memT", [128, KC, MEM])
        sq = self.B[:, 0:16 * 256].rearrange("p (c t) -> p c t", c=16)
        self.barrier_item()
        for c in range(KC):
            st = {}

            def load(c=c, st=st):
                st["k"], st["v"] = self.xring.next()
                self.dma("sp", st["v"][:, 0:MEM], memT[:, c, :], reads=[], writes=[st["k"]], chan=st["k"])

            def comp(c=c, st=st):
                P.op("act", lambda e: e.activation(out=sq[:, c, :], in_=st["v"][:, 0:MEM], func=AF.Square), reads=[st["k"]], writes=[("msq", c)])
                if c == KC - 1:
                    psk, psv = self.psum()
                    P.op("pe", self.mm_acc(psv, MEM, [self.onesD[:, :]] * KC, [sq[:, k, :] for k in range(KC)]),
                         reads=[("msq", k) for k in range(KC)], writes=[psk])
                    tk, tv = self.tmpring.next()
                    P.op("act", lambda e: e.activation(out=tv[:, 0:MEM], in_=psv[:, 0:MEM], func=AF.Sqrt, bias=self.epsc[:, 0:1]), reads=[psk], writes=[tk])
                    P.op("dve", lambda e: e.reciprocal(out=self.rstdm[:, :], in_=tv[:, 0:MEM]), reads=[tk], writes=["rstdm"])
            self.add_item([load], comp)

    def attn_block(self, h, kT, kkeys, vT, vkeys, qv, qkey, ov, okey, jw, PT, ptk, sel=None):
        P = self.P
        scale = 128 ** -0.5
        for b in range(2):
            sk, sv = self.psum()
            P.op("pe", self.mm_acc(sv, jw, [kT[:, b * 128:(b + 1) * 128]], [qv]), reads=kkeys + [qkey], writes=[sk])
            P.op("act", lambda e, sv=sv, b=b: e.activation(out=PT[:, b, 0:jw], in_=sv[:, 0:jw], func=AF.Exp, scale=scale),
                 reads=[sk], writes=[(ptk, b)])
        ok_, ovp = self.psum()
        dk, dv = self.psum()
        P.op("pe", self.mm_acc(ovp, jw, [vT[:, b, :] for b in range(2)], [PT[:, b, 0:jw] for b in range(2)]),
             reads=vkeys + [(ptk, 0), (ptk, 1)], writes=[ok_])
        P.op("pe", self.mm_acc(dv, jw, [self.ones1[:, :]] * 2, [PT[:, b, 0:jw] for b in range(2)]),
             reads=[(ptk, 0), (ptk, 1)], writes=[dk])
        tk, tv = self.tmpring.next()
        P.op("dve", lambda e: e.reciprocal(out=tv[:, 0:jw], in_=dv[:, 0:jw]), reads=[dk], writes=[tk])
        if sel is None:
            P.op("dve", lambda e: e.tensor_tensor(out=ov, in0=ovp[:, 0:jw], in1=tv[:, 0:jw], op=ALU.mult), reads=[ok_, tk], writes=[okey])
        else:
            P.op("dve", lambda e: e.tensor_tensor(out=ov, in0=ovp[:, sel:sel + 1], in1=tv[:, sel:sel + 1], op=ALU.mult), reads=[ok_, tk, okey], writes=[okey])

    def xattn(self, L):
        P = self.P
        Lw = self.li(L)
        w_kv = self.wt("xattn_w_kv", [DEPTH, 4, 128, 2 * KC * 128])
        w_q = self.wt("xattn_w_q", [DEPTH, 4, 128, KC * 128])
        w_o = self.wt("xattn_w_o", [DEPTH, KC, 128, 4 * 128])
        memT = self.din("memT", [128, KC, MEM])
        cmkT = self.din("cmkT", [128, DEPTH, NS, 4 * MEM])
        cmv = self.din("cmv", [128, DEPTH, NS, 2 * 512])
        o_pmk = self.dout("o_pmk", [128, DEPTH, 4, MEM])
        o_pmv = self.dout("o_pmv", [128, DEPTH, 2, 512])
        V = self.xattn_views()
        qT, oT, hmT, kmT, vm = V["qT"], V["oT"], V["hmT"], V["kmT"], V["vm"]
        self.barrier_item()
        for c in range(KC):
            st = {}

            def load(c=c, st=st):
                st["k"], st["v"] = self.xring.next()
                self.dma("sp", st["v"][:, 0:MEM], memT[:, c, :], reads=[], writes=[st["k"]], chan=st["k"])

            def comp(c=c, st=st):
                g = self.vec[:, (V_MEM + L) * 16 + c:(V_MEM + L) * 16 + c + 1]
                P.op("dve", lambda e: e.scalar_tensor_tensor(out=hmT[:, c, :], in0=st["v"][:, 0:MEM], scalar=g, in1=self.rstdm[:, :], op0=ALU.mult, op1=ALU.mult),
                     reads=[st["k"], "rstdm"], writes=[("hmT", c)])
            self.add_item([load], comp)
        for h in range(4 if "nomemkv" not in self.dbg else 0):
            st = {}

            def load(h=h, st=st):
                self.wload(st, w_kv[Lw, h], 2, KC, 128)

            def comp(h=h, st=st):
                w = st["w"]
                pk, pv = self.psum()
                P.op("pe", self.mm_acc(pv, MEM, [w[:, k, 0, :] for k in range(KC)], [hmT[:, k, :] for k in range(KC)]),
                     reads=[st["k"]] + [("hmT", k) for k in range(KC)], writes=[pk])
                sk, sv = self.xnring.next()
                P.op("act", lambda e: e.copy(out=sv[:, 0:MEM], in_=pv[:, 0:MEM]), reads=[pk], writes=[sk])
                P.op("dve", lambda e: e.tensor_copy(out=kmT[:, h, :], in_=sv[:, 0:MEM]), reads=[sk], writes=[("kmT", h)])
                self.dma("sp", o_pmk[:, L, h, :], sv[:, 0:MEM], reads=[sk], writes=[("o_pmk", L, h)], chan=sk)
                for b in range(2):
                    pk2, pv2 = self.psum()
                    P.op("pe", self.mm_acc(pv2, 128, [hmT[:, k, b * 128:(b + 1) * 128] for k in range(KC)], [w[:, k, 1, :] for k in range(KC)]),
                         reads=[st["k"]] + [("hmT", k) for k in range(KC)], writes=[pk2])
                    sk2, sv2 = self.xnring.next()
                    P.op("act", lambda e, sv2=sv2, pv2=pv2: e.copy(out=sv2[:, 0:128], in_=pv2[:, 0:128]), reads=[pk2], writes=[sk2])
                    P.op("dve", lambda e, b=b, sv2=sv2: e.tensor_copy(out=vm[:, b, h * 128:(h + 1) * 128], in_=sv2[:, 0:128]), reads=[sk2], writes=[("vm", h, b)])
                    self.dma("sp", o_pmv[:, L, b, h * 128:(h + 1) * 128], sv2[:, 0:128], reads=[sk2], writes=[("o_pmv", L, h, b)], chan=sk2)
            self.add_item([load], comp)
        for ti, (c0, tw) in enumerate(self.tiles()):
            cgs = self.colgroups(c0, tw)
            self.prologue(c0, tw, V_XA + L)
            if ti == 1 and "noloadc" not in self.dbg:
                def loadc():
                    for s in range(NS):
                        self.dma("pool", V["ckv"][s][:, 0:1024], cmkT[:, L, s, :], reads=[], writes=[("ckv", s)], chan=("ckv", s))
                        self.dma("pool", V["ckv"][s][:, 1024:2048], cmv[:, L, s, :], reads=[], writes=[("ckv", s)], chan=("ckv", s))
                self.add_item([], loadc)
            for h in range(4):
                st = {}

                def load(h=h, st=st):
                    self.wload(st, w_q[Lw, h], 1, KC, 128)

                def comp(h=h, st=st, c0=c0, cgs=cgs, ti=ti):
                    w = st["w"]
                    for (j0, jw) in cgs:
                        pk, pv = self.psum()
                        P.op("pe", self.mm_acc(pv, jw, [w[:, k, 0, :] for k in range(KC)], [self.hT[:, k, j0:j0 + jw] for k in range(KC)]),
                             reads=[st["k"]] + [("hT", k, c0 + j0) for k in range(KC)], writes=[pk])
                        P.op("act", lambda e, pv=pv, j0=j0, jw=jw: e.copy(out=qT[:, h, j0:j0 + jw], in_=pv[:, 0:jw]), reads=[pk], writes=[("qT", h, c0 + j0)])
                        if "noattn" in self.dbg:
                            continue
                        if j0 < 1024:
                            if "noprompt" in self.dbg:
                                continue
                            r = self.pt_i % 2
                            self.pt_i += 1
                            self.attn_block(h, kmT[:, h, :], [("kmT", h)], vm[:, :, h * 128:(h + 1) * 128], [("vm", h, 0), ("vm", h, 1)],
                                            qT[:, h, j0:j0 + jw], ("qT", h, c0 + j0), oT[:, h, j0:j0 + jw], ("oT", h, c0 + j0), jw, V["PT"][r], ("PT", r))
                        else:
                            for s in range(NS):
                                if "nosample" in self.dbg:
                                    continue
                                r = self.pt_i % 2
                                self.pt_i += 1
                                ck = V["ckv"][s]
                                kT = ck[:, 0:1024].rearrange("p (h t) -> p h t", h=4)[:, h, :]
                                vT = ck[:, 1024:2048].rearrange("p (b t) -> p b t", b=2)[:, :, h * 128:(h + 1) * 128]
                                self.attn_block(h, kT, [("ckv", s)], vT, [("ckv", s)], qT[:, h, j0:j0 + NS], ("qT", h, c0 + j0),
                                                oT[:, h, j0 + s:j0 + s + 1], ("oT", h, c0 + j0), NS, V["PT"][r], ("PT", r), sel=s)
                self.add_item([load], comp)
            self.out_proj(c0, tw, lambda m: w_o[Lw, m], 4, lambda k, j0, jw: oT[:, k, j0:j0 + jw],
                          lambda gc: [("oT", k, gc) for k in range(4)], 1.0)

    def hgrn_mixer(self, L):
        P = self.P
        w_in = self.wt("hgrn_w_in", [1, KC, 128, 4 * KC * 128])
        w_out = self.wt("hgrn_w_out", [1, KC, 128, KC * 128])
        shg = self.din("shg", [128, NS, KC, 128])
        o_phg = self.dout("o_phg", [128, KC, 128])
        o_shg = self.dout("o_shg", [128, NS, KC, 128])
        B = self.B
        W = 1026
        yT = B[:, 0:16 * W].rearrange("p (c t) -> p c t", c=16)
        o = 16 * W
        Sst = B[:, o:o + 4096].bitcast(F32).rearrange("p (h e) -> p h e", h=16); o += 4096
        Sbf2 = B[:, o:o + 256].rearrange("p (h e) -> p h e", h=2); o += 256
        Am = [B[:, o + i * 128:o + (i + 1) * 128] for i in range(2)]; o += 256
        lbv = B[:, o:o + 5 * 32].bitcast(F32); o += 160
        A2 = self.arena[:, 16 * W:16 * TWF]
        a = 0
        wk = []
        for i in range(5):
            wk.append(A2[:, a:a + 2 * W].bitcast(F32)); a += 2 * W
        w1, w2, w3, w4, vT = wk
        w4b = w4.bitcast(BF16)
        khb, vtb = w4b[:, 0:W], w4b[:, W:2 * W]
        qt = A2[:, a:a + W]; a += W
        kt = A2[:, a:a + W]; a += W
        gt = A2[:, a:a + W]; a += W
        vtok = A2[:, a:a + 1024].rearrange("p (b e) -> p b e", b=8); a += 1024
        ktok = A2[:, a:a + 1024].rearrange("p (b e) -> p b e", b=8); a += 1024
        assert a <= 16 * TWF - 16 * W, a
        wp = 1024
        self.barrier_item()

        def lbcomp():
            E = lbv[:, 0:64].rearrange("p (l h) -> p l h", l=4)
            for l in range(4):
                P.op("act", lambda e, l=l: e.activation(out=E[:, l, :], in_=self.vec[:, (V_LB + l) * 16:(V_LB + l + 1) * 16], func=AF.Exp), writes=[("lbE", l)])
            P.op("dve", lambda e: e.tensor_tensor(out=lbv[:, 64:80], in0=E[:, 0, :], in1=E[:, 1, :], op=ALU.add), reads=[("lbE", 0), ("lbE", 1)], writes=["lbs"])
            P.op("dve", lambda e: e.tensor_tensor(out=lbv[:, 64:80], in0=lbv[:, 64:80], in1=E[:, 2, :], op=ALU.add), reads=[("lbE", 2), "lbs"], writes=["lbs"])
            P.op("dve", lambda e: e.tensor_tensor(out=lbv[:, 64:80], in0=lbv[:, 64:80], in1=E[:, 3, :], op=ALU.add), reads=[("lbE", 3), "lbs"], writes=["lbs"])
            P.op("dve", lambda e: e.reciprocal(out=lbv[:, 64:80], in_=lbv[:, 64:80]), reads=["lbs"], writes=["lbs"])
            P.op("dve", lambda e: e.tensor_copy(out=lbv[:, 0:16], in_=E[:, 1, :]), reads=[("lbE", 1)], writes=[("lbE", 0)])
            for j in range(2, L + 1):
                P.op("dve", lambda e, j=j: e.tensor_tensor(out=lbv[:, 0:16], in0=lbv[:, 0:16], in1=E[:, j, :], op=ALU.add), reads=[("lbE", j), ("lbE", 0)], writes=[("lbE", 0)])
            P.op("dve", lambda e: e.tensor_tensor(out=lbv[:, 16:32], in0=lbv[:, 0:16], in1=lbv[:, 64:80], op=ALU.mult), reads=[("lbE", 0), "lbs"], writes=["lb"])
            P.op("dve", lambda e: e.tensor_scalar(out=lbv[:, 32:48], in0=lbv[:, 16:32], scalar1=-1.0, scalar2=1.0, op0=ALU.mult, op1=ALU.add), reads=["lb"], writes=["omlb"])
            P.op("dve", lambda e: e.tensor_scalar(out=lbv[:, 48:64], in0=lbv[:, 32:48], scalar1=-1.0, scalar2=None, op0=ALU.mult), reads=["omlb"], writes=["nomlb"])
            P.op("dve", lambda e: e.memset(Sst[:, :, :], 0.0), writes=[("S", h) for h in range(16)])
        self.add_item([], lbcomp)
        LB = lambda h: lbv[:, 16 + h:17 + h]
        OM = lambda h: lbv[:, 32 + h:33 + h]
        NOM = lambda h: lbv[:, 48 + h:49 + h]

        for ti, (c0, tw) in enumerate(self.tiles()):
            cgs = self.colgroups(c0, tw)
            self.prologue(c0, tw, V_MIX + L)
            for hd in range(KC):
                stA, stB = {}, {}

                def loadA(hd=hd, st=stA):
                    self.wload(st, w_in[0, hd][:, 0:4096], 2, KC, 128)

                def compA(hd=hd, st=stA, c0=c0, cgs=cgs):
                    for (j0, jw) in cgs:
                        qk, qv = self.psum()
                        fk, fv = self.psum()

                        def mm(e, j0=j0, jw=jw, qv=qv, fv=fv, v=st["w"]):
                            for t, pv in ((0, qv), (1, fv)):
                                for k in range(KC):
                                    ins = e.matmul(pv[:, 0:jw], lhsT=v[:, k, t, :], rhs=self.hT[:, k, j0:j0 + jw], start=(k == 0), stop=(k == KC - 1))
                            return ins
                        P.op("pe", mm, reads=[st["k"]] + [("hT", c, c0 + j0) for c in range(KC)], writes=[qk, fk])
                        P.op("act", lambda e, qv=qv, j0=j0, jw=jw: e.activation(out=w1[:, j0:j0 + jw], in_=qv[:, 0:jw], func=AF.Silu), reads=[qk], writes=["w1"])
                        P.op("act", lambda e, fv=fv, j0=j0, jw=jw: e.activation(out=w2[:, j0:j0 + jw], in_=fv[:, 0:jw], func=AF.Sigmoid), reads=[fk], writes=["w2"])
                self.add_item([loadA], compA)

                def loadB(hd=hd, st=stB):
                    self.wload(st, w_in[0, hd][:, 4096:8192], 2, KC, 128)

                def compB(hd=hd, st=stB, c0=c0, cgs=cgs, ti=ti, tw=tw):
                    for (j0, jw) in cgs:
                        ik, iv = self.psum()
                        gk_, gv = self.psum()

                        def mm(e, j0=j0, jw=jw, iv=iv, gv=gv, v=st["w"]):
                            for t, pv in ((0, iv), (1, gv)):
                                for k in range(KC):
                                    ins = e.matmul(pv[:, 0:jw], lhsT=v[:, k, t, :], rhs=self.hT[:, k, j0:j0 + jw], start=(k == 0), stop=(k == KC - 1))
                            return ins
                        P.op("pe", mm, reads=[st["k"]] + [("hT", c, c0 + j0) for c in range(KC)], writes=[ik, gk_])
                        P.op("act", lambda e, iv=iv, j0=j0, jw=jw: e.copy(out=vT[:, j0:j0 + jw], in_=iv[:, 0:jw]), reads=[ik], writes=["vT"])
                        P.op("act", lambda e, gv=gv, j0=j0, jw=jw: e.activation(out=gt[:, j0:j0 + jw], in_=gv[:, 0:jw], func=AF.Silu), reads=[gk_], writes=["gt"])
                    P.op("dve", lambda e: e.tensor_scalar(out=w3[:, 0:tw], in0=w2[:, 0:tw], scalar1=OM(hd), scalar2=LB(hd), op0=ALU.mult, op1=ALU.add), reads=["w2", "lb", "omlb"], writes=["w3"])
                    if ti == 1:
                        P.op("dve", lambda e: e.tensor_copy(out=self.fS[:, 0:NS], in_=w3[:, wp:wp + NS]), reads=["w3"], writes=["fS"])
                    P.op("act", lambda e: e.activation(out=w3[:, 0:wp], in_=w3[:, 0:wp], func=AF.Ln), reads=["w3"], writes=["w3"])
                    P.op("dve", lambda e: e.tensor_scalar(out=w2[:, 0:tw], in0=w2[:, 0:tw], scalar1=NOM(hd), scalar2=OM(hd), op0=ALU.mult, op1=ALU.add), reads=["w2", "nomlb", "omlb"], writes=["w2"])
                    P.op("dve", lambda e: e.tensor_tensor_scan(out=w4[:, 0:wp], data0=self.scanmask[:, 0:wp], data1=w3[:, 0:wp], initial=0.0, op0=ALU.mult, op1=ALU.add),
                         reads=["w3"], writes=["w4"])
                    P.op("act", lambda e: e.activation(out=w3[:, 0:wp], in_=w4[:, 0:wp], func=AF.Exp), reads=["w4"], writes=["w3"])
                    P.op("dve", lambda e: e.tensor_tensor(out=qt[:, 0:wp], in0=w1[:, 0:wp], in1=w3[:, 0:wp], op=ALU.mult), reads=["w1", "w3"], writes=["qt"])
                    if ti == 1:
                        P.op("dve", lambda e: e.tensor_copy(out=self.qS[:, 0:NS], in_=w1[:, wp:wp + NS]), reads=["w1"], writes=["qS"])
                    P.op("act", lambda e: e.activation(out=w1[:, 0:wp], in_=w4[:, 0:wp], func=AF.Exp, scale=-1.0), reads=["w4", "qt", "qS"], writes=["w1"])
                    if ti == 1:
                        P.op("dve", lambda e: e.tensor_copy(out=self.kS[:, 0:NS], in_=w2[:, wp:wp + NS]), reads=["w2"], writes=["kS"])
                    P.op("dve", lambda e: e.tensor_tensor(out=w2[:, 0:wp], in0=w2[:, 0:wp], in1=w1[:, 0:wp], op=ALU.mult), reads=["w2", "w1"], writes=["w2"])
                    P.op("act", lambda e: e.copy(out=kt[:, 0:wp], in_=w2[:, 0:wp]), reads=["w2"], writes=["kt"])
                    Ev = w3[:, 0:wp].rearrange("p (c t) -> p c t", t=64)
                    P.op("dve", lambda e: e.tensor_tensor(out=khb[:, 0:wp].rearrange("p (c t) -> p c t", t=64), in0=w2[:, 0:wp].rearrange("p (c t) -> p c t", t=64),
                                                          in1=Ev[:, :, 63:64].broadcast_to([128, wp // 64, 64]), op=ALU.mult), reads=["w2", "w3"], writes=["w4"])
                    P.op("pool", lambda e: e.tensor_copy(out=vtb[:, 0:wp], in_=vT[:, 0:wp]), reads=["vT", "w4"], writes=["w4"])
                    for (src, dst, dkey, eng) in ((khb, ktok, "ktok", "act"), (vtb, vtok, "vtok", "dve")):
                        pk, pv = self.psum()
                        pvb = pv.bitcast(BF16)

                        def tr(e, src=src, pvb=pvb):
                            for b in range(8):
                                ins = e.transpose(out=pvb[:, b * 128:(b + 1) * 128], in_=src[:, b * 128:(b + 1) * 128], identity=self.identB[:, :])
                            return ins
                        P.op("pe", tr, reads=["w4"], writes=[pk])
                        if eng == "act":
                            P.op("act", lambda e, pvb=pvb, dst=dst: e.copy(out=dst[:, :, :], in_=pvb[:, 0:1024].rearrange("p (b e) -> p b e", b=8)), reads=[pk], writes=[dkey])
                        else:
                            P.op("dve", lambda e, pvb=pvb, dst=dst: e.tensor_copy(out=dst[:, :, :], in_=pvb[:, 0:1024].rearrange("p (b e) -> p b e", b=8)), reads=[pk], writes=[dkey])
                    sb2 = [Sbf2[:, 0, :], Sbf2[:, 1, :]]
                    P.op("act", lambda e: e.copy(out=sb2[0], in_=Sst[:, hd, :]), reads=[("S", hd)], writes=[("Sb2", 0)])

                    def pre(b):
                        cs = slice(b * 128, (b + 1) * 128)
                        ak, av = self.psum()
                        P.op("pe", self.mm_acc(av, 128, [kt[:, cs]], [qt[:, cs]]), reads=["kt", "qt"], writes=[ak])
                        r = self.am_i % 2
                        self.am_i += 1
                        P.op("dve", lambda e, av=av, r=r: e.tensor_tensor(out=Am[r], in0=av[:, 0:128], in1=self.mask2[:, :], op=ALU.mult), reads=[ak], writes=[("Am", r)])
                        svs = []
                        for c in range(2):
                            sk, sv = self.psum()
                            ps_ = slice(c * 64, (c + 1) * 64)
                            P.op("pe", lambda e, sv=sv, b=b, ps_=ps_: e.matmul(sv[:, 0:128], lhsT=ktok[ps_, b, :], rhs=vtok[ps_, b, :], start=True, stop=True),
                                 reads=["ktok", "vtok"], writes=[sk])
                            svs.append((sk, sv))
                        return (b, r, svs)

                    ver = 0
                    pend = pre(0)
                    for b in range(8):
                        cur = pend
                        pend = pre(b + 1) if b + 1 < 8 else None
                        _, r, svs = cur
                        cs = slice(b * 128, (b + 1) * 128)
                        ok_, ov = self.psum()
                        P.op("pe", lambda e, ov=ov, r=r, b=b: e.matmul(ov[:, 0:128], lhsT=vtok[:, b, :], rhs=Am[r], start=True, stop=False),
                             reads=["vtok", ("Am", r)], writes=[ok_])
                        for c in range(2):
                            cc = slice(b * 128 + c * 64, b * 128 + (c + 1) * 64)
                            P.op("pe", lambda e, ov=ov, c=c, cc=cc, ver=ver: e.matmul(ov[:, c * 64:(c + 1) * 64], lhsT=sb2[ver], rhs=qt[:, cc], start=False, stop=(c == 1)),
                                 reads=[("Sb2", ver), "qt"], writes=[ok_])
                            sk, sv = svs[c]
                            el = w3[:, b * 128 + c * 64 + 63:b * 128 + c * 64 + 64]
                            P.op("dve", lambda e, sv=sv, el=el: e.scalar_tensor_tensor(out=Sst[:, hd, :], in0=Sst[:, hd, :], scalar=el, in1=sv[:, 0:128], op0=ALU.mult, op1=ALU.add),
                                 reads=[sk, "w3", ("S", hd)], writes=[("S", hd)])
                            ver ^= 1
                            P.op("act", lambda e, ver=ver: e.copy(out=sb2[ver], in_=Sst[:, hd, :]), reads=[("S", hd)], writes=[("Sb2", ver)])
                        P.op("act", lambda e, ov=ov, cs=cs: e.copy(out=w1[:, cs], in_=ov[:, 0:128]), reads=[ok_], writes=["w1"])
                    if ti == 1:
                        self.dma("sp", o_phg[:, hd, :], Sst[:, hd, :], reads=[("S", hd)], writes=[("o_phg", hd)], chan=("ophg", hd % 2))
                        self.hgrn_sample(hd, shg, o_shg, vT, wp, w1)
                    for (j0, jw) in cgs:
                        P.op("act", lambda e, j0=j0, jw=jw: e.activation(out=qt[:, j0:j0 + jw], in_=w1[:, j0:j0 + jw], func=AF.Square), reads=["w1"], writes=["qt"])
                        pk, pv = self.psum()
                        P.op("pe", self.mm_acc(pv, jw, [self.ones1[:, :]], [qt[:, j0:j0 + jw]]), reads=["qt"], writes=[pk])
                        tk, tv = self.tmpring.next()
                        P.op("act", lambda e, pv=pv, tv=tv, jw=jw: e.activation(out=tv[:, 0:jw], in_=pv[:, 0:jw], func=AF.Sqrt, bias=self.epsc[:, 0:1], scale=1.0 / 128), reads=[pk], writes=[tk])
                        P.op("dve", lambda e, tv=tv, jw=jw: e.reciprocal(out=tv[:, 0:jw], in_=tv[:, 0:jw]), reads=[tk], writes=[tk])
                        gn = self.vec[:, V_HGN * 16 + hd:V_HGN * 16 + hd + 1]
                        P.op("dve", lambda e, tv=tv, j0=j0, jw=jw, gn=gn: e.scalar_tensor_tensor(out=tv[:, 0:jw], in0=w1[:, j0:j0 + jw], scalar=gn, in1=tv[:, 0:jw], op0=ALU.mult, op1=ALU.mult),
                             reads=[tk, "w1"], writes=[tk])
                        P.op("dve", lambda e, tv=tv, j0=j0, jw=jw: e.tensor_tensor(out=yT[:, hd, j0:j0 + jw], in0=tv[:, 0:jw], in1=gt[:, j0:j0 + jw], op=ALU.mult),
                             reads=[tk, "gt"], writes=[("yT", hd, c0 + j0)])
                self.add_item([loadB], compB)
            self.out_proj(c0, tw, lambda m: w_out[0, m], KC, lambda k, j0, jw: yT[:, k, j0:j0 + jw],
                          lambda gc: [("yT", c, gc) for c in range(KC)], 1.0)

    def hgrn_sample(self, hd, shg, o_shg, vT, wp, w1):
        P = self.P
        fS, qS, kS = self.fS, self.qS, self.kS
        P.op("dve", lambda e: e.tensor_tensor(out=self.fq16[:, 0:NS], in0=fS[:, 0:NS], in1=qS[:, 0:NS], op=ALU.mult), reads=["fS", "qS"], writes=["fq"])
        P.op("dve", lambda e: e.tensor_tensor(out=self.fq16[:, NS:2 * NS], in0=kS[:, 0:NS], in1=qS[:, 0:NS], op=ALU.mult), reads=["kS", "qS"], writes=["fq"])
        kqk, kqv = self.psum()
        P.op("pe", self.mm_acc(kqv, NS, [self.ones1[:, :]], [self.fq16[:, NS:2 * NS]]), reads=["fq"], writes=[kqk])
        P.op("dve", lambda e: e.tensor_tensor(out=self.fq[:, 2 * NS:3 * NS], in0=kqv[:, 0:NS], in1=vT[:, wp:wp + NS], op=ALU.mult), reads=[kqk, "vT"], writes=["fq2"])
        for s in range(NS):
            r = self.s0_i % 2
            self.s0_i += 1
            s0 = self.S0[:, r, :]
            s0b = self.S0b[:, r, :]
            self.dma("sp", s0, shg[:, s, hd, :], reads=[], writes=[("S0", r)], chan=("S0", r))
            P.op("act", lambda e, s0=s0, s0b=s0b: e.copy(out=s0b, in_=s0), reads=[("S0", r)], writes=[("S0b", r)])
            ok_, ov = self.psum()
            P.op("pe", self.mm_acc(ov, NS, [s0b], [self.fq16[:, 0:NS]]), reads=[("S0b", r), "fq"], writes=[ok_])
            P.op("dve", lambda e, ov=ov, s=s: e.tensor_tensor(out=w1[:, wp + s:wp + s + 1], in0=ov[:, s:s + 1], in1=self.fq[:, 2 * NS + s:2 * NS + s + 1], op=ALU.add),
                 reads=[ok_, "fq2"], writes=["w1"])
            dg = self.dg16[:, r, :]
            P.op("dve", lambda e, dg=dg, s=s: e.tensor_scalar(out=dg, in0=self.identB[:, :], scalar1=vT[:, wp + s:wp + s + 1], scalar2=None, op0=ALU.mult), reads=["vT"], writes=[("dg16", r)])
            bk, bv = self.psum()
            P.op("pe", self.mm_acc(bv, 128, [self.ones1[:, :]], [dg]), reads=[("dg16", r)], writes=[bk])
            tk, tv = self.tmpring.next()
            P.op("dve", lambda e, bv=bv, tv=tv, s=s: e.tensor_scalar(out=tv[:, 0:128], in0=bv[:, 0:128], scalar1=kS[:, s:s + 1], scalar2=None, op0=ALU.mult), reads=[bk, "kS"], writes=[tk])
            xk_, xv = self.xnring.next()
            P.op("dve", lambda e, tv=tv, xv=xv, s0=s0, s=s: e.scalar_tensor_tensor(out=xv[:, 0:128], in0=s0, scalar=fS[:, s:s + 1], in1=tv[:, 0:128], op0=ALU.mult, op1=ALU.add),
                 reads=[tk, ("S0", r), "fS"], writes=[xk_])
            self.dma("sp", o_shg[:, s, hd, :], xv[:, 0:128], reads=[xk_], writes=[("o_shg", s, hd)], chan=xk_)

    def attn_mixer(self, L):
        P = self.P
        w_qkv = self.wt("attn_w_qkv", [1, 3, KC, 128, 3 * KC * 128])
        w_out = self.wt("attn_w_out", [1, KC, 128, KC * 128])
        cwkT = self.din("cwkT", [128, 3, NS, KC, 128])
        cwv = self.din("cwv", [128, 3, NS, KC, 128])
        WG = (128, 512, 2048)
        DIL = (1, 4, 16)
        NBR = (16, 4, 1)
        o_pwk = [self.dout("o_pwk%d" % g, [128, KC, WG[g]]) for g in range(3)]
        o_pwv = [self.dout("o_pwv%d" % g, [128, KC, WG[g]]) for g in range(3)]
        o_swk = self.dout("o_swk", [128, 3, KC, NS])
        o_swv = self.dout("o_swv", [128, 3, KC, NS])
        oscr = self.nc.dram_tensor("oscr", [128, KC, TWF], BF16).ap()
        B = self.B
        hTf = self.hTf
        o = 0
        qp = B[:, o:o + 2048]; o += 2048
        kp = B[:, o:o + 2048]; o += 2048
        vtok = B[:, o:o + 2048].rearrange("p (b e) -> p b e", b=16); o += 2048
        acc_o = B[:, o:o + 4096].bitcast(F32); o += 4096
        PT = [B[:, o + i * 1024:o + (i + 1) * 1024].rearrange("p (c t) -> p c t", c=2) for i in range(2)]; o += 2048
        cosT = B[:, o:o + TWF]; o += TWF
        sinT = B[:, o:o + TWF]; o += TWF
        xb = [B[:, o + i * 512:o + (i + 1) * 512] for i in range(2)]; o += 1024
        xf = [B[:, o + i * 1024:o + (i + 1) * 1024].bitcast(F32) for i in range(2)]; o += 2048
        oTb = B[:, o:o + TWF]; o += TWF
        kcv = [B[:, o + i * 256 * NS:o + (i + 1) * 256 * NS] for i in range(2)]; o += 512 * NS
        vTb = B[:, o:o + 2048]; o += 2048
        assert o <= NB_EL, o
        acc_d = self.yr[:, :, :].rearrange("p a b -> p (a b)")[:, 0:2048]
        scale = 128 ** -0.5
        self.barrier_item()

        def setup():
            self.dma("pool", cosT, self.cst[:, C_COS:C_COS + TWF], reads=[], writes=["cosT"], chan="ccos")
            self.dma("pool", sinT, self.cst[:, C_SIN:C_SIN + TWF], reads=[], writes=["sinT"], chan="csin")
        self.add_item([], setup)
        for (c0, tw) in self.tiles():
            self.prologue(c0, tw, V_MIX + L, hview=hTf, hoff=c0)
        cgs = self.colgroups(0, TWF)
        allh = [("hT", k, j0) for k in range(KC) for (j0, _) in cgs]

        for hd in range(KC):
            for g in range(3):
                d = DIL[g]
                stQ, stV, stA = {}, {}, {}
                base = (g * 3) * KC * 128 + hd * 128

                def loadQ(st=stQ, g=g, hd=hd):
                    self.wload(st, w_qkv[0, g, hd][:, 0:4096], 2, KC, 128)

                def compQ(st=stQ, hd=hd, g=g, d=d):
                    w = st["w"]

                    def proj(j0, jw):
                        qk_, qv = self.psum()
                        kk_, kv = self.psum()

                        def mm(e, j0=j0, jw=jw, qv=qv, kv=kv):
                            for t, pv in ((0, qv), (1, kv)):
                                for k in range(KC):
                                    ins = e.matmul(pv[:, 0:jw], lhsT=w[:, k, t, :], rhs=hTf[:, k, j0:j0 + jw], start=(k == 0), stop=(k == KC - 1))
                            return ins
                        P.op("pe", mm, reads=[st["k"]] + [("hT", k, j0) for k in range(KC)], writes=[qk_, kk_])
                        return (j0, jw, qk_, qv, kk_, kv)

                    pend = None
                    for cg_ in list(cgs) + [None]:
                        nxt = proj(cg_[0], cg_[1]) if cg_ is not None else None
                        cur, pend = pend, nxt
                        if cur is None:
                            continue
                        j0, jw, qk_, qv, kk_, kv = cur
                        for t, pk, pv in ((0, qk_, qv), (1, kk_, kv)):
                            r = self.xf_i % 2
                            self.xf_i += 1
                            P.op("act", lambda e, pv=pv, r=r, jw=jw: e.copy(out=xf[r][:, 0:jw], in_=pv[:, 0:jw]), reads=[pk], writes=[("xf", r)])
                            P.op("pool", lambda e, r=r, jw=jw: e.tensor_copy(out=xb[r][:, 0:jw], in_=xf[r][:, 0:jw]), reads=[("xf", r)], writes=[("xb", r)])
                            rk, rv = self.psum()
                            P.op("pe", self.mm_acc(rv, jw, [self.Pm[:, :]], [xb[r][:, 0:jw]]), reads=[("xb", r)], writes=[rk])
                            t1k, t1 = self.tmpring.next()
                            P.op("dve", lambda e, t1=t1, r=r, j0=j0, jw=jw: e.tensor_tensor(out=t1[:, 0:jw], in0=xf[r][:, 0:jw], in1=cosT[:, j0:j0 + jw], op=ALU.mult),
                                 reads=[("xf", r), "cosT"], writes=[t1k])
                            t2k, t2 = self.tmpring.next()
                            P.op("dve", lambda e, t2=t2, rv=rv, j0=j0, jw=jw: e.tensor_tensor(out=t2[:, 0:jw], in0=rv[:, 0:jw], in1=sinT[:, j0:j0 + jw], op=ALU.mult),
                                 reads=[rk, "sinT"], writes=[t2k])
                            if j0 < 2048:
                                dstbuf, dkey = (qp, "qp") if t == 0 else (kp, "kp")
                                dst = dstbuf.rearrange("p (r m) -> p m r", r=d)[:, j0 // d:(j0 + jw) // d, :]
                            if t == 0:
                                if j0 < 2048:
                                    P.op("dve", lambda e, t1=t1, t2=t2, dst=dst, jw=jw: e.tensor_tensor(out=dst, in0=t1[:, 0:jw].rearrange("p (m r) -> p m r", r=d),
                                                                                                      in1=t2[:, 0:jw].rearrange("p (m r) -> p m r", r=d), op=ALU.add),
                                         reads=[t1k, t2k], writes=["qp"])
                                else:
                                    P.op("dve", lambda e, t1=t1, t2=t2: e.tensor_tensor(out=self.qS16[:, 0:NS], in0=t1[:, 0:NS], in1=t2[:, 0:NS], op=ALU.add), reads=[t1k, t2k], writes=["qS16"])
                            else:
                                fk, kf = self.xnring.next()
                                P.op("dve", lambda e, t1=t1, t2=t2, kf=kf, jw=jw: e.tensor_tensor(out=kf[:, 0:jw], in0=t1[:, 0:jw], in1=t2[:, 0:jw], op=ALU.add), reads=[t1k, t2k], writes=[fk])
                                if j0 < 2048:
                                    P.op("act", lambda e, kf=kf, dst=dst, jw=jw: e.copy(out=dst, in_=kf[:, 0:jw].rearrange("p (m r) -> p m r", r=d)), reads=[fk], writes=["kp"])
                                    lo = 2048 - WG[g]
                                    a = max(lo, j0)
                                    if a < j0 + jw:
                                        self.dma("sp", o_pwk[g][:, hd, a - lo:j0 + jw - lo], kf[:, a - j0:jw], reads=[fk], writes=[("o_pwk", g, hd, j0)], chan=fk)
                                else:
                                    P.op("act", lambda e, kf=kf: e.copy(out=self.kSf[:, 0:NS], in_=kf[:, 0:NS]), reads=[fk], writes=["kSf"])
                                    self.dma("sp", o_swk[:, g, hd, :], kf[:, 0:NS], reads=[fk], writes=[("o_swk", g, hd)], chan=fk)
                self.add_item([loadQ], compQ)

                def loadV(st=stV, g=g, hd=hd):
                    self.wload(st, w_qkv[0, g, hd][:, 4096:6144], 1, KC, 128)

                def compV(st=stV, hd=hd, g=g, d=d):
                    w = st["w"]
                    nb = 2048 // d // 128
                    lo = 2048 - WG[g]
                    for (j0, jw) in cgs:
                        pk, pv = self.psum()
                        P.op("pe", self.mm_acc(pv, jw, [w[:, k, 0, :] for k in range(KC)], [hTf[:, k, j0:j0 + jw] for k in range(KC)]),
                             reads=[st["k"]] + [("hT", k, j0) for k in range(KC)], writes=[pk])
                        sk, sv = self.xnring.next()
                        P.op("act", lambda e, pv=pv, sv=sv, jw=jw: e.copy(out=sv[:, 0:jw], in_=pv[:, 0:jw]), reads=[pk], writes=[sk])
                        if j0 < 2048:
                            P.op("dve", lambda e, sv=sv, j0=j0, jw=jw: e.tensor_copy(out=vTb[:, j0:j0 + jw], in_=sv[:, 0:jw]), reads=[sk], writes=[("vTb", j0)])
                            a = max(lo, j0)
                            if a < j0 + jw:
                                self.dma("sp", o_pwv[g][:, hd, a - lo:j0 + jw - lo], sv[:, a - j0:jw], reads=[sk], writes=[("o_pwv", g, hd, j0)], chan=sk)
                        else:
                            P.op("dve", lambda e, sv=sv: e.tensor_copy(out=self.vSf[:, 0:NS], in_=sv[:, 0:NS]), reads=[sk], writes=["vSf"])
                            self.dma("sp", o_swv[:, g, hd, :], sv[:, 0:NS], reads=[sk], writes=[("o_swv", g, hd)], chan=sk)
                    vperm = vTb.rearrange("p (m r) -> p r m", r=d)
                    for half in range(2):
                        pk, pv = self.psum()
                        pvb = pv.bitcast(BF16)

                        def tr(e, half=half, pvb=pvb):
                            for i in range(8):
                                blk = half * 8 + i
                                r_, mi = blk // nb, blk % nb
                                ins = e.transpose(out=pvb[:, i * 128:(i + 1) * 128], in_=vperm[:, r_, mi * 128:(mi + 1) * 128], identity=self.identB[:, :])
                            return ins
                        P.op("pe", tr, reads=[("vTb", j) for j in (0, 512, 1024, 1536)], writes=[pk])
                        P.op("dve", lambda e, half=half, pvb=pvb: e.tensor_copy(out=vtok[:, half * 8:(half + 1) * 8, :], in_=pvb[:, 0:1024].rearrange("p (b e) -> p b e", b=8)),
                             reads=[pk], writes=[("vtok", 2 * half), ("vtok", 2 * half + 1)])
                self.add_item([loadV], compV)

                def loadA(st=stA, hd=hd, g=g):
                    r = self.kcv_i % 2
                    self.kcv_i += 1
                    st["r"] = r
                    self.dma("pool", kcv[r][:, 0:128 * NS].rearrange("p (s e) -> p s e", s=NS), cwkT[:, g, :, hd, :], reads=[], writes=[("kcv", r)], chan=("kcv", r))
                    self.dma("pool", kcv[r][:, 128 * NS:256 * NS].rearrange("p (s e) -> p s e", s=NS), cwv[:, g, :, hd, :], reads=[], writes=[("kcv", r)], chan=("kcv", r))

                def compA(st=stA, hd=hd, g=g, d=d):
                    nbr = NBR[g]

                    def scores(q4):
                        qbs = [q4 * 4 + i for i in range(4)]
                        hp = [(qb % nbr) != 0 for qb in qbs]
                        r = self.pt_i % 2
                        self.pt_i += 1
                        ck, cv_ = self.psum()

                        def mmc(e, qbs=qbs, cv_=cv_):
                            for i, qb in enumerate(qbs):
                                e.matmul(cv_[:, i * 128:(i + 1) * 128], lhsT=kp[:, qb * 128:(qb + 1) * 128], rhs=qp[:, qb * 128:(qb + 1) * 128], start=True, stop=False)
                                ins = e.matmul(cv_[:, i * 128:(i + 1) * 128], lhsT=self.identB[:, :], rhs=self.maskC[:, :], start=False, stop=True)
                            return ins
                        P.op("pe", mmc, reads=["kp", "qp"], writes=[ck])
                        P.op("act", lambda e, cv_=cv_, r=r: e.activation(out=PT[r][:, 0, :], in_=cv_[:, 0:512], func=AF.Exp, scale=scale), reads=[ck], writes=[("PT", r, 0)])
                        if any(hp):
                            pk_, pv_ = self.psum()

                            def mmp(e, qbs=qbs, pv_=pv_, hp=hp):
                                for i, qb in enumerate(qbs):
                                    if not hp[i]:
                                        continue
                                    e.matmul(pv_[:, i * 128:(i + 1) * 128], lhsT=kp[:, (qb - 1) * 128:qb * 128], rhs=qp[:, qb * 128:(qb + 1) * 128], start=True, stop=False)
                                    ins = e.matmul(pv_[:, i * 128:(i + 1) * 128], lhsT=self.identB[:, :], rhs=self.maskP[:, :], start=False, stop=True)
                                return ins
                            P.op("pe", mmp, reads=["kp", "qp"], writes=[pk_])
                            lo_ = 0 if hp[0] else 128
                            P.op("act", lambda e, pv_=pv_, r=r, lo_=lo_: e.activation(out=PT[r][:, 1, lo_:512], in_=pv_[:, lo_:512], func=AF.Exp, scale=scale), reads=[pk_], writes=[("PT", r, 1)])

                        return (q4, qbs, hp, r)

                    def pvstage(info):
                        q4, qbs, hp, r = info
                        ok_, ov = self.psum()
                        dk_, dv = self.psum()
                        for (okk, outv, is_den) in ((ok_, ov, False), (dk_, dv, True)):
                            def mmo(e, qbs=qbs, outv=outv, hp=hp, is_den=is_den, r=r):
                                for i, qb in enumerate(qbs):
                                    l1 = self.ones1[:, :] if is_den else vtok[:, qb, :]
                                    ins = e.matmul(outv[:, i * 128:(i + 1) * 128], lhsT=l1, rhs=PT[r][:, 0, i * 128:(i + 1) * 128], start=True, stop=not hp[i])
                                    if hp[i]:
                                        l2 = self.ones1[:, :] if is_den else vtok[:, qb - 1, :]
                                        ins = e.matmul(outv[:, i * 128:(i + 1) * 128], lhsT=l2, rhs=PT[r][:, 1, i * 128:(i + 1) * 128], start=False, stop=True)
                                return ins
                            P.op("pe", mmo, reads=[("PT", r, 0), ("PT", r, 1)] + [("vtok", b) for b in range(4)], writes=[okk])
                        if g == 0:
                            vo = acc_o[:, q4 * 512:(q4 + 1) * 512]
                            vd = acc_d[:, q4 * 512:(q4 + 1) * 512]
                            P.op("dve", lambda e, vo=vo, ov=ov: e.tensor_copy(out=vo, in_=ov[:, 0:512]), reads=[ok_], writes=["acc_o"])
                            P.op("act", lambda e, vd=vd, dv=dv: e.copy(out=vd, in_=dv[:, 0:512]), reads=[dk_], writes=["acc_d"])
                        else:
                            if g == 1:
                                vo = acc_o.rearrange("p (m r) -> p r m", r=4)[:, q4, :]
                                vd = acc_d.rearrange("p (m r) -> p r m", r=4)[:, q4, :]
                                io, id_ = ov[:, 0:512], dv[:, 0:512]
                            else:
                                vo = acc_o.rearrange("p (m r) -> p r m", r=16)[:, q4 * 4:(q4 + 1) * 4, :]
                                vd = acc_d.rearrange("p (m r) -> p r m", r=16)[:, q4 * 4:(q4 + 1) * 4, :]
                                io, id_ = ov[:, 0:512].rearrange("p (r m) -> p r m", r=4), dv[:, 0:512].rearrange("p (r m) -> p r m", r=4)
                            P.op("dve", lambda e, vo=vo, io=io: e.tensor_tensor(out=vo, in0=vo, in1=io, op=ALU.add), reads=[ok_, "acc_o"], writes=["acc_o"])
                            P.op("dve", lambda e, vd=vd, id_=id_: e.tensor_tensor(out=vd, in0=vd, in1=id_, op=ALU.add), reads=[dk_, "acc_d"], writes=["acc_d"])

                    pend = None
                    for q4_ in list(range(4)) + [None]:
                        nxt = scores(q4_) if q4_ is not None else None
                        cur, pend = pend, nxt
                        if cur is not None:
                            pvstage(cur)
                    r = st["r"]
                    kcT = kcv[r][:, 0:128 * NS].rearrange("p (s e) -> p s e", s=NS)
                    vc = kcv[r][:, 128 * NS:256 * NS].rearrange("p (s e) -> p s e", s=NS)
                    P.op("dve", lambda e: e.tensor_tensor(out=self.pr16[:, 0:NS], in0=self.qS16[:, 0:NS], in1=self.kSf[:, 0:NS], op=ALU.mult), reads=["qS16", "kSf"], writes=["pr16"])
                    nk_, nv = self.psum()
                    P.op("pe", self.mm_acc(nv, NS, [self.ones1[:, :]], [self.pr16[:, 0:NS]]), reads=["pr16"], writes=[nk_])
                    sk_, sv_ = self.psum()

                    def mms(e, sv_=sv_):
                        for s in range(NS):
                            ins = e.matmul(sv_[:, s * NS:(s + 1) * NS], lhsT=kcT[:, s, :], rhs=self.qS16[:, 0:NS], start=True, stop=True)
                        return ins
                    P.op("pe", mms, reads=[("kcv", r), "qS16"], writes=[sk_])
                    P.op("act", lambda e, sv_=sv_: e.activation(out=self.PS[:, 0:NS * NS], in_=sv_[:, 0:NS * NS], func=AF.Exp, scale=scale), reads=[sk_], writes=["PS"])
                    P.op("act", lambda e, nv=nv: e.activation(out=self.pn[:, 0:NS], in_=nv[:, 0:NS], func=AF.Exp, scale=scale), reads=[nk_], writes=["pn"])
                    ok_, ov = self.psum()
                    dk_, dv = self.psum()

                    def mmso(e, ov=ov, dv=dv):
                        for s in range(NS):
                            e.matmul(ov[:, s * NS:(s + 1) * NS], lhsT=vc[:, s, :], rhs=self.PS[:, s * NS:(s + 1) * NS], start=True, stop=True)
                        for s in range(NS):
                            ins = e.matmul(dv[:, s * NS:(s + 1) * NS], lhsT=self.ones1[:, :], rhs=self.PS[:, s * NS:(s + 1) * NS], start=True, stop=True)
                        return ins
                    P.op("pe", mmso, reads=[("kcv", r), "PS"], writes=[ok_, dk_])
                    P.op("dve", lambda e: e.tensor_tensor(out=self.vpn[:, 0:NS], in0=self.vSf[:, 0:NS], in1=self.pn[:, 0:NS], op=ALU.mult), reads=["vSf", "pn"], writes=["vpn"])
                    for s in range(NS):
                        P.op("dve", lambda e, s=s, ov=ov: e.tensor_tensor(out=self.aSo[:, g, s:s + 1], in0=ov[:, s * NS + s:s * NS + s + 1], in1=self.vpn[:, s:s + 1], op=ALU.add),
                             reads=[ok_, "vpn"], writes=[("aSo", g)])
                        P.op("dve", lambda e, s=s, dv=dv: e.tensor_tensor(out=self.aSd[:, g, s:s + 1], in0=dv[:, s * NS + s:s * NS + s + 1], in1=self.pn[:, s:s + 1], op=ALU.add),
                             reads=[dk_, "pn"], writes=[("aSd", g)])
                    if g == 2:
                        P.op("dve", lambda e: e.reciprocal(out=acc_d, in_=acc_d), reads=["acc_d"], writes=["acc_d"])
                        P.op("dve", lambda e: e.tensor_tensor(out=oTb[:, 0:2048], in0=acc_o, in1=acc_d, op=ALU.mult), reads=["acc_o", "acc_d"], writes=["oTb"])
                        for (buf, key) in ((self.aSo, "aSo"), (self.aSd, "aSd")):
                            P.op("dve", lambda e, buf=buf: e.tensor_tensor(out=buf[:, 0, :], in0=buf[:, 0, :], in1=buf[:, 1, :], op=ALU.add), reads=[(key, 0), (key, 1)], writes=[(key, 0)])
                            P.op("dve", lambda e, buf=buf: e.tensor_tensor(out=buf[:, 0, :], in0=buf[:, 0, :], in1=buf[:, 2, :], op=ALU.add), reads=[(key, 0), (key, 2)], writes=[(key, 0)])
                        P.op("dve", lambda e: e.reciprocal(out=self.aSd[:, 0, :], in_=self.aSd[:, 0, :]), reads=[("aSd", 0)], writes=[("aSd", 0)])
                        P.op("dve", lambda e: e.tensor_tensor(out=oTb[:, 2048:2048 + NS], in0=self.aSo[:, 0, :], in1=self.aSd[:, 0, :], op=ALU.mult), reads=[("aSo", 0), ("aSd", 0)], writes=["oTb"])
                        self.dma("sp", oscr[:, hd, :], oTb, reads=["oTb"], writes=[("oscr", hd)], chan="oscr")
                self.add_item([loadA], compA)

        self.barrier_item()
        oTj = [B[:, i * 8192:(i + 1) * 8192].rearrange("p (k t) -> p k t", k=KC) for i in range(2)]
        for ji, (j0, jw) in enumerate(cgs):
            r = ji % 2

            def loadO(j0=j0, jw=jw, r=r):
                self.dma("sp", oTj[r][:, :, 0:jw], oscr[:, :, j0:j0 + jw], reads=[("oscr", h) for h in range(KC)], writes=[("oTj", r)], chan=("oTj", r))
            self.add_item([], loadO)
            self.out_proj(j0, jw, lambda m: w_out[0, m], KC, lambda k, a, b_, r=r: oTj[r][:, k, a:a + b_],
                          lambda gc, r=r: [("oTj", r)], 1.0, hv=hTf, ho=j0)

    def build(self):
        nc = self.nc
        self.xT = self.din("xT", [128, KC, TWF])
        self.vecT = self.din("vecT", [128, NV * 16])
        self.yT = self.dout("yT", [128, KC, TWF])
        self.cst = self.din("cst", [128, NCST])
        self.xres = nc.dram_tensor("xres", [128, KC, TWF], F32).ap()
        P = self.P
        with contextlib.ExitStack() as st:
            def sb(name, shape, dt):
                return st.enter_context(nc.sbuf_tensor(name, list(shape), dt))
            NA = 16 * TWF
            NB = NB_EL
            arena = sb("arena", [128, NA + NB], BF16)
            self.arena = arena
            self.hT = arena[:, 0:16 * 1026].rearrange("p (c t) -> p c t", c=16)
            self.hTf = arena[:, 0:NA].rearrange("p (c t) -> p c t", c=16)
            self.B = arena[:, NA:NA + NB]
            wr = sb("wring", [128, 4, 4096], BF16)
            self.wring = Ring("w", [wr[:, i, :] for i in range(4)])
            xr = sb("xring", [128, 4, 1026], F32)
            self.xring = Ring("x", [xr[:, i, :] for i in range(4)])
            xn = sb("xnring", [128, 3, 512], F32)
            self.xnring = Ring("xn", [xn[:, i, :] for i in range(3)])
            tr = sb("tmpring", [128, 3, 512], F32)
            self.tmpring = Ring("tmp", [tr[:, i, :] for i in range(3)])
            yr = sb("yring", [128, 2, 1026], F32)
            self.yring = Ring("y", [yr[:, i, :] for i in range(2)])
            self.yr = yr
            self.Pm = sb("Pm", [128, 128], BF16)
            self.identB = sb("identB", [128, 128], BF16)
            self.maskC = sb("maskC", [128, 128], BF16)
            self.maskP = sb("maskP", [128, 128], BF16)
            self.qS16 = sb("qS16", [128, NS], BF16)
            self.pr16 = sb("pr16", [128, NS], BF16)
            self.PS = sb("PS", [128, NS * NS], BF16)
            self.kSf = sb("kSf", [128, NS], F32)
            self.vSf = sb("vSf", [128, NS], F32)
            self.pn = sb("pn", [128, NS], F32)
            self.vpn = sb("vpn", [128, NS], F32)
            self.aSo = sb("aSo", [128, 3, NS], F32)
            self.aSd = sb("aSd", [128, 3, NS], F32)
            self.xf_i = 0
            self.kcv_i = 0
            self.rstd = sb("rstd", [128, TWF], F32)
            self.rstdm = sb("rstdm", [128, MEM], F32)
            self.vec = sb("vec", [128, NV * 16], F32)
            self.onesD = sb("onesD", [128, 128], BF16)
            self.ones1 = sb("ones1", [128, 128], BF16)
            self.epsc = sb("epsc", [128, 1], F32)
            self.scr = sb("scr", [128, 8], F32)
            self.carry = sb("carry", [128, KC, 2], F32)
            self.fq16 = sb("fq16", [128, 2 * NS], BF16)
            self.S0b = sb("S0b", [128, 2, 128], BF16)
            self.dg16 = sb("dg16", [128, 2, 128], BF16)
            self.mask2 = sb("mask2", [128, 128], BF16)
            self.scanmask = sb("scanmask", [128, 1024], BF16)
            self.fS = sb("fS", [128, NS], F32)
            self.qS = sb("qS", [128, NS], F32)
            self.kS = sb("kS", [128, NS], F32)
            self.fq = sb("fq", [128, 3 * NS], F32)
            self.S0 = sb("S0", [128, 2, 128], F32)
            self.am_i = 0
            self.s0_i = 0
            self.ps = [st.enter_context(nc.psum_tensor("ps%d" % i, [128, 512], F32)) for i in range(8)]
            self.ps_i = 0
            self.ue_i = 0
            self.pt_i = 0

            self.dma("sp", self.vec[:, :], self.vecT, reads=[], writes=["vec"], chan="vec")
            P.op("dve", lambda e: e.memset(self.onesD[:, :], 1.0 / D), writes=["onesD"])
            P.op("dve", lambda e: e.memset(self.ones1[:, :], 1.0), writes=["ones1"])
            P.op("dve", lambda e: e.memset(self.epsc[:, :], EPS), writes=["epsc"])
            self.dma("pool", self.mask2[:, :], self.cst[:, C_MASK2:C_MASK2 + 128], reads=[], writes=["mask2"], chan="cstp")
            self.dma("pool", self.scanmask[:, :], self.cst[:, C_SCAN:C_SCAN + 1024], reads=[], writes=["scanmask"], chan="cstp")
            self.dma("pool", self.Pm[:, :], self.cst[:, C_PM:C_PM + 128], reads=[], writes=["Pm"], chan="cstp")
            self.dma("pool", self.identB[:, :], self.cst[:, C_IDENT:C_IDENT + 128], reads=[], writes=["identB"], chan="cstp")
            self.dma("pool", self.maskC[:, :], self.cst[:, C_MASKC:C_MASKC + 128], reads=[], writes=["maskC"], chan="cstp")
            self.dma("pool", self.maskP[:, :], self.cst[:, C_MASKP:C_MASKP + 128], reads=[], writes=["maskP"], chan="cstp")
            P.barrier("dve", lambda e: e.memset(self.scr[:, 0:1], 0.0))

            self.init_stats()
            if "xattn" in self.phases:
                self.mem_stats()
            for L in self.layers:
                for ph in self.phases:
                    if ph == "ffn1":
                        self.ffn(L, 1)
                    elif ph == "ffn2":
                        self.ffn(L, 2)
                    elif ph == "mix":
                        if L % 3 == 0:
                            self.conv_mixer(L)
                        elif L % 3 == 1:
                            self.hgrn_mixer(L)
                        else:
                            self.attn_mixer(L)
                    elif ph == "xattn":
                        self.xattn(L)
            if self.do_final:
                self.final_norm()

            DEP = 3
            n = len(self.items)
            for i in range(n + DEP):
                if i < n:
                    for ld in self.items[i].loads:
                        ld()
                if i >= DEP:
                    self.items[i - DEP].compute()
            P.emit(nc)
        return nc


def _fm(a):
    t = a.shape[0]
    return np.ascontiguousarray(a.reshape(t, 16, 128).transpose(2, 1, 0))


def _unfm(a):
    t = a.shape[2]
    return np.ascontiguousarray(a.transpose(2, 1, 0).reshape(t, 2048))


def pack_vecs(inp):
    rows = []
    for nm in ("norm_ffn1", "norm_mix", "norm_xattn", "norm_ffn2", "norm_mem"):
        rows += [inp[nm][i] for i in range(4)]
    rows.append(inp["norm_final"])
    for l in range(2):
        for j in range(3):
            rows.append(inp["conv_w"][l, j])
    rows += [inp["hgrn_lb_logits"][i] for i in range(4)]
    rows.append(np.zeros(2048, np.float32))
    v = np.stack([np.asarray(r, np.float32).reshape(2048) for r in rows])
    vt = np.ascontiguousarray(v.reshape(NV, 16, 128).transpose(2, 0, 1).reshape(128, NV * 16))
    vt[:, V_HGN * 16:(V_HGN + 1) * 16] = np.asarray(inp["hgrn_norm"][0], np.float32).T
    return vt


def make_consts():
    c = np.zeros((128, NCST), np.float32)
    i = np.arange(128)
    c[:, C_IDENT:C_IDENT + 128] = np.eye(128, dtype=np.float32)
    s_, t_ = i[:, None], i[None, :]
    c[:, C_MASK2:C_MASK2 + 128] = ((s_ // 64 == t_ // 64) & (s_ <= t_)).astype(np.float32)
    c[:, C_SCAN:C_SCAN + 1024] = (np.arange(1024) % 64 != 0).astype(np.float32)[None, :]
    for k in range(16):
        c[16 + k, C_PM + k] = -1.0
        c[k, C_PM + 16 + k] = 1.0
    NEG = -30000.0
    c[:, C_MASKC:C_MASKC + 128] = np.where(s_ <= t_, 0.0, NEG)
    c[:, C_MASKP:C_MASKP + 128] = np.where(s_ >= t_, 0.0, NEG)
    pos = np.concatenate([np.arange(S), np.full(NS, 16384)]).astype(np.float32)
    inv = (np.float32(500000.0) ** (-np.arange(16, dtype=np.float32) * np.float32(2.0) / np.float32(32.0))).astype(np.float32)
    ang = pos[None, :] * inv[:, None]
    c[:, C_COS:C_COS + TWF] = 1.0
    c[0:16, C_COS:C_COS + TWF] = np.cos(ang)
    c[16:32, C_COS:C_COS + TWF] = np.cos(ang)
    c[0:16, C_SIN:C_SIN + TWF] = np.sin(ang)
    c[16:32, C_SIN:C_SIN + TWF] = np.sin(ang)
    return c


def unperm_v(a, g):
    if g == 0:
        return a
    d = (1, 4, 16)[g]
    return np.ascontiguousarray(a.reshape(128 * d, 16, 128))


WIN_CACHE_NAMES = {"cwkT": ("cache_win_k0", "cache_win_k1", "cache_win_k2"),
                   "cwv": ("cache_win_v0", "cache_win_v1", "cache_win_v2")}
FFN_W_NAMES = {1: ("ffn1_w_gu", "ffn1_w_down"), 2: ("ffn2_w_gu", "ffn2_w_down")}


def _lay_in(w, nsec, nblk):
    L = w.shape[0]
    v = w.reshape(L, KC, 128, nsec, nblk, 128).transpose(0, 4, 2, 3, 1, 5)
    return np.ascontiguousarray(v).reshape(L, nblk, 128, nsec * KC * 128)


def _lay_out(w, nk):
    L = w.shape[0]
    v = w.reshape(L, nk, 128, KC, 128).transpose(0, 3, 2, 1, 4)
    return np.ascontiguousarray(v).reshape(L, KC, 128, nk * 128)


def _lay_qkv(w):
    L = w.shape[0]
    v = w.reshape(L, KC, 128, 3, 3, KC, 128).transpose(0, 3, 5, 2, 4, 1, 6)
    return np.ascontiguousarray(v).reshape(L, 3, KC, 128, 3 * KC * 128)


WEIGHT_LAYOUT = {
    "ffn1_w_gu": lambda w: _lay_in(w, 2, FC), "ffn2_w_gu": lambda w: _lay_in(w, 2, FC),
    "ffn1_w_down": lambda w: _lay_out(w, FC), "ffn2_w_down": lambda w: _lay_out(w, FC),
    "conv_w_in": lambda w: _lay_in(w, 3, KC), "conv_w_out": lambda w: _lay_out(w, KC),
    "hgrn_w_in": lambda w: _lay_in(w, 4, KC), "hgrn_w_out": lambda w: _lay_out(w, KC),
    "attn_w_qkv": _lay_qkv, "attn_w_out": lambda w: _lay_out(w, KC),
    "xattn_w_kv": lambda w: _lay_in(w, 2, 4), "xattn_w_q": lambda w: _lay_in(w, 1, 4), "xattn_w_o": lambda w: _lay_out(w, 4),
}


def core_inputs(inp, c, names, cache=None):
    sl = slice(c * NS, (c + 1) * NS)
    m = {}
    for nm in names:
        if nm == "xT":
            m[nm] = _fm(np.concatenate([inp["x_prompt"][c], inp["x_sample"][sl, 0]], axis=0))
        elif nm == "vecT":
            m[nm] = pack_vecs(inp)
        elif nm == "memT":
            m[nm] = _fm(inp["mem_prompt"][c])
        elif nm == "cst":
            m[nm] = make_consts()
        elif nm in ("cwkT", "cwv"):
            arrs = []
            for g, dil in enumerate((1, 4, 16)):
                a = inp[WIN_CACHE_NAMES[nm][g]][0, sl][:, ::dil]
                if nm == "cwkT":
                    arrs.append(a.transpose(3, 0, 2, 1))
                else:
                    arrs.append(a.transpose(1, 0, 2, 3))
            m[nm] = np.ascontiguousarray(np.stack(arrs, axis=1))
        elif nm == "shg":
            m[nm] = np.ascontiguousarray(inp["state_hgrn"][0, sl].transpose(2, 0, 1, 3))
        elif nm == "sconvT":
            a = inp["state_conv"][:, sl]
            m[nm] = np.ascontiguousarray(a.reshape(2, NS, 2, 16, 128).transpose(4, 0, 3, 1, 2))
        elif nm == "cmkT":
            a = inp["cache_mem_k"][:, sl]
            m[nm] = np.ascontiguousarray(a.transpose(4, 0, 1, 3, 2).reshape(128, DEPTH, NS, 4 * MEM))
        elif nm == "cmv":
            a = inp["cache_mem_v"][:, sl].reshape(DEPTH, NS, 2, 128, 512)
            m[nm] = np.ascontiguousarray(a.transpose(3, 0, 1, 2, 4).reshape(128, DEPTH, NS, 1024))
        elif nm in WEIGHT_LAYOUT:
            if cache is not None and nm in cache:
                m[nm] = cache[nm]
            else:
                m[nm] = WEIGHT_LAYOUT[nm](inp[nm])
                if cache is not None:
                    cache[nm] = m[nm]
        else:
            m[nm] = inp[nm]
    return m


def kernel(**inp):
    inp = {k: np.asarray(v) for k, v in inp.items()}
    b = Builder()
    nc = b.build()
    names = list(b.w.keys())
    cache = {}
    in_maps = [core_inputs(inp, c, names, cache) for c in range(NCORES)]
    res = run_bass_kernel_spmd(nc, in_maps, core_ids=list(range(NCORES)))
    R = res.results
    f32 = np.float32

    def cat(fn, axis=0):
        return np.ascontiguousarray(np.concatenate([fn(R[c]) for c in range(NCORES)], axis=axis).astype(f32))

    y_prompt = cat(lambda r: _unfm(r["yT"][:, :, :S])[None])
    y_sample = cat(lambda r: _unfm(r["yT"][:, :, S:])[:, None, :])
    p_state_conv = cat(lambda r: r["o_pconv"].transpose(1, 3, 2, 0).reshape(2, 1, 2, D), axis=1)
    s_state_conv = cat(lambda r: r["o_sconv"].transpose(1, 3, 4, 2, 0).reshape(2, NS, 2, D), axis=1)
    p_state_hgrn = cat(lambda r: r["o_phg"].transpose(1, 0, 2)[None, None], axis=1)
    s_state_hgrn = cat(lambda r: r["o_shg"].transpose(1, 2, 0, 3)[None], axis=1)
    p_win, s_win = [], []
    for g in range(3):
        p_win.append(cat(lambda r, g=g: r["o_pwk%d" % g].transpose(2, 1, 0)[None, None], axis=1))
        p_win.append(cat(lambda r, g=g: r["o_pwv%d" % g].transpose(2, 1, 0)[None, None], axis=1))
        s_win.append(cat(lambda r, g=g: r["o_swk"][:, g].transpose(2, 1, 0)[None, :, None], axis=1))
        s_win.append(cat(lambda r, g=g: r["o_swv"][:, g].transpose(2, 1, 0)[None, :, None], axis=1))
    p_mem_k = cat(lambda r: r["o_pmk"].transpose(1, 3, 2, 0)[:, None], axis=1)
    p_mem_v = cat(lambda r: r["o_pmv"].transpose(1, 2, 0, 3).reshape(DEPTH, 1, MEM, 4, 128), axis=1)
    return (y_prompt, y_sample, p_state_conv, p_state_hgrn, *p_win, p_mem_k, p_mem_v,
            s_state_conv, s_state_hgrn, *s_win)
```

```python
import contextlib
import numpy as np
import concourse.bass as bass
import concourse.mybir as mybir
from concourse.bass_utils import run_bass_kernel_spmd

F32 = mybir.dt.float32
BF16 = mybir.dt.bfloat16
AF = mybir.ActivationFunctionType
ALU = mybir.AluOpType

D = 2048
KC = 16
S = 2048
DFF = 5632
FC = 44
DEPTH = 4
EPS = 1e-6
NCORES = 4
NS = 8 // NCORES
TWF = S + NS
MEM = 256
NB_EL = 26048

V_FFN1, V_MIX, V_XA, V_FFN2, V_MEM = 0, 4, 8, 12, 16
V_FINAL = 20
V_CONVW = 21
V_LB = 27
V_HGN = 31
NV = 32

C_IDENT, C_MASK2, C_SCAN = 0, 128, 256
C_PM, C_MASKC, C_MASKP = 1280, 1408, 1536
C_COS = 1664
C_SIN = C_COS + TWF
NCST = C_SIN + TWF


class Op:
    __slots__ = ("eng", "fn", "deps", "chan", "inc", "val", "idx")


class Prog:
    COMPUTE = ("pe", "act", "dve", "pool")

    def __init__(self):
        self.ops = []
        self.lastw = {}
        self.readers = {}
        self.latest = {}
        self.barrier_idx = None

    def op(self, eng, fn, reads=(), writes=(), chan=None):
        o = Op()
        o.eng, o.fn, o.chan, o.inc, o.val = eng, fn, chan, False, 0
        o.idx = len(self.ops)
        deps = set()
        for k in reads:
            w = self.lastw.get(k)
            if w is not None:
                deps.add(w)
        for k in writes:
            w = self.lastw.get(k)
            if w is not None:
                deps.add(w)
            r = self.readers.get(k)
            if r:
                deps.update(r.values())
        if self.barrier_idx is not None:
            deps.add(self.barrier_idx)
        o.deps = deps
        self.ops.append(o)
        lane = ("c", chan) if chan is not None else ("e", eng)
        for k in writes:
            self.lastw[k] = o.idx
            self.readers[k] = {}
        for k in reads:
            rd = self.readers.setdefault(k, {})
            if isinstance(k, tuple) and k[0] == "ps" and lane != ("e", "pe") and any(l != ("e", "pe") and l != lane for l in rd):
                raise RuntimeError("PSUM tile %r read by two engines (%r and %r) - crashes the device" % (k, list(rd), lane))
            rd[lane] = o.idx
        self.latest[lane] = o.idx
        return o

    def barrier(self, eng, fn):
        o = self.op(eng, fn)
        o.deps.update(self.latest.values())
        o.deps.discard(o.idx)
        self.barrier_idx = o.idx
        return o

    def emit(self, nc, final_wait_eng="sp"):
        ops = self.ops
        for o in ops:
            for j in o.deps:
                d = ops[j]
                if d.chan is not None:
                    continue
                if d.eng == o.eng and o.eng == "pe" and o.chan is None:
                    continue
                d.inc = True
        cnt = {}
        for o in ops:
            if o.chan is not None:
                cnt[("c", o.chan)] = cnt.get(("c", o.chan), 0) + 16
                o.val = cnt[("c", o.chan)]
            elif o.inc:
                cnt[("e", o.eng)] = cnt.get(("e", o.eng), 0) + 1
                o.val = cnt[("e", o.eng)]
        lanes = sorted(cnt.keys(), key=str)
        with contextlib.ExitStack() as st:
            sems = {ln: st.enter_context(nc.semaphore("s%d" % i)) for i, ln in enumerate(lanes)}
            block = st.enter_context(nc.Block())
            by_eng = {}
            for o in ops:
                by_eng.setdefault(o.eng, []).append(o)

            def run(engname, e):
                known = {}
                for o in by_eng.get(engname, []):
                    need = {}
                    for j in o.deps:
                        d = ops[j]
                        if d.chan is not None:
                            ln = ("c", d.chan)
                        else:
                            if d.eng == o.eng and o.eng == "pe" and o.chan is None:
                                continue
                            ln = ("e", d.eng)
                        if d.val > need.get(ln, 0):
                            need[ln] = d.val
                    for ln, v in need.items():
                        if known.get(ln, 0) < v:
                            e.wait_ge(sems[ln], v)
                            known[ln] = v
                    ins = o.fn(e)
                    if o.chan is not None:
                        ins.then_inc(sems[("c", o.chan)], 16)
                    elif o.inc:
                        ins.then_inc(sems[("e", o.eng)], 1)
                if engname == final_wait_eng:
                    for ln in lanes:
                        if known.get(ln, 0) < cnt[ln]:
                            e.wait_ge(sems[ln], cnt[ln])

            @block.tensor
            def _(e):
                run("pe", e)

            @block.scalar
            def _(e):
                run("act", e)

            @block.vector
            def _(e):
                run("dve", e)

            @block.gpsimd
            def _(e):
                run("pool", e)

            @block.sync
            def _(e):
                run("sp", e)


class Ring:
    def __init__(self, name, views):
        self.name, self.views, self.i = name, views, 0

    def next(self):
        k = self.i % len(self.views)
        self.i += 1
        return (self.name, k), self.views[k]


class Item:
    def __init__(self):
        self.loads = []
        self.compute = None


class Builder:
    def __init__(self, layers=(0, 1, 2, 3), phases=("ffn1", "mix", "xattn", "ffn2"), do_final=True, wl=None, dbg=()):
        self.dbg = set(dbg)
        self.layers, self.phases, self.do_final = layers, phases, do_final
        self.wl = wl
        self.nc = bass.Bass("TRN2", target_bir_lowering=False)
        self.P = Prog()
        self.items = []
        self.w = {}
        self.outs = {}

    def din(self, name, shape):
        if name not in self.w:
            self.w[name] = self.nc.dram_tensor(name, list(shape), F32, kind="ExternalInput").ap()
        return self.w[name]

    def dout(self, name, shape):
        if name not in self.outs:
            self.outs[name] = self.nc.dram_tensor(name, list(shape), F32, kind="ExternalOutput").ap()
        return self.outs[name]

    def wt(self, nm, shp):
        shp = list(shp)
        if self.wl is not None:
            shp[0] = 1
        return self.din(nm, shp)

    def li(self, idx):
        return 0 if self.wl is not None else idx

    def tiles(self, full=False):
        if full:
            return [(0, TWF)]
        return [(0, 1024), (1024, TWF - 1024)]

    @staticmethod
    def colgroups(c0, w):
        out = []
        o = 0
        while o < w:
            ww = min(512, w - o)
            if o + ww > 2048 - c0 > o:
                ww = 2048 - c0 - o
            out.append((o, ww))
            o += ww
        return out

    def gk(self, name, c, c0, tw):
        return [(name, c, c0 + j0) for (j0, _) in self.colgroups(c0, tw)]

    def add_item(self, loads, compute):
        it = Item()
        it.loads, it.compute = loads, compute
        self.items.append(it)
        return it

    def psum(self):
        k = self.ps_i % 8
        self.ps_i += 1
        return ("ps", k), self.ps[k]

    def dma(self, eng, out, in_, reads, writes, chan):
        self.P.op(eng, lambda e, o=out, i=in_: e.dma_start(out=o, in_=i), reads=reads, writes=writes, chan=chan)

    def barrier_item(self):
        self.add_item([], lambda: self.P.barrier("dve", lambda e: e.memset(self.scr[:, 0:1], 0.0)))

    def wload(self, st, src, n, nk, ncol):
        st["k"], st["v"] = self.wring.next()
        tot = n * nk * ncol
        st["w"] = st["v"][:, 0:tot].rearrange("p (t k n) -> p k t n", t=n, k=nk)
        self.dma("pool", st["v"][:, 0:tot], src, reads=[], writes=[st["k"]], chan=st["k"])

    def mm_acc(self, pv, jw, lhs_list, rhs_list):
        n = len(lhs_list)

        def fn(e):
            for i in range(n):
                ins = e.matmul(pv[:, 0:jw], lhsT=lhs_list[i], rhs=rhs_list[i], start=(i == 0), stop=(i == n - 1))
            return ins
        return fn

    def prologue(self, c0, tw, grow, hview=None, hoff=0):
        P = self.P
        if hview is None:
            hview = self.hT
        for c in range(KC):
            st = {}

            def load(c=c, st=st):
                st["k"], st["v"] = self.xring.next()
                self.dma("sp", st["v"][:, 0:tw], self.xres[:, c, c0:c0 + tw], reads=self.gk("xres", c, c0, tw), writes=[st["k"]], chan=st["k"])

            def comp(c=c, st=st):
                g = self.vec[:, grow * 16 + c: grow * 16 + c + 1]
                P.op("dve", lambda e: e.scalar_tensor_tensor(out=hview[:, c, hoff:hoff + tw], in0=st["v"][:, 0:tw], scalar=g,
                                                             in1=self.rstd[:, c0:c0 + tw], op0=ALU.mult, op1=ALU.mult),
                     reads=[st["k"]] + self.gk("rstd", 0, c0, tw), writes=self.gk("hT", c, c0, tw))
            self.add_item([load], comp)

    def epilogue(self, c0, m, j0, jw, psk, psv, xk, xv, scale, final, hv=None, ho=0):
        P = self.P
        hv = self.hT if hv is None else hv
        sk, sv = self.xnring.next()
        P.op("dve", lambda e: e.scalar_tensor_tensor(out=sv[:, 0:jw], in0=psv[:, 0:jw], scalar=float(scale), in1=xv[:, j0:j0 + jw],
                                                     op0=ALU.mult, op1=ALU.add),
             reads=[psk, xk], writes=[sk])
        self.dma("sp", self.xres[:, m, c0 + j0:c0 + j0 + jw], sv[:, 0:jw], reads=[sk], writes=[("xres", m, c0 + j0)], chan=sk)
        if final:
            P.op("act", lambda e: e.activation(out=hv[:, m, ho + j0:ho + j0 + jw], in_=sv[:, 0:jw], func=AF.Square),
                 reads=[sk], writes=[("hT", m, c0 + j0)])

    def stats(self, c0, tw, hv=None, ho=0):
        P = self.P
        hv = self.hT if hv is None else hv
        for (j0, jw) in self.colgroups(c0, tw):
            psk, psv = self.psum()
            P.op("pe", self.mm_acc(psv, jw, [self.onesD[:, :]] * KC, [hv[:, c, ho + j0:ho + j0 + jw] for c in range(KC)]),
                 reads=[("hT", c, c0 + j0) for c in range(KC)], writes=[psk])
            tk, tv = self.tmpring.next()
            P.op("act", lambda e, psv=psv, tv=tv, jw=jw: e.activation(out=tv[:, 0:jw], in_=psv[:, 0:jw], func=AF.Sqrt, bias=self.epsc[:, 0:1]),
                 reads=[psk], writes=[tk])
            P.op("dve", lambda e, tv=tv, j0=j0, jw=jw: e.reciprocal(out=self.rstd[:, c0 + j0:c0 + j0 + jw], in_=tv[:, 0:jw]),
                 reads=[tk], writes=[("rstd", 0, c0 + j0)])

    def out_proj(self, c0, tw, w_fn, nk, rhs_fn, rhs_keys_fn, scale, final=True, hv=None, ho=0):
        P = self.P
        cgs = self.colgroups(c0, tw)
        for m in range(KC):
            st = {}

            def load(m=m, st=st):
                self.wload(st, w_fn(m), 1, nk, 128)
                st["xk"], st["xv"] = self.xring.next()
                self.dma("sp", st["xv"][:, 0:tw], self.xres[:, m, c0:c0 + tw], reads=self.gk("xres", m, c0, tw), writes=[st["xk"]], chan=st["xk"])

            def comp(m=m, st=st):
                for (j0, jw) in cgs:
                    pk, pv = self.psum()
                    P.op("pe", self.mm_acc(pv, jw, [st["w"][:, k, 0, :] for k in range(nk)], [rhs_fn(k, j0, jw) for k in range(nk)]),
                         reads=[st["k"]] + rhs_keys_fn(c0 + j0), writes=[pk])
                    self.epilogue(c0, m, j0, jw, pk, pv, st["xk"], st["xv"], scale, final, hv, ho)
                if final and m == KC - 1:
                    self.stats(c0, tw, hv, ho)
            self.add_item([load], comp)

    def init_stats(self):
        P = self.P
        for (c0, tw) in self.tiles():
            for c in range(KC):
                st = {}

                def load(c=c, st=st, c0=c0, tw=tw):
                    st["k"], st["v"] = self.xring.next()
                    self.dma("sp", st["v"][:, 0:tw], self.xT[:, c, c0:c0 + tw], reads=[], writes=[st["k"]], chan=st["k"])

                def comp(c=c, st=st, c0=c0, tw=tw):
                    P.op("act", lambda e: e.activation(out=self.hT[:, c, 0:tw], in_=st["v"][:, 0:tw], func=AF.Square),
                         reads=[st["k"]], writes=self.gk("hT", c, c0, tw))
                    self.dma("sp", self.xres[:, c, c0:c0 + tw], st["v"][:, 0:tw], reads=[st["k"]], writes=self.gk("xres", c, c0, tw), chan=("xs", c % 4))
                    if c == KC - 1:
                        self.stats(c0, tw)
                self.add_item([load], comp)

    def final_norm(self):
        P = self.P
        for (c0, tw) in self.tiles():
            for c in range(KC):
                st = {}

                def load(c=c, st=st, c0=c0, tw=tw):
                    st["k"], st["v"] = self.xring.next()
                    self.dma("sp", st["v"][:, 0:tw], self.xres[:, c, c0:c0 + tw], reads=self.gk("xres", c, c0, tw), writes=[st["k"]], chan=st["k"])

                def comp(c=c, st=st, c0=c0, tw=tw):
                    g = self.vec[:, V_FINAL * 16 + c: V_FINAL * 16 + c + 1]
                    ok, ov = self.yring.next()
                    P.op("dve", lambda e: e.scalar_tensor_tensor(out=ov[:, 0:tw], in0=st["v"][:, 0:tw], scalar=g,
                                                                 in1=self.rstd[:, c0:c0 + tw], op0=ALU.mult, op1=ALU.mult),
                         reads=[st["k"]] + self.gk("rstd", 0, c0, tw), writes=[ok])
                    self.dma("sp", self.yT[:, c, c0:c0 + tw], ov[:, 0:tw], reads=[ok], writes=[("yT", c, c0)], chan=ok)
                self.add_item([load], comp)

    def ffn(self, L, which):
        P = self.P
        wgu = self.wt(FFN_W_NAMES[which][0], [DEPTH, FC, 128, 2 * KC * 128])
        wdn = self.wt(FFN_W_NAMES[which][1], [DEPTH, KC, 128, FC * 128])
        Lw = self.li(L)
        grow = (V_FFN1 if which == 1 else V_FFN2) + L
        act = self.B[:, 0:22 * 1026].rearrange("p (c t) -> p c t", c=22)
        self.barrier_item()
        hTf = self.hTf
        tiles = self.tiles()
        for ti, (c0, tw) in enumerate(tiles):
            cgs = self.colgroups(c0, tw)
            if ti == 0:
                self.prologue(c0, tw, grow, hview=hTf, hoff=c0)
            for half in range(2):
                for cc in range(22):
                    fc = half * 22 + cc
                    st = {}

                    def load(fc=fc, st=st):
                        self.wload(st, wgu[Lw, fc], 2, KC, 128)

                    def comp(cc=cc, st=st, cgs=cgs, c0=c0):
                        v = st["w"]
                        for (j0, jw) in cgs:
                            gk, gv = self.psum()
                            uk, uv = self.psum()

                            def mm(e, j0=j0, jw=jw, gv=gv, uv=uv, v=v):
                                for t, pv in ((0, gv), (1, uv)):
                                    for k in range(KC):
                                        ins = e.matmul(pv[:, 0:jw], lhsT=v[:, k, t, :], rhs=hTf[:, k, c0 + j0:c0 + j0 + jw], start=(k == 0), stop=(k == KC - 1))
                                return ins
                            P.op("pe", mm, reads=[st["k"]] + [("hT", c, c0 + j0) for c in range(KC)], writes=[gk, uk])
                            tk, tv = self.tmpring.next()
                            P.op("act", lambda e, gv=gv, tv=tv, jw=jw: e.activation(out=tv[:, 0:jw], in_=gv[:, 0:jw], func=AF.Silu),
                                 reads=[gk], writes=[tk])
                            P.op("dve", lambda e, uv=uv, tv=tv, j0=j0, jw=jw, cc=cc: e.tensor_tensor(out=act[:, cc, j0:j0 + jw], in0=tv[:, 0:jw], in1=uv[:, 0:jw], op=ALU.mult),
                                 reads=[tk, uk], writes=[("act", cc, c0 + j0)])
                    self.add_item([load], comp)
                if half == 1 and ti + 1 < len(tiles):
                    self.prologue(tiles[ti + 1][0], tiles[ti + 1][1], grow, hview=hTf, hoff=tiles[ti + 1][0])
                self.out_proj(c0, tw, lambda m, half=half: wdn[Lw, m][:, half * 2816:(half + 1) * 2816], 22,
                              lambda k, j0, jw: act[:, k, j0:j0 + jw],
                              lambda gc: [("act", c, gc) for c in range(22)], 0.5, final=(half == 1), hv=hTf, ho=c0)

    def conv_mixer(self, L):
        P = self.P
        l = L // 3
        lw = self.li(l)
        w_in = self.wt("conv_w_in", [2, KC, 128, 3 * KC * 128])
        w_out = self.wt("conv_w_out", [2, KC, 128, KC * 128])
        sconvT = self.din("sconvT", [128, 2, KC, NS, 2])
        o_pconv = self.dout("o_pconv", [128, 2, KC, 2])
        o_sconv = self.dout("o_sconv", [128, 2, KC, NS, 2])
        B = self.B
        yT = B[:, 0:16 * 1026].rearrange("p (c t) -> p c t", c=16)
        o = 16 * 1026
        bsb = [B[:, o + i * 1026: o + (i + 1) * 1026] for i in range(2)]
        o += 2 * 1026
        UW = 1026 + 3 * NS + 2
        ue = [B[:, o + i * 2 * UW: o + (i + 1) * 2 * UW].bitcast(F32) for i in range(2)]
        o += 4 * UW
        tsm = [B[:, o + i * 2 * NS: o + (i + 1) * 2 * NS].bitcast(F32) for i in range(2)]
        wp = 1024
        self.barrier_item()
        hTf = self.hTf
        tiles = self.tiles()
        for ti, (c0, tw) in enumerate(tiles):
            cgs = self.colgroups(c0, tw)
            if ti == 0:
                self.prologue(c0, tw, V_MIX + L, hview=hTf, hoff=c0)
            for m in range(KC):
                mst = {}
                stA, stB = {}, {}
                wv = [self.vec[:, (V_CONVW + l * 3 + t) * 16 + m:(V_CONVW + l * 3 + t) * 16 + m + 1] for t in range(3)]

                def loadA(m=m, st=stA):
                    self.wload(st, w_in[lw, m][:, 2048:6144], 2, KC, 128)

                def compA(m=m, st=stA, mst=mst, ti=ti, c0=c0, cgs=cgs):
                    r = self.ue_i % 2
                    self.ue_i += 1
                    mst["r"] = r
                    uk = ("ue", r)
                    u = ue[r]
                    us = u[:, 1026:1026 + 3 * NS].rearrange("p (s t) -> p s t", t=3)
                    if ti == 0:
                        P.op("dve", lambda e: e.memset(u[:, 0:2], 0.0), writes=[uk])
                    else:
                        P.op("act", lambda e: e.copy(out=u[:, 0:2], in_=self.carry[:, m, :]), reads=[("carry", m)], writes=[uk])
                        self.dma("sp", us[:, :, 0:2], sconvT[:, l, m, :, :], reads=[], writes=[uk], chan=("us", r))
                    for (j0, jw) in cgs:
                        ck, cv = self.psum()
                        zk, zv = self.psum()

                        def mm(e, j0=j0, jw=jw, cv=cv, zv=zv, v=st["w"]):
                            for t, pv in ((0, cv), (1, zv)):
                                for k in range(KC):
                                    ins = e.matmul(pv[:, 0:jw], lhsT=v[:, k, t, :], rhs=hTf[:, k, c0 + j0:c0 + j0 + jw], start=(k == 0), stop=(k == KC - 1))
                            return ins
                        P.op("pe", mm, reads=[st["k"]] + [("hT", c, c0 + j0) for c in range(KC)], writes=[ck, zk])
                        tk, tv = self.tmpring.next()
                        P.op("act", lambda e, cv=cv, tv=tv, jw=jw: e.copy(out=tv[:, 0:jw], in_=cv[:, 0:jw]), reads=[ck], writes=[tk])
                        if j0 < wp:
                            dst = u[:, 2 + j0:2 + j0 + jw]
                        else:
                            dst = us[:, :, 2]
                        P.op("dve", lambda e, zv=zv, tv=tv, jw=jw, dst=dst: e.tensor_tensor(out=dst, in0=tv[:, 0:jw], in1=zv[:, 0:jw], op=ALU.mult),
                             reads=[tk, zk, uk], writes=[uk])
                self.add_item([loadA], compA)

                def loadB(m=m, st=stB):
                    self.wload(st, w_in[lw, m][:, 0:2048], 1, KC, 128)

                def compB(m=m, st=stB, mst=mst, ti=ti, c0=c0, cgs=cgs, wv=wv):
                    r = mst["r"]
                    uk = ("ue", r)
                    u = ue[r]
                    us = u[:, 1026:1026 + 3 * NS].rearrange("p (s t) -> p s t", t=3)
                    bk = ("bsb", r)
                    bs = bsb[r]
                    for (j0, jw) in cgs:
                        pk, pv = self.psum()
                        P.op("pe", self.mm_acc(pv, jw, [st["w"][:, k, 0, :] for k in range(KC)], [hTf[:, k, c0 + j0:c0 + j0 + jw] for k in range(KC)]),
                             reads=[st["k"]] + [("hT", c, c0 + j0) for c in range(KC)], writes=[pk])
                        P.op("act", lambda e, pv=pv, j0=j0, jw=jw: e.copy(out=bs[:, j0:j0 + jw], in_=pv[:, 0:jw]), reads=[pk], writes=[bk])
                    yk, t = self.yring.next()
                    P.op("dve", lambda e: e.tensor_scalar(out=t[:, 0:wp], in0=u[:, 0:wp], scalar1=wv[0], scalar2=None, op0=ALU.mult), reads=[uk], writes=[yk])
                    P.op("dve", lambda e: e.scalar_tensor_tensor(out=t[:, 0:wp], in0=u[:, 1:wp + 1], scalar=wv[1], in1=t[:, 0:wp], op0=ALU.mult, op1=ALU.add), reads=[uk, yk], writes=[yk])
                    P.op("dve", lambda e: e.scalar_tensor_tensor(out=t[:, 0:wp], in0=u[:, 2:wp + 2], scalar=wv[2], in1=t[:, 0:wp], op0=ALU.mult, op1=ALU.add), reads=[uk, yk], writes=[yk])
                    P.op("dve", lambda e: e.tensor_tensor(out=yT[:, m, 0:wp], in0=t[:, 0:wp], in1=bs[:, 0:wp], op=ALU.mult), reads=[yk, bk],
                         writes=[("yT", m, c0), ("yT", m, c0 + 512)])
                    if ti == 0:
                        P.op("act", lambda e: e.copy(out=self.carry[:, m, :], in_=u[:, wp:wp + 2]), reads=[uk], writes=[("carry", m)])
                    else:
                        ts = tsm[r]
                        sk = ("tsm", r)
                        P.op("dve", lambda e: e.tensor_scalar(out=ts[:, 0:NS], in0=us[:, :, 0], scalar1=wv[0], scalar2=None, op0=ALU.mult), reads=[uk], writes=[sk])
                        P.op("dve", lambda e: e.scalar_tensor_tensor(out=ts[:, 0:NS], in0=us[:, :, 1], scalar=wv[1], in1=ts[:, 0:NS], op0=ALU.mult, op1=ALU.add), reads=[uk, sk], writes=[sk])
                        P.op("dve", lambda e: e.scalar_tensor_tensor(out=ts[:, 0:NS], in0=us[:, :, 2], scalar=wv[2], in1=ts[:, 0:NS], op0=ALU.mult, op1=ALU.add), reads=[uk, sk], writes=[sk])
                        P.op("dve", lambda e: e.tensor_tensor(out=yT[:, m, wp:wp + NS], in0=ts[:, 0:NS], in1=bs[:, wp:wp + NS], op=ALU.mult), reads=[sk, bk],
                             writes=[("yT", m, c0 + wp)])
                        self.dma("sp", o_pconv[:, l, m, :], u[:, wp:wp + 2], reads=[uk], writes=[("o_pconv", l, m)], chan=("uo", r))
                        self.dma("sp", o_sconv[:, l, m, :, :], us[:, :, 1:3], reads=[uk], writes=[("o_sconv", l, m)], chan=("uo", r))
                self.add_item([loadB], compB)
            if ti + 1 < len(tiles):
                self.prologue(tiles[ti + 1][0], tiles[ti + 1][1], V_MIX + L, hview=hTf, hoff=tiles[ti + 1][0])
            self.out_proj(c0, tw, lambda m: w_out[lw, m], KC, lambda k, j0, jw: yT[:, k, j0:j0 + jw],
                          lambda gc: [("yT", c, gc) for c in range(KC)], 1.0, hv=hTf, ho=c0)

    def xattn_views(self):
        B = self.B
        o = 0
        v = {}
        v["qT"] = B[:, o:o + 4 * 1026].rearrange("p (h t) -> p h t", h=4); o += 4 * 1026
        v["oT"] = B[:, o:o + 4 * 1026].rearrange("p (h t) -> p h t", h=4); o += 4 * 1026
        v["PT"] = [B[:, o + i * 1024:o + (i + 1) * 1024].rearrange("p (b t) -> p b t", b=2) for i in range(2)]; o += 2048
        v["hmT"] = B[:, o:o + 16 * 256].rearrange("p (c t) -> p c t", c=16); o += 4096
        v["kmT"] = B[:, o:o + 1024].rearrange("p (h t) -> p h t", h=4); o += 1024
        v["vm"] = B[:, o:o + 1024].rearrange("p (b t) -> p b t", b=2); o += 1024
        v["ckv"] = [B[:, o + i * 2048:o + (i + 1) * 2048] for i in range(NS)]; o += 2048 * NS
        return v

    def mem_stats(self):
        P = self.P
        memT = self.din("memT", [128, KC, MEM])
        sq = self.B[:, 0:16 * 256].rearrange("p (c t) -> p c t", c=16)
        self.barrier_item()
        for c in range(KC):
            st = {}

            def load(c=c, st=st):
                st["k"], st["v"] = self.xring.next()
                self.dma("sp", st["v"][:, 0:MEM], memT[:, c, :], reads=[], writes=[st["k"]], chan=st["k"])

            def comp(c=c, st=st):
                P.op("act", lambda e: e.activation(out=sq[:, c, :], in_=st["v"][:, 0:MEM], func=AF.Square), reads=[st["k"]], writes=[("msq", c)])
                if c == KC - 1:
                    psk, psv = self.psum()
                    P.op("pe", self.mm_acc(psv, MEM, [self.onesD[:, :]] * KC, [sq[:, k, :] for k in range(KC)]),
                         reads=[("msq", k) for k in range(KC)], writes=[psk])
                    tk, tv = self.tmpring.next()
                    P.op("act", lambda e: e.activation(out=tv[:, 0:MEM], in_=psv[:, 0:MEM], func=AF.Sqrt, bias=self.epsc[:, 0:1]), reads=[psk], writes=[tk])
                    P.op("dve", lambda e: e.reciprocal(out=self.rstdm[:, :], in_=tv[:, 0:MEM]), reads=[tk], writes=["rstdm"])
            self.add_item([load], comp)

    def attn_block(self, h, kT, kkeys, vT, vkeys, qv, qkey, ov, okey, jw, PT, ptk, sel=None):
        P = self.P
        scale = 128 ** -0.5
        for b in range(2):
            sk, sv = self.psum()
            P.op("pe", self.mm_acc(sv, jw, [kT[:, b * 128:(b + 1) * 128]], [qv]), reads=kkeys + [qkey], writes=[sk])
            P.op("act", lambda e, sv=sv, b=b: e.activation(out=PT[:, b, 0:jw], in_=sv[:, 0:jw], func=AF.Exp, scale=scale),
                 reads=[sk], writes=[(ptk, b)])
        ok_, ovp = self.psum()
        dk, dv = self.psum()
        P.op("pe", self.mm_acc(ovp, jw, [vT[:, b, :] for b in range(2)], [PT[:, b, 0:jw] for b in range(2)]),
             reads=vkeys + [(ptk, 0), (ptk, 1)], writes=[ok_])
        P.op("pe", self.mm_acc(dv, jw, [self.ones1[:, :]] * 2, [PT[:, b, 0:jw] for b in range(2)]),
             reads=[(ptk, 0), (ptk, 1)], writes=[dk])
        tk, tv = self.tmpring.next()
        P.op("dve", lambda e: e.reciprocal(out=tv[:, 0:jw], in_=dv[:, 0:jw]), reads=[dk], writes=[tk])
        if sel is None:
            P.op("dve", lambda e: e.tensor_tensor(out=ov, in0=ovp[:, 0:jw], in1=tv[:, 0:jw], op=ALU.mult), reads=[ok_, tk], writes=[okey])
        else:
            P.op("dve", lambda e: e.tensor_tensor(out=ov, in0=ovp[:, sel:sel + 1], in1=tv[:, sel:sel + 1], op=ALU.mult), reads=[ok_, tk, okey], writes=[okey])

    def xattn(self, L):
        P = self.P
        Lw = self.li(L)
        w_kv = self.wt("xattn_w_kv", [DEPTH, 4, 128, 2 * KC * 128])
        w_q = self.wt("xattn_w_q", [DEPTH, 4, 128, KC * 128])
        w_o = self.wt("xattn_w_o", [DEPTH, KC, 128, 4 * 128])
        memT = self.din("memT", [128, KC, MEM])
        cmkT = self.din("cmkT", [128, DEPTH, NS, 4 * MEM])
        cmv = self.din("cmv", [128, DEPTH, NS, 2 * 512])
        o_pmk = self.dout("o_pmk", [128, DEPTH, 4, MEM])
        o_pmv = self.dout("o_pmv", [128, DEPTH, 2, 512])
        V = self.xattn_views()
        qT, oT, hmT, kmT, vm = V["qT"], V["oT"], V["hmT"], V["kmT"], V["vm"]
        self.barrier_item()
        for c in range(KC):
            st = {}

            def load(c=c, st=st):
                st["k"], st["v"] = self.xring.next()
                self.dma("sp", st["v"][:, 0:MEM], memT[:, c, :], reads=[], writes=[st["k"]], chan=st["k"])

            def comp(c=c, st=st):
                g = self.vec[:, (V_MEM + L) * 16 + c:(V_MEM + L) * 16 + c + 1]
                P.op("dve", lambda e: e.scalar_tensor_tensor(out=hmT[:, c, :], in0=st["v"][:, 0:MEM], scalar=g, in1=self.rstdm[:, :], op0=ALU.mult, op1=ALU.mult),
                     reads=[st["k"], "rstdm"], writes=[("hmT", c)])
            self.add_item([load], comp)
        for h in range(4 if "nomemkv" not in self.dbg else 0):
            st = {}

            def load(h=h, st=st):
                self.wload(st, w_kv[Lw, h], 2, KC, 128)

            def comp(h=h, st=st):
                w = st["w"]
                pk, pv = self.psum()
                P.op("pe", self.mm_acc(pv, MEM, [w[:, k, 0, :] for k in range(KC)], [hmT[:, k, :] for k in range(KC)]),
                     reads=[st["k"]] + [("hmT", k) for k in range(KC)], writes=[pk])
                sk, sv = self.xnring.next()
                P.op("act", lambda e: e.copy(out=sv[:, 0:MEM], in_=pv[:, 0:MEM]), reads=[pk], writes=[sk])
                P.op("dve", lambda e: e.tensor_copy(out=kmT[:, h, :], in_=sv[:, 0:MEM]), reads=[sk], writes=[("kmT", h)])
                self.dma("sp", o_pmk[:, L, h, :], sv[:, 0:MEM], reads=[sk], writes=[("o_pmk", L, h)], chan=sk)
                for b in range(2):
                    pk2, pv2 = self.psum()
                    P.op("pe", self.mm_acc(pv2, 128, [hmT[:, k, b * 128:(b + 1) * 128] for k in range(KC)], [w[:, k, 1, :] for k in range(KC)]),
                         reads=[st["k"]] + [("hmT", k) for k in range(KC)], writes=[pk2])
                    sk2, sv2 = self.xnring.next()
                    P.op("act", lambda e, sv2=sv2, pv2=pv2: e.copy(out=sv2[:, 0:128], in_=pv2[:, 0:128]), reads=[pk2], writes=[sk2])
                    P.op("dve", lambda e, b=b, sv2=sv2: e.tensor_copy(out=vm[:, b, h * 128:(h + 1) * 128], in_=sv2[:, 0:128]), reads=[sk2], writes=[("vm", h, b)])
                    self.dma("sp", o_pmv[:, L, b, h * 128:(h + 1) * 128], sv2[:, 0:128], reads=[sk2], writes=[("o_pmv", L, h, b)], chan=sk2)
            self.add_item([load], comp)
        hTf = self.hTf
        tiles = self.tiles()
        for ti, (c0, tw) in enumerate(tiles):
            cgs = self.colgroups(c0, tw)
            if ti == 0:
                self.prologue(c0, tw, V_XA + L, hview=hTf, hoff=c0)
            if ti == 1 and "noloadc" not in self.dbg:
                def loadc():
                    for s in range(NS):
                        self.dma("pool", V["ckv"][s][:, 0:1024], cmkT[:, L, s, :], reads=[], writes=[("ckv", s)], chan=("ckv", s))
                        self.dma("pool", V["ckv"][s][:, 1024:2048], cmv[:, L, s, :], reads=[], writes=[("ckv", s)], chan=("ckv", s))
                self.add_item([], loadc)
            for h in range(4):
                st = {}

                def load(h=h, st=st):
                    self.wload(st, w_q[Lw, h], 1, KC, 128)

                def comp(h=h, st=st, c0=c0, cgs=cgs, ti=ti):
                    w = st["w"]
                    for (j0, jw) in cgs:
                        pk, pv = self.psum()
                        P.op("pe", self.mm_acc(pv, jw, [w[:, k, 0, :] for k in range(KC)], [hTf[:, k, c0 + j0:c0 + j0 + jw] for k in range(KC)]),
                             reads=[st["k"]] + [("hT", k, c0 + j0) for k in range(KC)], writes=[pk])
                        P.op("act", lambda e, pv=pv, j0=j0, jw=jw: e.copy(out=qT[:, h, j0:j0 + jw], in_=pv[:, 0:jw]), reads=[pk], writes=[("qT", h, c0 + j0)])
                        if "noattn" in self.dbg:
                            continue
                        if j0 < 1024:
                            if "noprompt" in self.dbg:
                                continue
                            r = self.pt_i % 2
                            self.pt_i += 1
                            self.attn_block(h, kmT[:, h, :], [("kmT", h)], vm[:, :, h * 128:(h + 1) * 128], [("vm", h, 0), ("vm", h, 1)],
                                            qT[:, h, j0:j0 + jw], ("qT", h, c0 + j0), oT[:, h, j0:j0 + jw], ("oT", h, c0 + j0), jw, V["PT"][r], ("PT", r))
                        else:
                            for s in range(NS):
                                if "nosample" in self.dbg:
                                    continue
                                r = self.pt_i % 2
                                self.pt_i += 1
                                ck = V["ckv"][s]
                                kT = ck[:, 0:1024].rearrange("p (h t) -> p h t", h=4)[:, h, :]
                                vT = ck[:, 1024:2048].rearrange("p (b t) -> p b t", b=2)[:, :, h * 128:(h + 1) * 128]
                                self.attn_block(h, kT, [("ckv", s)], vT, [("ckv", s)], qT[:, h, j0:j0 + NS], ("qT", h, c0 + j0),
                                                oT[:, h, j0 + s:j0 + s + 1], ("oT", h, c0 + j0), NS, V["PT"][r], ("PT", r), sel=s)
                self.add_item([load], comp)
            if ti + 1 < len(tiles):
                self.prologue(tiles[ti + 1][0], tiles[ti + 1][1], V_XA + L, hview=hTf, hoff=tiles[ti + 1][0])
            self.out_proj(c0, tw, lambda m: w_o[Lw, m], 4, lambda k, j0, jw: oT[:, k, j0:j0 + jw],
                          lambda gc: [("oT", k, gc) for k in range(4)], 1.0, hv=hTf, ho=c0)

    def hgrn_mixer(self, L):
        P = self.P
        w_in = self.wt("hgrn_w_in", [1, KC, 128, 4 * KC * 128])
        w_out = self.wt("hgrn_w_out", [1, KC, 128, KC * 128])
        shg = self.din("shg", [128, NS, KC, 128])
        o_phg = self.dout("o_phg", [128, KC, 128])
        o_shg = self.dout("o_shg", [128, NS, KC, 128])
        B = self.B
        W = 1026
        yT = B[:, 0:16 * W].rearrange("p (c t) -> p c t", c=16)
        o = 16 * W
        Sst = B[:, o:o + 4096].bitcast(F32).rearrange("p (h e) -> p h e", h=16); o += 4096
        Sbf2 = B[:, o:o + 256].rearrange("p (h e) -> p h e", h=2); o += 256
        Am = [B[:, o + i * 128:o + (i + 1) * 128] for i in range(2)]; o += 256
        lbv = B[:, o:o + 5 * 32].bitcast(F32); o += 160
        A2 = self.arena[:, 16 * W:16 * TWF]
        a = 0
        wk = []
        for i in range(5):
            wk.append(A2[:, a:a + 2 * W].bitcast(F32)); a += 2 * W
        w1, w2, w3, w4, vT = wk
        w4b = w4.bitcast(BF16)
        khb, vtb = w4b[:, 0:W], w4b[:, W:2 * W]
        qt = A2[:, a:a + W]; a += W
        kt = A2[:, a:a + W]; a += W
        gt = A2[:, a:a + W]; a += W
        vtok = A2[:, a:a + 1024].rearrange("p (b e) -> p b e", b=8); a += 1024
        ktok = A2[:, a:a + 1024].rearrange("p (b e) -> p b e", b=8); a += 1024
        assert a <= 16 * TWF - 16 * W, a
        wp = 1024
        self.barrier_item()

        def lbcomp():
            E = lbv[:, 0:64].rearrange("p (l h) -> p l h", l=4)
            for l in range(4):
                P.op("act", lambda e, l=l: e.activation(out=E[:, l, :], in_=self.vec[:, (V_LB + l) * 16:(V_LB + l + 1) * 16], func=AF.Exp), writes=[("lbE", l)])
            P.op("dve", lambda e: e.tensor_tensor(out=lbv[:, 64:80], in0=E[:, 0, :], in1=E[:, 1, :], op=ALU.add), reads=[("lbE", 0), ("lbE", 1)], writes=["lbs"])
            P.op("dve", lambda e: e.tensor_tensor(out=lbv[:, 64:80], in0=lbv[:, 64:80], in1=E[:, 2, :], op=ALU.add), reads=[("lbE", 2), "lbs"], writes=["lbs"])
            P.op("dve", lambda e: e.tensor_tensor(out=lbv[:, 64:80], in0=lbv[:, 64:80], in1=E[:, 3, :], op=ALU.add), reads=[("lbE", 3), "lbs"], writes=["lbs"])
            P.op("dve", lambda e: e.reciprocal(out=lbv[:, 64:80], in_=lbv[:, 64:80]), reads=["lbs"], writes=["lbs"])
            P.op("dve", lambda e: e.tensor_copy(out=lbv[:, 0:16], in_=E[:, 1, :]), reads=[("lbE", 1)], writes=[("lbE", 0)])
            for j in range(2, L + 1):
                P.op("dve", lambda e, j=j: e.tensor_tensor(out=lbv[:, 0:16], in0=lbv[:, 0:16], in1=E[:, j, :], op=ALU.add), reads=[("lbE", j), ("lbE", 0)], writes=[("lbE", 0)])
            P.op("dve", lambda e: e.tensor_tensor(out=lbv[:, 16:32], in0=lbv[:, 0:16], in1=lbv[:, 64:80], op=ALU.mult), reads=[("lbE", 0), "lbs"], writes=["lb"])
            P.op("dve", lambda e: e.tensor_scalar(out=lbv[:, 32:48], in0=lbv[:, 16:32], scalar1=-1.0, scalar2=1.0, op0=ALU.mult, op1=ALU.add), reads=["lb"], writes=["omlb"])
            P.op("dve", lambda e: e.tensor_scalar(out=lbv[:, 48:64], in0=lbv[:, 32:48], scalar1=-1.0, scalar2=None, op0=ALU.mult), reads=["omlb"], writes=["nomlb"])
            P.op("dve", lambda e: e.memset(Sst[:, :, :], 0.0), writes=[("S", h) for h in range(16)])
        self.add_item([], lbcomp)
        LB = lambda h: lbv[:, 16 + h:17 + h]
        OM = lambda h: lbv[:, 32 + h:33 + h]
        NOM = lambda h: lbv[:, 48 + h:49 + h]

        for ti, (c0, tw) in enumerate(self.tiles()):
            cgs = self.colgroups(c0, tw)
            self.prologue(c0, tw, V_MIX + L)
            for hd in range(KC):
                stA, stB = {}, {}

                def loadA(hd=hd, st=stA):
                    self.wload(st, w_in[0, hd][:, 0:4096], 2, KC, 128)

                def compA(hd=hd, st=stA, c0=c0, cgs=cgs):
                    for (j0, jw) in cgs:
                        qk, qv = self.psum()
                        fk, fv = self.psum()

                        def mm(e, j0=j0, jw=jw, qv=qv, fv=fv, v=st["w"]):
                            for t, pv in ((0, qv), (1, fv)):
                                for k in range(KC):
                                    ins = e.matmul(pv[:, 0:jw], lhsT=v[:, k, t, :], rhs=self.hT[:, k, j0:j0 + jw], start=(k == 0), stop=(k == KC - 1))
                            return ins
                        P.op("pe", mm, reads=[st["k"]] + [("hT", c, c0 + j0) for c in range(KC)], writes=[qk, fk])
                        P.op("act", lambda e, qv=qv, j0=j0, jw=jw: e.activation(out=w1[:, j0:j0 + jw], in_=qv[:, 0:jw], func=AF.Silu), reads=[qk], writes=["w1"])
                        P.op("act", lambda e, fv=fv, j0=j0, jw=jw: e.activation(out=w2[:, j0:j0 + jw], in_=fv[:, 0:jw], func=AF.Sigmoid), reads=[fk], writes=["w2"])
                self.add_item([loadA], compA)

                def loadB(hd=hd, st=stB):
                    self.wload(st, w_in[0, hd][:, 4096:8192], 2, KC, 128)

                def compB(hd=hd, st=stB, c0=c0, cgs=cgs, ti=ti, tw=tw):
                    for (j0, jw) in cgs:
                        ik, iv = self.psum()
                        gk_, gv = self.psum()

                        def mm(e, j0=j0, jw=jw, iv=iv, gv=gv, v=st["w"]):
                            for t, pv in ((0, iv), (1, gv)):
                                for k in range(KC):
                                    ins = e.matmul(pv[:, 0:jw], lhsT=v[:, k, t, :], rhs=self.hT[:, k, j0:j0 + jw], start=(k == 0), stop=(k == KC - 1))
                            return ins
                        P.op("pe", mm, reads=[st["k"]] + [("hT", c, c0 + j0) for c in range(KC)], writes=[ik, gk_])
                        P.op("act", lambda e, iv=iv, j0=j0, jw=jw: e.copy(out=vT[:, j0:j0 + jw], in_=iv[:, 0:jw]), reads=[ik], writes=["vT"])
                        P.op("act", lambda e, gv=gv, j0=j0, jw=jw: e.activation(out=gt[:, j0:j0 + jw], in_=gv[:, 0:jw], func=AF.Silu), reads=[gk_], writes=["gt"])
                    P.op("dve", lambda e: e.tensor_scalar(out=w3[:, 0:tw], in0=w2[:, 0:tw], scalar1=OM(hd), scalar2=LB(hd), op0=ALU.mult, op1=ALU.add), reads=["w2", "lb", "omlb"], writes=["w3"])
                    if ti == 1:
                        P.op("dve", lambda e: e.tensor_copy(out=self.fS[:, 0:NS], in_=w3[:, wp:wp + NS]), reads=["w3"], writes=["fS"])
                    P.op("act", lambda e: e.activation(out=w3[:, 0:wp], in_=w3[:, 0:wp], func=AF.Ln), reads=["w3"], writes=["w3"])
                    P.op("dve", lambda e: e.tensor_scalar(out=w2[:, 0:tw], in0=w2[:, 0:tw], scalar1=NOM(hd), scalar2=OM(hd), op0=ALU.mult, op1=ALU.add), reads=["w2", "nomlb", "omlb"], writes=["w2"])
                    P.op("dve", lambda e: e.tensor_tensor_scan(out=w4[:, 0:wp], data0=self.scanmask[:, 0:wp], data1=w3[:, 0:wp], initial=0.0, op0=ALU.mult, op1=ALU.add),
                         reads=["w3"], writes=["w4"])
                    P.op("act", lambda e: e.activation(out=w3[:, 0:wp], in_=w4[:, 0:wp], func=AF.Exp), reads=["w4"], writes=["w3"])
                    P.op("dve", lambda e: e.tensor_tensor(out=qt[:, 0:wp], in0=w1[:, 0:wp], in1=w3[:, 0:wp], op=ALU.mult), reads=["w1", "w3"], writes=["qt"])
                    if ti == 1:
                        P.op("dve", lambda e: e.tensor_copy(out=self.qS[:, 0:NS], in_=w1[:, wp:wp + NS]), reads=["w1"], writes=["qS"])
                    P.op("act", lambda e: e.activation(out=w1[:, 0:wp], in_=w4[:, 0:wp], func=AF.Exp, scale=-1.0), reads=["w4", "qt", "qS"], writes=["w1"])
                    if ti == 1:
                        P.op("dve", lambda e: e.tensor_copy(out=self.kS[:, 0:NS], in_=w2[:, wp:wp + NS]), reads=["w2"], writes=["kS"])
                    P.op("dve", lambda e: e.tensor_tensor(out=w2[:, 0:wp], in0=w2[:, 0:wp], in1=w1[:, 0:wp], op=ALU.mult), reads=["w2", "w1"], writes=["w2"])
                    P.op("act", lambda e: e.copy(out=kt[:, 0:wp], in_=w2[:, 0:wp]), reads=["w2"], writes=["kt"])
                    Ev = w3[:, 0:wp].rearrange("p (c t) -> p c t", t=64)
                    P.op("dve", lambda e: e.tensor_tensor(out=khb[:, 0:wp].rearrange("p (c t) -> p c t", t=64), in0=w2[:, 0:wp].rearrange("p (c t) -> p c t", t=64),
                                                          in1=Ev[:, :, 63:64].broadcast_to([128, wp // 64, 64]), op=ALU.mult), reads=["w2", "w3"], writes=["w4"])
                    P.op("pool", lambda e: e.tensor_copy(out=vtb[:, 0:wp], in_=vT[:, 0:wp]), reads=["vT", "w4"], writes=["w4"])
                    for (src, dst, dkey, eng) in ((khb, ktok, "ktok", "act"), (vtb, vtok, "vtok", "dve")):
                        pk, pv = self.psum()
                        pvb = pv.bitcast(BF16)

                        def tr(e, src=src, pvb=pvb):
                            for b in range(8):
                                ins = e.transpose(out=pvb[:, b * 128:(b + 1) * 128], in_=src[:, b * 128:(b + 1) * 128], identity=self.identB[:, :])
                            return ins
                        P.op("pe", tr, reads=["w4"], writes=[pk])
                        if eng == "act":
                            P.op("act", lambda e, pvb=pvb, dst=dst: e.copy(out=dst[:, :, :], in_=pvb[:, 0:1024].rearrange("p (b e) -> p b e", b=8)), reads=[pk], writes=[dkey])
                        else:
                            P.op("dve", lambda e, pvb=pvb, dst=dst: e.tensor_copy(out=dst[:, :, :], in_=pvb[:, 0:1024].rearrange("p (b e) -> p b e", b=8)), reads=[pk], writes=[dkey])
                    sb2 = [Sbf2[:, 0, :], Sbf2[:, 1, :]]
                    P.op("act", lambda e: e.copy(out=sb2[0], in_=Sst[:, hd, :]), reads=[("S", hd)], writes=[("Sb2", 0)])

                    def pre(b):
                        cs = slice(b * 128, (b + 1) * 128)
                        ak, av = self.psum()
                        P.op("pe", self.mm_acc(av, 128, [kt[:, cs]], [qt[:, cs]]), reads=["kt", "qt"], writes=[ak])
                        r = self.am_i % 2
                        self.am_i += 1
                        P.op("dve", lambda e, av=av, r=r: e.tensor_tensor(out=Am[r], in0=av[:, 0:128], in1=self.mask2[:, :], op=ALU.mult), reads=[ak], writes=[("Am", r)])
                        svs = []
                        for c in range(2):
                            sk, sv = self.psum()
                            ps_ = slice(c * 64, (c + 1) * 64)
                            P.op("pe", lambda e, sv=sv, b=b, ps_=ps_: e.matmul(sv[:, 0:128], lhsT=ktok[ps_, b, :], rhs=vtok[ps_, b, :], start=True, stop=True),
                                 reads=["ktok", "vtok"], writes=[sk])
                            svs.append((sk, sv))
                        return (b, r, svs)

                    ver = 0
                    pend = pre(0)
                    for b in range(8):
                        cur = pend
                        pend = pre(b + 1) if b + 1 < 8 else None
                        _, r, svs = cur
                        cs = slice(b * 128, (b + 1) * 128)
                        ok_, ov = self.psum()
                        P.op("pe", lambda e, ov=ov, r=r, b=b: e.matmul(ov[:, 0:128], lhsT=vtok[:, b, :], rhs=Am[r], start=True, stop=False),
                             reads=["vtok", ("Am", r)], writes=[ok_])
                        for c in range(2):
                            cc = slice(b * 128 + c * 64, b * 128 + (c + 1) * 64)
                            P.op("pe", lambda e, ov=ov, c=c, cc=cc, ver=ver: e.matmul(ov[:, c * 64:(c + 1) * 64], lhsT=sb2[ver], rhs=qt[:, cc], start=False, stop=(c == 1)),
                                 reads=[("Sb2", ver), "qt"], writes=[ok_])
                            sk, sv = svs[c]
                            el = w3[:, b * 128 + c * 64 + 63:b * 128 + c * 64 + 64]
                            P.op("dve", lambda e, sv=sv, el=el: e.scalar_tensor_tensor(out=Sst[:, hd, :], in0=Sst[:, hd, :], scalar=el, in1=sv[:, 0:128], op0=ALU.mult, op1=ALU.add),
                                 reads=[sk, "w3", ("S", hd)], writes=[("S", hd)])
                            ver ^= 1
                            P.op("act", lambda e, ver=ver: e.copy(out=sb2[ver], in_=Sst[:, hd, :]), reads=[("S", hd)], writes=[("Sb2", ver)])
                        P.op("act", lambda e, ov=ov, cs=cs: e.copy(out=w1[:, cs], in_=ov[:, 0:128]), reads=[ok_], writes=["w1"])
                    if ti == 1:
                        self.dma("sp", o_phg[:, hd, :], Sst[:, hd, :], reads=[("S", hd)], writes=[("o_phg", hd)], chan=("ophg", hd % 2))
                        self.hgrn_sample(hd, shg, o_shg, vT, wp, w1)
                    for (j0, jw) in cgs:
                        P.op("act", lambda e, j0=j0, jw=jw: e.activation(out=qt[:, j0:j0 + jw], in_=w1[:, j0:j0 + jw], func=AF.Square), reads=["w1"], writes=["qt"])
                        pk, pv = self.psum()
                        P.op("pe", self.mm_acc(pv, jw, [self.ones1[:, :]], [qt[:, j0:j0 + jw]]), reads=["qt"], writes=[pk])
                        tk, tv = self.tmpring.next()
                        P.op("act", lambda e, pv=pv, tv=tv, jw=jw: e.activation(out=tv[:, 0:jw], in_=pv[:, 0:jw], func=AF.Sqrt, bias=self.epsc[:, 0:1], scale=1.0 / 128), reads=[pk], writes=[tk])
                        P.op("dve", lambda e, tv=tv, jw=jw: e.reciprocal(out=tv[:, 0:jw], in_=tv[:, 0:jw]), reads=[tk], writes=[tk])
                        gn = self.vec[:, V_HGN * 16 + hd:V_HGN * 16 + hd + 1]
                        P.op("dve", lambda e, tv=tv, j0=j0, jw=jw, gn=gn: e.scalar_tensor_tensor(out=tv[:, 0:jw], in0=w1[:, j0:j0 + jw], scalar=gn, in1=tv[:, 0:jw], op0=ALU.mult, op1=ALU.mult),
                             reads=[tk, "w1"], writes=[tk])
                        P.op("dve", lambda e, tv=tv, j0=j0, jw=jw: e.tensor_tensor(out=yT[:, hd, j0:j0 + jw], in0=tv[:, 0:jw], in1=gt[:, j0:j0 + jw], op=ALU.mult),
                             reads=[tk, "gt"], writes=[("yT", hd, c0 + j0)])
                self.add_item([loadB], compB)
            self.out_proj(c0, tw, lambda m: w_out[0, m], KC, lambda k, j0, jw: yT[:, k, j0:j0 + jw],
                          lambda gc: [("yT", c, gc) for c in range(KC)], 1.0)

    def hgrn_sample(self, hd, shg, o_shg, vT, wp, w1):
        P = self.P
        fS, qS, kS = self.fS, self.qS, self.kS
        P.op("dve", lambda e: e.tensor_tensor(out=self.fq16[:, 0:NS], in0=fS[:, 0:NS], in1=qS[:, 0:NS], op=ALU.mult), reads=["fS", "qS"], writes=["fq"])
        P.op("dve", lambda e: e.tensor_tensor(out=self.fq16[:, NS:2 * NS], in0=kS[:, 0:NS], in1=qS[:, 0:NS], op=ALU.mult), reads=["kS", "qS"], writes=["fq"])
        kqk, kqv = self.psum()
        P.op("pe", self.mm_acc(kqv, NS, [self.ones1[:, :]], [self.fq16[:, NS:2 * NS]]), reads=["fq"], writes=[kqk])
        P.op("dve", lambda e: e.tensor_tensor(out=self.fq[:, 2 * NS:3 * NS], in0=kqv[:, 0:NS], in1=vT[:, wp:wp + NS], op=ALU.mult), reads=[kqk, "vT"], writes=["fq2"])
        for s in range(NS):
            r = self.s0_i % 2
            self.s0_i += 1
            s0 = self.S0[:, r, :]
            s0b = self.S0b[:, r, :]
            self.dma("sp", s0, shg[:, s, hd, :], reads=[], writes=[("S0", r)], chan=("S0", r))
            P.op("act", lambda e, s0=s0, s0b=s0b: e.copy(out=s0b, in_=s0), reads=[("S0", r)], writes=[("S0b", r)])
            ok_, ov = self.psum()
            P.op("pe", self.mm_acc(ov, NS, [s0b], [self.fq16[:, 0:NS]]), reads=[("S0b", r), "fq"], writes=[ok_])
            P.op("dve", lambda e, ov=ov, s=s: e.tensor_tensor(out=w1[:, wp + s:wp + s + 1], in0=ov[:, s:s + 1], in1=self.fq[:, 2 * NS + s:2 * NS + s + 1], op=ALU.add),
                 reads=[ok_, "fq2"], writes=["w1"])
            dg = self.dg16[:, r, :]
            P.op("dve", lambda e, dg=dg, s=s: e.tensor_scalar(out=dg, in0=self.identB[:, :], scalar1=vT[:, wp + s:wp + s + 1], scalar2=None, op0=ALU.mult), reads=["vT"], writes=[("dg16", r)])
            bk, bv = self.psum()
            P.op("pe", self.mm_acc(bv, 128, [self.ones1[:, :]], [dg]), reads=[("dg16", r)], writes=[bk])
            tk, tv = self.tmpring.next()
            P.op("dve", lambda e, bv=bv, tv=tv, s=s: e.tensor_scalar(out=tv[:, 0:128], in0=bv[:, 0:128], scalar1=kS[:, s:s + 1], scalar2=None, op0=ALU.mult), reads=[bk, "kS"], writes=[tk])
            xk_, xv = self.xnring.next()
            P.op("dve", lambda e, tv=tv, xv=xv, s0=s0, s=s: e.scalar_tensor_tensor(out=xv[:, 0:128], in0=s0, scalar=fS[:, s:s + 1], in1=tv[:, 0:128], op0=ALU.mult, op1=ALU.add),
                 reads=[tk, ("S0", r), "fS"], writes=[xk_])
            self.dma("sp", o_shg[:, s, hd, :], xv[:, 0:128], reads=[xk_], writes=[("o_shg", s, hd)], chan=xk_)

    def attn_mixer(self, L):
        P = self.P
        w_qkv = self.wt("attn_w_qkv", [1, 3, KC, 128, 3 * KC * 128])
        w_out = self.wt("attn_w_out", [1, KC, 128, KC * 128])
        cwkT = self.din("cwkT", [128, 3, NS, KC, 128])
        cwv = self.din("cwv", [128, 3, NS, KC, 128])
        WG = (128, 512, 2048)
        DIL = (1, 4, 16)
        NBR = (16, 4, 1)
        o_pwk = [self.dout("o_pwk%d" % g, [128, KC, WG[g]]) for g in range(3)]
        o_pwv = [self.dout("o_pwv%d" % g, [128, KC, WG[g]]) for g in range(3)]
        o_swk = self.dout("o_swk", [128, 3, KC, NS])
        o_swv = self.dout("o_swv", [128, 3, KC, NS])
        oscr = self.nc.dram_tensor("oscr", [128, KC, TWF], BF16).ap()
        B = self.B
        hTf = self.hTf
        o = 0
        qp = B[:, o:o + 2048]; o += 2048
        kp = B[:, o:o + 2048]; o += 2048
        vtok = B[:, o:o + 2048].rearrange("p (b e) -> p b e", b=16); o += 2048
        acc_o = B[:, o:o + 4096].bitcast(F32); o += 4096
        PT = [B[:, o + i * 1024:o + (i + 1) * 1024].rearrange("p (c t) -> p c t", c=2) for i in range(2)]; o += 2048
        cosT = B[:, o:o + TWF]; o += TWF
        sinT = B[:, o:o + TWF]; o += TWF
        xb = [B[:, o + i * 512:o + (i + 1) * 512] for i in range(2)]; o += 1024
        xf = [B[:, o + i * 1024:o + (i + 1) * 1024].bitcast(F32) for i in range(2)]; o += 2048
        oTb = B[:, o:o + TWF]; o += TWF
        kcv = [B[:, o + i * 256 * NS:o + (i + 1) * 256 * NS] for i in range(2)]; o += 512 * NS
        vTb = B[:, o:o + 2048]; o += 2048
        assert o <= NB_EL, o
        acc_d = self.yr[:, :, :].rearrange("p a b -> p (a b)")[:, 0:2048]
        scale = 128 ** -0.5
        self.barrier_item()

        def setup():
            self.dma("pool", cosT, self.cst[:, C_COS:C_COS + TWF], reads=[], writes=["cosT"], chan="ccos")
            self.dma("pool", sinT, self.cst[:, C_SIN:C_SIN + TWF], reads=[], writes=["sinT"], chan="csin")
        self.add_item([], setup)
        for (c0, tw) in self.tiles():
            self.prologue(c0, tw, V_MIX + L, hview=hTf, hoff=c0)
        cgs = self.colgroups(0, TWF)
        allh = [("hT", k, j0) for k in range(KC) for (j0, _) in cgs]

        for hd in range(KC):
            for g in range(3):
                d = DIL[g]
                stQ, stV, stA = {}, {}, {}
                base = (g * 3) * KC * 128 + hd * 128

                def loadQ(st=stQ, g=g, hd=hd):
                    self.wload(st, w_qkv[0, g, hd][:, 0:4096], 2, KC, 128)

                def compQ(st=stQ, hd=hd, g=g, d=d):
                    w = st["w"]

                    def proj(j0, jw):
                        qk_, qv = self.psum()
                        kk_, kv = self.psum()

                        def mm(e, j0=j0, jw=jw, qv=qv, kv=kv):
                            for t, pv in ((0, qv), (1, kv)):
                                for k in range(KC):
                                    ins = e.matmul(pv[:, 0:jw], lhsT=w[:, k, t, :], rhs=hTf[:, k, j0:j0 + jw], start=(k == 0), stop=(k == KC - 1))
                            return ins
                        P.op("pe", mm, reads=[st["k"]] + [("hT", k, j0) for k in range(KC)], writes=[qk_, kk_])
                        return (j0, jw, qk_, qv, kk_, kv)

                    pend = None
                    for cg_ in list(cgs) + [None]:
                        nxt = proj(cg_[0], cg_[1]) if cg_ is not None else None
                        cur, pend = pend, nxt
                        if cur is None:
                            continue
                        j0, jw, qk_, qv, kk_, kv = cur
                        for t, pk, pv in ((0, qk_, qv), (1, kk_, kv)):
                            r = self.xf_i % 2
                            self.xf_i += 1
                            P.op("act", lambda e, pv=pv, r=r, jw=jw: e.copy(out=xf[r][:, 0:jw], in_=pv[:, 0:jw]), reads=[pk], writes=[("xf", r)])
                            P.op("pool", lambda e, r=r, jw=jw: e.tensor_copy(out=xb[r][:, 0:jw], in_=xf[r][:, 0:jw]), reads=[("xf", r)], writes=[("xb", r)])
                            rk, rv = self.psum()
                            P.op("pe", self.mm_acc(rv, jw, [self.Pm[:, :]], [xb[r][:, 0:jw]]), reads=[("xb", r)], writes=[rk])
                            t1k, t1 = self.tmpring.next()
                            P.op("dve", lambda e, t1=t1, r=r, j0=j0, jw=jw: e.tensor_tensor(out=t1[:, 0:jw], in0=xf[r][:, 0:jw], in1=cosT[:, j0:j0 + jw], op=ALU.mult),
                                 reads=[("xf", r), "cosT"], writes=[t1k])
                            t2k, t2 = self.tmpring.next()
                            P.op("dve", lambda e, t2=t2, rv=rv, j0=j0, jw=jw: e.tensor_tensor(out=t2[:, 0:jw], in0=rv[:, 0:jw], in1=sinT[:, j0:j0 + jw], op=ALU.mult),
                                 reads=[rk, "sinT"], writes=[t2k])
                            if j0 < 2048:
                                dstbuf, dkey = (qp, "qp") if t == 0 else (kp, "kp")
                                dst = dstbuf.rearrange("p (r m) -> p m r", r=d)[:, j0 // d:(j0 + jw) // d, :]
                            if t == 0:
                                if j0 < 2048:
                                    P.op("dve", lambda e, t1=t1, t2=t2, dst=dst, jw=jw: e.tensor_tensor(out=dst, in0=t1[:, 0:jw].rearrange("p (m r) -> p m r", r=d),
                                                                                                      in1=t2[:, 0:jw].rearrange("p (m r) -> p m r", r=d), op=ALU.add),
                                         reads=[t1k, t2k], writes=["qp"])
                                else:
                                    P.op("dve", lambda e, t1=t1, t2=t2: e.tensor_tensor(out=self.qS16[:, 0:NS], in0=t1[:, 0:NS], in1=t2[:, 0:NS], op=ALU.add), reads=[t1k, t2k], writes=["qS16"])
                            else:
                                fk, kf = self.xnring.next()
                                P.op("dve", lambda e, t1=t1, t2=t2, kf=kf, jw=jw: e.tensor_tensor(out=kf[:, 0:jw], in0=t1[:, 0:jw], in1=t2[:, 0:jw], op=ALU.add), reads=[t1k, t2k], writes=[fk])
                                if j0 < 2048:
                                    P.op("act", lambda e, kf=kf, dst=dst, jw=jw: e.copy(out=dst, in_=kf[:, 0:jw].rearrange("p (m r) -> p m r", r=d)), reads=[fk], writes=["kp"])
                                    lo = 2048 - WG[g]
                                    a = max(lo, j0)
                                    if a < j0 + jw:
                                        self.dma("sp", o_pwk[g][:, hd, a - lo:j0 + jw - lo], kf[:, a - j0:jw], reads=[fk], writes=[("o_pwk", g, hd, j0)], chan=fk)
                                else:
                                    P.op("act", lambda e, kf=kf: e.copy(out=self.kSf[:, 0:NS], in_=kf[:, 0:NS]), reads=[fk], writes=["kSf"])
                                    self.dma("sp", o_swk[:, g, hd, :], kf[:, 0:NS], reads=[fk], writes=[("o_swk", g, hd)], chan=fk)
                self.add_item([loadQ], compQ)

                def loadV(st=stV, g=g, hd=hd):
                    self.wload(st, w_qkv[0, g, hd][:, 4096:6144], 1, KC, 128)

                def compV(st=stV, hd=hd, g=g, d=d):
                    w = st["w"]
                    nb = 2048 // d // 128
                    lo = 2048 - WG[g]
                    for (j0, jw) in cgs:
                        pk, pv = self.psum()
                        P.op("pe", self.mm_acc(pv, jw, [w[:, k, 0, :] for k in range(KC)], [hTf[:, k, j0:j0 + jw] for k in range(KC)]),
                             reads=[st["k"]] + [("hT", k, j0) for k in range(KC)], writes=[pk])
                        sk, sv = self.xnring.next()
                        P.op("act", lambda e, pv=pv, sv=sv, jw=jw: e.copy(out=sv[:, 0:jw], in_=pv[:, 0:jw]), reads=[pk], writes=[sk])
                        if j0 < 2048:
                            P.op("dve", lambda e, sv=sv, j0=j0, jw=jw: e.tensor_copy(out=vTb[:, j0:j0 + jw], in_=sv[:, 0:jw]), reads=[sk], writes=[("vTb", j0)])
                            a = max(lo, j0)
                            if a < j0 + jw:
                                self.dma("sp", o_pwv[g][:, hd, a - lo:j0 + jw - lo], sv[:, a - j0:jw], reads=[sk], writes=[("o_pwv", g, hd, j0)], chan=sk)
                        else:
                            P.op("dve", lambda e, sv=sv: e.tensor_copy(out=self.vSf[:, 0:NS], in_=sv[:, 0:NS]), reads=[sk], writes=["vSf"])
                            self.dma("sp", o_swv[:, g, hd, :], sv[:, 0:NS], reads=[sk], writes=[("o_swv", g, hd)], chan=sk)
                    vperm = vTb.rearrange("p (m r) -> p r m", r=d)
                    for half in range(2):
                        pk, pv = self.psum()
                        pvb = pv.bitcast(BF16)

                        def tr(e, half=half, pvb=pvb):
                            for i in range(8):
                                blk = half * 8 + i
                                r_, mi = blk // nb, blk % nb
                                ins = e.transpose(out=pvb[:, i * 128:(i + 1) * 128], in_=vperm[:, r_, mi * 128:(mi + 1) * 128], identity=self.identB[:, :])
                            return ins
                        P.op("pe", tr, reads=[("vTb", j) for j in (0, 512, 1024, 1536)], writes=[pk])
                        P.op("dve", lambda e, half=half, pvb=pvb: e.tensor_copy(out=vtok[:, half * 8:(half + 1) * 8, :], in_=pvb[:, 0:1024].rearrange("p (b e) -> p b e", b=8)),
                             reads=[pk], writes=[("vtok", 2 * half), ("vtok", 2 * half + 1)])
                self.add_item([loadV], compV)

                def loadA(st=stA, hd=hd, g=g):
                    r = self.kcv_i % 2
                    self.kcv_i += 1
                    st["r"] = r
                    self.dma("pool", kcv[r][:, 0:128 * NS].rearrange("p (s e) -> p s e", s=NS), cwkT[:, g, :, hd, :], reads=[], writes=[("kcv", r)], chan=("kcv", r))
                    self.dma("pool", kcv[r][:, 128 * NS:256 * NS].rearrange("p (s e) -> p s e", s=NS), cwv[:, g, :, hd, :], reads=[], writes=[("kcv", r)], chan=("kcv", r))

                def compA(st=stA, hd=hd, g=g, d=d):
                    nbr = NBR[g]

                    def scores(q4):
                        qbs = [q4 * 4 + i for i in range(4)]
                        hp = [(qb % nbr) != 0 for qb in qbs]
                        r = self.pt_i % 2
                        self.pt_i += 1
                        ck, cv_ = self.psum()

                        def mmc(e, qbs=qbs, cv_=cv_):
                            for i, qb in enumerate(qbs):
                                e.matmul(cv_[:, i * 128:(i + 1) * 128], lhsT=kp[:, qb * 128:(qb + 1) * 128], rhs=qp[:, qb * 128:(qb + 1) * 128], start=True, stop=False)
                                ins = e.matmul(cv_[:, i * 128:(i + 1) * 128], lhsT=self.identB[:, :], rhs=self.maskC[:, :], start=False, stop=True)
                            return ins
                        P.op("pe", mmc, reads=["kp", "qp"], writes=[ck])
                        P.op("act", lambda e, cv_=cv_, r=r: e.activation(out=PT[r][:, 0, :], in_=cv_[:, 0:512], func=AF.Exp, scale=scale), reads=[ck], writes=[("PT", r, 0)])
                        if any(hp):
                            pk_, pv_ = self.psum()

                            def mmp(e, qbs=qbs, pv_=pv_, hp=hp):
                                for i, qb in enumerate(qbs):
                                    if not hp[i]:
                                        continue
                                    e.matmul(pv_[:, i * 128:(i + 1) * 128], lhsT=kp[:, (qb - 1) * 128:qb * 128], rhs=qp[:, qb * 128:(qb + 1) * 128], start=True, stop=False)
                                    ins = e.matmul(pv_[:, i * 128:(i + 1) * 128], lhsT=self.identB[:, :], rhs=self.maskP[:, :], start=False, stop=True)
                                return ins
                            P.op("pe", mmp, reads=["kp", "qp"], writes=[pk_])
                            lo_ = 0 if hp[0] else 128
                            P.op("act", lambda e, pv_=pv_, r=r, lo_=lo_: e.activation(out=PT[r][:, 1, lo_:512], in_=pv_[:, lo_:512], func=AF.Exp, scale=scale), reads=[pk_], writes=[("PT", r, 1)])

                        return (q4, qbs, hp, r)

                    def pvstage(info):
                        q4, qbs, hp, r = info
                        ok_, ov = self.psum()
                        dk_, dv = self.psum()
                        for (okk, outv, is_den) in ((ok_, ov, False), (dk_, dv, True)):
                            def mmo(e, qbs=qbs, outv=outv, hp=hp, is_den=is_den, r=r):
                                for i, qb in enumerate(qbs):
                                    l1 = self.ones1[:, :] if is_den else vtok[:, qb, :]
                                    ins = e.matmul(outv[:, i * 128:(i + 1) * 128], lhsT=l1, rhs=PT[r][:, 0, i * 128:(i + 1) * 128], start=True, stop=not hp[i])
                                    if hp[i]:
                                        l2 = self.ones1[:, :] if is_den else vtok[:, qb - 1, :]
                                        ins = e.matmul(outv[:, i * 128:(i + 1) * 128], lhsT=l2, rhs=PT[r][:, 1, i * 128:(i + 1) * 128], start=False, stop=True)
                                return ins
                            P.op("pe", mmo, reads=[("PT", r, 0), ("PT", r, 1)] + [("vtok", b) for b in range(4)], writes=[okk])
                        if g == 0:
                            vo = acc_o[:, q4 * 512:(q4 + 1) * 512]
                            vd = acc_d[:, q4 * 512:(q4 + 1) * 512]
                            P.op("dve", lambda e, vo=vo, ov=ov: e.tensor_copy(out=vo, in_=ov[:, 0:512]), reads=[ok_], writes=["acc_o"])
                            P.op("act", lambda e, vd=vd, dv=dv: e.copy(out=vd, in_=dv[:, 0:512]), reads=[dk_], writes=["acc_d"])
                        else:
                            if g == 1:
                                vo = acc_o.rearrange("p (m r) -> p r m", r=4)[:, q4, :]
                                vd = acc_d.rearrange("p (m r) -> p r m", r=4)[:, q4, :]
                                io, id_ = ov[:, 0:512], dv[:, 0:512]
                            else:
                                vo = acc_o.rearrange("p (m r) -> p r m", r=16)[:, q4 * 4:(q4 + 1) * 4, :]
                                vd = acc_d.rearrange("p (m r) -> p r m", r=16)[:, q4 * 4:(q4 + 1) * 4, :]
                                io, id_ = ov[:, 0:512].rearrange("p (r m) -> p r m", r=4), dv[:, 0:512].rearrange("p (r m) -> p r m", r=4)
                            P.op("dve", lambda e, vo=vo, io=io: e.tensor_tensor(out=vo, in0=vo, in1=io, op=ALU.add), reads=[ok_, "acc_o"], writes=["acc_o"])
                            P.op("dve", lambda e, vd=vd, id_=id_: e.tensor_tensor(out=vd, in0=vd, in1=id_, op=ALU.add), reads=[dk_, "acc_d"], writes=["acc_d"])

                    pend = None
                    for q4_ in list(range(4)) + [None]:
                        nxt = scores(q4_) if q4_ is not None else None
                        cur, pend = pend, nxt
                        if cur is not None:
                            pvstage(cur)
                    r = st["r"]
                    kcT = kcv[r][:, 0:128 * NS].rearrange("p (s e) -> p s e", s=NS)
                    vc = kcv[r][:, 128 * NS:256 * NS].rearrange("p (s e) -> p s e", s=NS)
                    P.op("dve", lambda e: e.tensor_tensor(out=self.pr16[:, 0:NS], in0=self.qS16[:, 0:NS], in1=self.kSf[:, 0:NS], op=ALU.mult), reads=["qS16", "kSf"], writes=["pr16"])
                    nk_, nv = self.psum()
                    P.op("pe", self.mm_acc(nv, NS, [self.ones1[:, :]], [self.pr16[:, 0:NS]]), reads=["pr16"], writes=[nk_])
                    sk_, sv_ = self.psum()

                    def mms(e, sv_=sv_):
                        for s in range(NS):
                            ins = e.matmul(sv_[:, s * NS:(s + 1) * NS], lhsT=kcT[:, s, :], rhs=self.qS16[:, 0:NS], start=True, stop=True)
                        return ins
                    P.op("pe", mms, reads=[("kcv", r), "qS16"], writes=[sk_])
                    P.op("act", lambda e, sv_=sv_: e.activation(out=self.PS[:, 0:NS * NS], in_=sv_[:, 0:NS * NS], func=AF.Exp, scale=scale), reads=[sk_], writes=["PS"])
                    P.op("act", lambda e, nv=nv: e.activation(out=self.pn[:, 0:NS], in_=nv[:, 0:NS], func=AF.Exp, scale=scale), reads=[nk_], writes=["pn"])
                    ok_, ov = self.psum()
                    dk_, dv = self.psum()

                    def mmso(e, ov=ov, dv=dv):
                        for s in range(NS):
                            e.matmul(ov[:, s * NS:(s + 1) * NS], lhsT=vc[:, s, :], rhs=self.PS[:, s * NS:(s + 1) * NS], start=True, stop=True)
                        for s in range(NS):
                            ins = e.matmul(dv[:, s * NS:(s + 1) * NS], lhsT=self.ones1[:, :], rhs=self.PS[:, s * NS:(s + 1) * NS], start=True, stop=True)
                        return ins
                    P.op("pe", mmso, reads=[("kcv", r), "PS"], writes=[ok_, dk_])
                    P.op("dve", lambda e: e.tensor_tensor(out=self.vpn[:, 0:NS], in0=self.vSf[:, 0:NS], in1=self.pn[:, 0:NS], op=ALU.mult), reads=["vSf", "pn"], writes=["vpn"])
                    for s in range(NS):
                        P.op("dve", lambda e, s=s, ov=ov: e.tensor_tensor(out=self.aSo[:, g, s:s + 1], in0=ov[:, s * NS + s:s * NS + s + 1], in1=self.vpn[:, s:s + 1], op=ALU.add),
                             reads=[ok_, "vpn"], writes=[("aSo", g)])
                        P.op("dve", lambda e, s=s, dv=dv: e.tensor_tensor(out=self.aSd[:, g, s:s + 1], in0=dv[:, s * NS + s:s * NS + s + 1], in1=self.pn[:, s:s + 1], op=ALU.add),
                             reads=[dk_, "pn"], writes=[("aSd", g)])
                    if g == 2:
                        P.op("dve", lambda e: e.reciprocal(out=acc_d, in_=acc_d), reads=["acc_d"], writes=["acc_d"])
                        P.op("dve", lambda e: e.tensor_tensor(out=oTb[:, 0:2048], in0=acc_o, in1=acc_d, op=ALU.mult), reads=["acc_o", "acc_d"], writes=["oTb"])
                        for (buf, key) in ((self.aSo, "aSo"), (self.aSd, "aSd")):
                            P.op("dve", lambda e, buf=buf: e.tensor_tensor(out=buf[:, 0, :], in0=buf[:, 0, :], in1=buf[:, 1, :], op=ALU.add), reads=[(key, 0), (key, 1)], writes=[(key, 0)])
                            P.op("dve", lambda e, buf=buf: e.tensor_tensor(out=buf[:, 0, :], in0=buf[:, 0, :], in1=buf[:, 2, :], op=ALU.add), reads=[(key, 0), (key, 2)], writes=[(key, 0)])
                        P.op("dve", lambda e: e.reciprocal(out=self.aSd[:, 0, :], in_=self.aSd[:, 0, :]), reads=[("aSd", 0)], writes=[("aSd", 0)])
                        P.op("dve", lambda e: e.tensor_tensor(out=oTb[:, 2048:2048 + NS], in0=self.aSo[:, 0, :], in1=self.aSd[:, 0, :], op=ALU.mult), reads=[("aSo", 0), ("aSd", 0)], writes=["oTb"])
                        self.dma("sp", oscr[:, hd, :], oTb, reads=["oTb"], writes=[("oscr", hd)], chan="oscr")
                self.add_item([loadA], compA)

        self.barrier_item()
        oTj = [B[:, i * 8192:(i + 1) * 8192].rearrange("p (k t) -> p k t", k=KC) for i in range(2)]
        for ji, (j0, jw) in enumerate(cgs):
            r = ji % 2

            def loadO(j0=j0, jw=jw, r=r):
                self.dma("sp", oTj[r][:, :, 0:jw], oscr[:, :, j0:j0 + jw], reads=[("oscr", h) for h in range(KC)], writes=[("oTj", r)], chan=("oTj", r))
            self.add_item([], loadO)
            self.out_proj(j0, jw, lambda m: w_out[0, m], KC, lambda k, a, b_, r=r: oTj[r][:, k, a:a + b_],
                          lambda gc, r=r: [("oTj", r)], 1.0, hv=hTf, ho=j0)

    def build(self):
        nc = self.nc
        self.xT = self.din("xT", [128, KC, TWF])
        self.vecT = self.din("vecT", [128, NV * 16])
        self.yT = self.dout("yT", [128, KC, TWF])
        self.cst = self.din("cst", [128, NCST])
        self.xres = nc.dram_tensor("xres", [128, KC, TWF], F32).ap()
        P = self.P
        with contextlib.ExitStack() as st:
            def sb(name, shape, dt):
                return st.enter_context(nc.sbuf_tensor(name, list(shape), dt))
            NA = 16 * TWF
            NB = NB_EL
            arena = sb("arena", [128, NA + NB], BF16)
            self.arena = arena
            self.hT = arena[:, 0:16 * 1026].rearrange("p (c t) -> p c t", c=16)
            self.hTf = arena[:, 0:NA].rearrange("p (c t) -> p c t", c=16)
            self.B = arena[:, NA:NA + NB]
            wr = sb("wring", [128, 4, 4096], BF16)
            self.wring = Ring("w", [wr[:, i, :] for i in range(4)])
            xr = sb("xring", [128, 4, 1026], F32)
            self.xring = Ring("x", [xr[:, i, :] for i in range(4)])
            xn = sb("xnring", [128, 3, 512], F32)
            self.xnring = Ring("xn", [xn[:, i, :] for i in range(3)])
            tr = sb("tmpring", [128, 3, 512], F32)
            self.tmpring = Ring("tmp", [tr[:, i, :] for i in range(3)])
            yr = sb("yring", [128, 2, 1026], F32)
            self.yring = Ring("y", [yr[:, i, :] for i in range(2)])
            self.yr = yr
            self.Pm = sb("Pm", [128, 128], BF16)
            self.identB = sb("identB", [128, 128], BF16)
            self.maskC = sb("maskC", [128, 128], BF16)
            self.maskP = sb("maskP", [128, 128], BF16)
            self.qS16 = sb("qS16", [128, NS], BF16)
            self.pr16 = sb("pr16", [128, NS], BF16)
            self.PS = sb("PS", [128, NS * NS], BF16)
            self.kSf = sb("kSf", [128, NS], F32)
            self.vSf = sb("vSf", [128, NS], F32)
            self.pn = sb("pn", [128, NS], F32)
            self.vpn = sb("vpn", [128, NS], F32)
            self.aSo = sb("aSo", [128, 3, NS], F32)
            self.aSd = sb("aSd", [128, 3, NS], F32)
            self.xf_i = 0
            self.kcv_i = 0
            self.rstd = sb("rstd", [128, TWF], F32)
            self.rstdm = sb("rstdm", [128, MEM], F32)
            self.vec = sb("vec", [128, NV * 16], F32)
            self.onesD = sb("onesD", [128, 128], BF16)
            self.ones1 = sb("ones1", [128, 128], BF16)
            self.epsc = sb("epsc", [128, 1], F32)
            self.scr = sb("scr", [128, 8], F32)
            self.carry = sb("carry", [128, KC, 2], F32)
            self.fq16 = sb("fq16", [128, 2 * NS], BF16)
            self.S0b = sb("S0b", [128, 2, 128], BF16)
            self.dg16 = sb("dg16", [128, 2, 128], BF16)
            self.mask2 = sb("mask2", [128, 128], BF16)
            self.scanmask = sb("scanmask", [128, 1024], BF16)
            self.fS = sb("fS", [128, NS], F32)
            self.qS = sb("qS", [128, NS], F32)
            self.kS = sb("kS", [128, NS], F32)
            self.fq = sb("fq", [128, 3 * NS], F32)
            self.S0 = sb("S0", [128, 2, 128], F32)
            self.am_i = 0
            self.s0_i = 0
            self.ps = [st.enter_context(nc.psum_tensor("ps%d" % i, [128, 512], F32)) for i in range(8)]
            self.ps_i = 0
            self.ue_i = 0
            self.pt_i = 0

            self.dma("sp", self.vec[:, :], self.vecT, reads=[], writes=["vec"], chan="vec")
            P.op("dve", lambda e: e.memset(self.onesD[:, :], 1.0 / D), writes=["onesD"])
            P.op("dve", lambda e: e.memset(self.ones1[:, :], 1.0), writes=["ones1"])
            P.op("dve", lambda e: e.memset(self.epsc[:, :], EPS), writes=["epsc"])
            self.dma("pool", self.mask2[:, :], self.cst[:, C_MASK2:C_MASK2 + 128], reads=[], writes=["mask2"], chan="cstp")
            self.dma("pool", self.scanmask[:, :], self.cst[:, C_SCAN:C_SCAN + 1024], reads=[], writes=["scanmask"], chan="cstp")
            self.dma("pool", self.Pm[:, :], self.cst[:, C_PM:C_PM + 128], reads=[], writes=["Pm"], chan="cstp")
            self.dma("pool", self.identB[:, :], self.cst[:, C_IDENT:C_IDENT + 128], reads=[], writes=["identB"], chan="cstp")
            self.dma("pool", self.maskC[:, :], self.cst[:, C_MASKC:C_MASKC + 128], reads=[], writes=["maskC"], chan="cstp")
            self.dma("pool", self.maskP[:, :], self.cst[:, C_MASKP:C_MASKP + 128], reads=[], writes=["maskP"], chan="cstp")
            P.barrier("dve", lambda e: e.memset(self.scr[:, 0:1], 0.0))

            self.init_stats()
            if "xattn" in self.phases:
                self.mem_stats()
            for L in self.layers:
                for ph in self.phases:
                    if ph == "ffn1":
                        self.ffn(L, 1)
                    elif ph == "ffn2":
                        self.ffn(L, 2)
                    elif ph == "mix":
                        if L % 3 == 0:
                            self.conv_mixer(L)
                        elif L % 3 == 1:
                            self.hgrn_mixer(L)
                        else:
                            self.attn_mixer(L)
                    elif ph == "xattn":
                        self.xattn(L)
            if self.do_final:
                self.final_norm()

            DEP = 3
            n = len(self.items)
            for i in range(n + DEP):
                if i < n:
                    for ld in self.items[i].loads:
                        ld()
                if i >= DEP:
                    self.items[i - DEP].compute()
            P.emit(nc)
        return nc


def _fm(a):
    t = a.shape[0]
    return np.ascontiguousarray(a.reshape(t, 16, 128).transpose(2, 1, 0))


def _unfm(a):
    t = a.shape[2]
    return np.ascontiguousarray(a.transpose(2, 1, 0).reshape(t, 2048))


def pack_vecs(inp):
    rows = []
    for nm in ("norm_ffn1", "norm_mix", "norm_xattn", "norm_ffn2", "norm_mem"):
        rows += [inp[nm][i] for i in range(4)]
    rows.append(inp["norm_final"])
    for l in range(2):
        for j in range(3):
            rows.append(inp["conv_w"][l, j])
    rows += [inp["hgrn_lb_logits"][i] for i in range(4)]
    rows.append(np.zeros(2048, np.float32))
    v = np.stack([np.asarray(r, np.float32).reshape(2048) for r in rows])
    vt = np.ascontiguousarray(v.reshape(NV, 16, 128).transpose(2, 0, 1).reshape(128, NV * 16))
    vt[:, V_HGN * 16:(V_HGN + 1) * 16] = np.asarray(inp["hgrn_norm"][0], np.float32).T
    return vt


def make_consts():
    c = np.zeros((128, NCST), np.float32)
    i = np.arange(128)
    c[:, C_IDENT:C_IDENT + 128] = np.eye(128, dtype=np.float32)
    s_, t_ = i[:, None], i[None, :]
    c[:, C_MASK2:C_MASK2 + 128] = ((s_ // 64 == t_ // 64) & (s_ <= t_)).astype(np.float32)
    c[:, C_SCAN:C_SCAN + 1024] = (np.arange(1024) % 64 != 0).astype(np.float32)[None, :]
    for k in range(16):
        c[16 + k, C_PM + k] = -1.0
        c[k, C_PM + 16 + k] = 1.0
    NEG = -30000.0
    c[:, C_MASKC:C_MASKC + 128] = np.where(s_ <= t_, 0.0, NEG)
    c[:, C_MASKP:C_MASKP + 128] = np.where(s_ >= t_, 0.0, NEG)
    pos = np.concatenate([np.arange(S), np.full(NS, 16384)]).astype(np.float32)
    inv = (np.float32(500000.0) ** (-np.arange(16, dtype=np.float32) * np.float32(2.0) / np.float32(32.0))).astype(np.float32)
    ang = pos[None, :] * inv[:, None]
    c[:, C_COS:C_COS + TWF] = 1.0
    c[0:16, C_COS:C_COS + TWF] = np.cos(ang)
    c[16:32, C_COS:C_COS + TWF] = np.cos(ang)
    c[0:16, C_SIN:C_SIN + TWF] = np.sin(ang)
    c[16:32, C_SIN:C_SIN + TWF] = np.sin(ang)
    return c


def unperm_v(a, g):
    if g == 0:
        return a
    d = (1, 4, 16)[g]
    return np.ascontiguousarray(a.reshape(128 * d, 16, 128))


WIN_CACHE_NAMES = {"cwkT": ("cache_win_k0", "cache_win_k1", "cache_win_k2"),
                   "cwv": ("cache_win_v0", "cache_win_v1", "cache_win_v2")}
FFN_W_NAMES = {1: ("ffn1_w_gu", "ffn1_w_down"), 2: ("ffn2_w_gu", "ffn2_w_down")}


def _lay_in(w, nsec, nblk):
    L = w.shape[0]
    v = w.reshape(L, KC, 128, nsec, nblk, 128).transpose(0, 4, 2, 3, 1, 5)
    return np.ascontiguousarray(v).reshape(L, nblk, 128, nsec * KC * 128)


def _lay_out(w, nk):
    L = w.shape[0]
    v = w.reshape(L, nk, 128, KC, 128).transpose(0, 3, 2, 1, 4)
    return np.ascontiguousarray(v).reshape(L, KC, 128, nk * 128)


def _lay_qkv(w):
    L = w.shape[0]
    v = w.reshape(L, KC, 128, 3, 3, KC, 128).transpose(0, 3, 5, 2, 4, 1, 6)
    return np.ascontiguousarray(v).reshape(L, 3, KC, 128, 3 * KC * 128)


WEIGHT_LAYOUT = {
    "ffn1_w_gu": lambda w: _lay_in(w, 2, FC), "ffn2_w_gu": lambda w: _lay_in(w, 2, FC),
    "ffn1_w_down": lambda w: _lay_out(w, FC), "ffn2_w_down": lambda w: _lay_out(w, FC),
    "conv_w_in": lambda w: _lay_in(w, 3, KC), "conv_w_out": lambda w: _lay_out(w, KC),
    "hgrn_w_in": lambda w: _lay_in(w, 4, KC), "hgrn_w_out": lambda w: _lay_out(w, KC),
    "attn_w_qkv": _lay_qkv, "attn_w_out": lambda w: _lay_out(w, KC),
    "xattn_w_kv": lambda w: _lay_in(w, 2, 4), "xattn_w_q": lambda w: _lay_in(w, 1, 4), "xattn_w_o": lambda w: _lay_out(w, 4),
}


def core_inputs(inp, c, names, cache=None):
    sl = slice(c * NS, (c + 1) * NS)
    m = {}
    for nm in names:
        if nm == "xT":
            m[nm] = _fm(np.concatenate([inp["x_prompt"][c], inp["x_sample"][sl, 0]], axis=0))
        elif nm == "vecT":
            m[nm] = pack_vecs(inp)
        elif nm == "memT":
            m[nm] = _fm(inp["mem_prompt"][c])
        elif nm == "cst":
            m[nm] = make_consts()
        elif nm in ("cwkT", "cwv"):
            arrs = []
            for g, dil in enumerate((1, 4, 16)):
                a = inp[WIN_CACHE_NAMES[nm][g]][0, sl][:, ::dil]
                if nm == "cwkT":
                    arrs.append(a.transpose(3, 0, 2, 1))
                else:
                    arrs.append(a.transpose(1, 0, 2, 3))
            m[nm] = np.ascontiguousarray(np.stack(arrs, axis=1))
        elif nm == "shg":
            m[nm] = np.ascontiguousarray(inp["state_hgrn"][0, sl].transpose(2, 0, 1, 3))
        elif nm == "sconvT":
            a = inp["state_conv"][:, sl]
            m[nm] = np.ascontiguousarray(a.reshape(2, NS, 2, 16, 128).transpose(4, 0, 3, 1, 2))
        elif nm == "cmkT":
            a = inp["cache_mem_k"][:, sl]
            m[nm] = np.ascontiguousarray(a.transpose(4, 0, 1, 3, 2).reshape(128, DEPTH, NS, 4 * MEM))
        elif nm == "cmv":
            a = inp["cache_mem_v"][:, sl].reshape(DEPTH, NS, 2, 128, 512)
            m[nm] = np.ascontiguousarray(a.transpose(3, 0, 1, 2, 4).reshape(128, DEPTH, NS, 1024))
        elif nm in WEIGHT_LAYOUT:
            if cache is not None and nm in cache:
                m[nm] = cache[nm]
            else:
                m[nm] = WEIGHT_LAYOUT[nm](inp[nm])
                if cache is not None:
                    cache[nm] = m[nm]
        else:
            m[nm] = inp[nm]
    return m


def kernel(**inp):
    inp = {k: np.asarray(v) for k, v in inp.items()}
    b = Builder()
    nc = b.build()
    names = list(b.w.keys())
    cache = {}
    in_maps = [core_inputs(inp, c, names, cache) for c in range(NCORES)]
    res = run_bass_kernel_spmd(nc, in_maps, core_ids=list(range(NCORES)))
    R = res.results
    f32 = np.float32

    def cat(fn, axis=0):
        return np.ascontiguousarray(np.concatenate([fn(R[c]) for c in range(NCORES)], axis=axis).astype(f32))

    y_prompt = cat(lambda r: _unfm(r["yT"][:, :, :S])[None])
    y_sample = cat(lambda r: _unfm(r["yT"][:, :, S:])[:, None, :])
    p_state_conv = cat(lambda r: r["o_pconv"].transpose(1, 3, 2, 0).reshape(2, 1, 2, D), axis=1)
    s_state_conv = cat(lambda r: r["o_sconv"].transpose(1, 3, 4, 2, 0).reshape(2, NS, 2, D), axis=1)
    p_state_hgrn = cat(lambda r: r["o_phg"].transpose(1, 0, 2)[None, None], axis=1)
    s_state_hgrn = cat(lambda r: r["o_shg"].transpose(1, 2, 0, 3)[None], axis=1)
    p_win, s_win = [], []
    for g in range(3):
        p_win.append(cat(lambda r, g=g: r["o_pwk%d" % g].transpose(2, 1, 0)[None, None], axis=1))
        p_win.append(cat(lambda r, g=g: r["o_pwv%d" % g].transpose(2, 1, 0)[None, None], axis=1))
        s_win.append(cat(lambda r, g=g: r["o_swk"][:, g].transpose(2, 1, 0)[None, :, None], axis=1))
        s_win.append(cat(lambda r, g=g: r["o_swv"][:, g].transpose(2, 1, 0)[None, :, None], axis=1))
    p_mem_k = cat(lambda r: r["o_pmk"].transpose(1, 3, 2, 0)[:, None], axis=1)
    p_mem_v = cat(lambda r: r["o_pmv"].transpose(1, 2, 0, 3).reshape(DEPTH, 1, MEM, 4, 128), axis=1)
    return (y_prompt, y_sample, p_state_conv, p_state_hgrn, *p_win, p_mem_k, p_mem_v,
            s_state_conv, s_state_hgrn, *s_win)
```
